# Optimizing a Trainium2 kernel written in Bass

```python
import math
import jax, jax.numpy as jnp
from jax import lax
import numpy as np

D_MODEL = 1024
BATCH = 8
SEQ = 2048
DEPTH = 2

D_MIX = D_MODEL
GROUP_W = D_MIX // 4
CHUNK = 128

GM_HEADS = 4
GM_HEAD_DIM = GROUP_W // GM_HEADS

SSM_D_INNER = GROUP_W
SSM_HEAD_DIM = 64
SSM_HEADS = SSM_D_INNER // SSM_HEAD_DIM
SSM_GROUPS = 2
SSM_D_STATE = 128
SSM_CONV_K = 4
SSM_CONV_DIM = SSM_D_INNER + 2 * SSM_GROUPS * SSM_D_STATE
SSM_CHUNK = CHUNK

POOL_WINDOWS = (2, 4, 8, 16)
POOL_GROUPS = len(POOL_WINDOWS)
POOL_GROUP_DIM = GROUP_W // POOL_GROUPS

DA_HEADS = 4
DA_V_DIM = GROUP_W // DA_HEADS
DA_QK_DIM = DA_V_DIM // 2
Q_BLOCK = 128

D_FF = -(-8 * D_MODEL // (3 * 256)) * 256

GM_IN = 2 * GROUP_W
SSM_IN = SSM_D_INNER + SSM_CONV_DIM + SSM_HEADS
POOL_IN = GROUP_W
DA_IN = 3 * GROUP_W
D_IN = GM_IN + SSM_IN + POOL_IN + DA_IN
IN_SPLITS = (GM_IN, GM_IN + SSM_IN, GM_IN + SSM_IN + POOL_IN)

RMS_EPS = 1e-6

kernel_name = 'hybrid_parallel_heads_gmlp_ssd_pool_diffattn'


def rms_norm(x, w):
    xf = x.astype(jnp.float32)
    y = xf * lax.rsqrt(jnp.mean(xf * xf, axis=-1, keepdims=True) + RMS_EPS)
    return (y * w.astype(jnp.float32)).astype(x.dtype)


def gmlp_mixer(h, norm_w, ws, bs):
    b, L, _ = h.shape
    h = jax.nn.gelu(h)
    u, v = jnp.split(h, 2, axis=-1)
    v = rms_norm(v.reshape(b, L, GM_HEADS, GM_HEAD_DIM), norm_w)
    v = v.reshape(b, L // CHUNK, CHUNK, GM_HEADS, GM_HEAD_DIM)
    causal = jnp.tril(jnp.ones((CHUNK, CHUNK), dtype=bool))
    ws = jnp.where(causal[None], ws, jnp.zeros_like(ws))
    s = jnp.einsum('hts,bcshd->bcthd', ws, v) + bs.T[None, None, :, :, None]
    return u * s.reshape(b, L, GROUP_W)


def ssd_scan(x, dt, A, B, C):
    b, L, H, P = x.shape
    N = B.shape[-1]
    nc = L // SSM_CHUNK
    rep = H // SSM_GROUPS
    B = jnp.repeat(B, rep, axis=2).reshape(b, nc, SSM_CHUNK, H, N)
    C = jnp.repeat(C, rep, axis=2).reshape(b, nc, SSM_CHUNK, H, N)
    xdt = (x * dt[..., None]).reshape(b, nc, SSM_CHUNK, H, P)
    a_cs = jnp.cumsum((dt * A).reshape(b, nc, SSM_CHUNK, H), axis=2)
    causal = jnp.tril(jnp.ones((SSM_CHUNK, SSM_CHUNK), dtype=bool))[None, None, :, :, None]
    seg = a_cs[:, :, :, None, :] - a_cs[:, :, None, :, :]
    decay = jnp.exp(jnp.where(causal, seg, -jnp.inf))
    cb = jnp.einsum('bclhn,bcshn->bclsh', C, B)
    y_diag = jnp.einsum('bclsh,bcshp->bclhp', cb * decay, xdt)
    decay_to_end = jnp.exp(a_cs[:, :, -1:, :] - a_cs)
    states = jnp.einsum('bclhn,bclh,bclhp->bchpn', B, decay_to_end, xdt)
    chunk_decay = jnp.exp(a_cs[:, :, -1, :])

    def step(carry, inp):
        st, dec = inp
        return carry * dec[:, :, None, None] + st, carry

    init = jnp.zeros((b, H, P, N), dtype=states.dtype)
    _, prev_states = lax.scan(step, init, (jnp.moveaxis(states, 1, 0), jnp.moveaxis(chunk_decay, 1, 0)))
    prev_states = jnp.moveaxis(prev_states, 0, 1)
    y_off = jnp.einsum('bclhn,bchpn,bclh->bclhp', C, prev_states, jnp.exp(a_cs))
    return (y_diag + y_off).reshape(b, L, H, P)


def ssd_mixer(h, conv_w, conv_b, dt_bias, a_log, d_skip, norm_w):
    b, L, _ = h.shape
    z, xbc, dt = jnp.split(h, [SSM_D_INNER, SSM_D_INNER + SSM_CONV_DIM], axis=-1)
    xbc = lax.conv_general_dilated(
        xbc, conv_w.T[:, None, :], window_strides=(1,), padding=[(SSM_CONV_K - 1, 0)],
        dimension_numbers=('NWC', 'WIO', 'NWC'), feature_group_count=SSM_CONV_DIM)
    xbc = jax.nn.silu(xbc + conv_b)
    xs, Bm, Cm = jnp.split(xbc, [SSM_D_INNER, SSM_D_INNER + SSM_GROUPS * SSM_D_STATE], axis=-1)
    xs = xs.reshape(b, L, SSM_HEADS, SSM_HEAD_DIM).astype(jnp.float32)
    Bm = Bm.reshape(b, L, SSM_GROUPS, SSM_D_STATE).astype(jnp.float32)
    Cm = Cm.reshape(b, L, SSM_GROUPS, SSM_D_STATE).astype(jnp.float32)
    dt = jax.nn.softplus(dt.astype(jnp.float32) + dt_bias.astype(jnp.float32))
    A = -jnp.exp(a_log.astype(jnp.float32))
    y = ssd_scan(xs, dt, A, Bm, Cm) + d_skip.astype(jnp.float32)[:, None] * xs
    y = y.reshape(b, L, SSM_D_INNER) * jax.nn.silu(z.astype(jnp.float32))
    y = rms_norm(y.reshape(b, L, SSM_GROUPS, SSM_D_INNER // SSM_GROUPS),
                 norm_w.reshape(SSM_GROUPS, SSM_D_INNER // SSM_GROUPS))
    return y.reshape(b, L, SSM_D_INNER)


def pool_mixer(h, w, scale):
    b, L, _ = h.shape
    hf = h.astype(jnp.float32).reshape(b, L, POOL_GROUPS, POOL_GROUP_DIM)
    cs = jnp.concatenate([jnp.zeros((b, 1, POOL_GROUPS, POOL_GROUP_DIM), jnp.float32),
                          jnp.cumsum(hf, axis=1)], axis=1)
    hi = jnp.arange(1, L + 1)
    outs = []
    for g, win in enumerate(POOL_WINDOWS):
        lo = jnp.maximum(hi - win, 0)
        cs_g = cs[:, :, g]
        window_sum = cs_g[:, 1:] - jnp.take(cs_g, lo, axis=1)
        count = (hi - lo).astype(jnp.float32)[None, :, None]
        outs.append(window_sum / count - hf[:, :, g])
    p = jnp.stack(outs, axis=2)
    y = jnp.einsum('blgd,gde->blge', p, w.astype(jnp.float32)) * scale.astype(jnp.float32).reshape(POOL_GROUPS, POOL_GROUP_DIM)
    return y.reshape(b, L, GROUP_W)


def diff_attn_mixer(h, q_norm_w, k_norm_w, lq1, lk1, lq2, lk2, subln_w, layer_idx):
    b, L, _ = h.shape
    q, k, v = jnp.split(h, 3, axis=-1)
    q = rms_norm(q.reshape(b, L, DA_HEADS, 2, DA_QK_DIM), q_norm_w)
    k = rms_norm(k.reshape(b, L, DA_HEADS, 2, DA_QK_DIM), k_norm_w)
    v = v.reshape(b, L, DA_HEADS, DA_V_DIM)
    lam_init = 0.8 - 0.6 * math.exp(-0.3 * layer_idx)
    lam = (jnp.exp(jnp.sum(lq1.astype(jnp.float32) * lk1.astype(jnp.float32)))
           - jnp.exp(jnp.sum(lq2.astype(jnp.float32) * lk2.astype(jnp.float32))) + lam_init)
    slopes = jnp.exp2(-8.0 * jnp.arange(1, DA_HEADS + 1, dtype=jnp.float32) / DA_HEADS)
    kpos = jnp.arange(L)
    nb = L // Q_BLOCK
    q_blocks = jnp.moveaxis(q.reshape(b, nb, Q_BLOCK, DA_HEADS, 2, DA_QK_DIM), 1, 0)
    starts = jnp.arange(nb) * Q_BLOCK
    sm_scale = DA_QK_DIM ** -0.5

    def block(args):
        qi, start = args
        s = jnp.einsum('bqhcd,bkhcd->bhcqk', qi, k).astype(jnp.float32) * sm_scale
        dist = (start + jnp.arange(Q_BLOCK))[:, None] - kpos[None, :]
        bias = jnp.where(dist >= 0, -slopes[:, None, None] * dist.astype(jnp.float32), -jnp.inf)
        p = jax.nn.softmax(s + bias[None, :, None], axis=-1)
        a = p[:, :, 0] - lam * p[:, :, 1]
        return jnp.einsum('bhqk,bkhd->bqhd', a.astype(v.dtype), v)

    o = lax.map(block, (q_blocks, starts))
    o = jnp.moveaxis(o, 0, 1).reshape(b, L, DA_HEADS, DA_V_DIM)
    o = rms_norm(o, subln_w) * (1.0 - lam_init)
    return o.reshape(b, L, GROUP_W)


def setup_inputs(seed: int = 0) -> dict:
    key = jax.random.key(seed)
    ks = jax.random.split(key, 32)
    f32 = jnp.float32
    nrm = lambda k, shape, s: jax.random.normal(k, shape, f32) * s
    dt0 = jnp.exp(jax.random.uniform(ks[7], (DEPTH, SSM_HEADS), f32, math.log(1e-3), math.log(1e-1)))
    return {
        'x': jax.random.normal(ks[0], (BATCH, SEQ, D_MODEL), f32),
        'norm1_w': 1.0 + nrm(ks[1], (DEPTH, D_MODEL), 0.05),
        'w_in': nrm(ks[2], (DEPTH, D_MODEL, D_IN), D_MODEL ** -0.5),
        'gm_norm_w': 1.0 + nrm(ks[3], (DEPTH, GM_HEADS, GM_HEAD_DIM), 0.05),
        'gm_ws': nrm(ks[4], (DEPTH, GM_HEADS, CHUNK, CHUNK), CHUNK ** -0.5),
        'gm_bs': 1.0 + nrm(ks[5], (DEPTH, GM_HEADS, CHUNK), 0.05),
        'ssm_conv_w': nrm(ks[6], (DEPTH, SSM_CONV_DIM, SSM_CONV_K), SSM_CONV_K ** -0.5),
        'ssm_conv_b': nrm(ks[8], (DEPTH, SSM_CONV_DIM), 0.02),
        'ssm_dt_bias': dt0 + jnp.log(-jnp.expm1(-dt0)),
        'ssm_a_log': jnp.log(jax.random.uniform(ks[9], (DEPTH, SSM_HEADS), f32, 1.0, 16.0)),
        'ssm_d': 1.0 + nrm(ks[10], (DEPTH, SSM_HEADS), 0.1),
        'ssm_norm_w': 1.0 + nrm(ks[11], (DEPTH, SSM_D_INNER), 0.05),
        'pool_w': nrm(ks[12], (DEPTH, POOL_GROUPS, POOL_GROUP_DIM, POOL_GROUP_DIM), POOL_GROUP_DIM ** -0.5),
        'pool_scale': 1.0 + nrm(ks[13], (DEPTH, GROUP_W), 0.1),
        'da_q_norm_w': 1.0 + nrm(ks[14], (DEPTH, DA_QK_DIM), 0.05),
        'da_k_norm_w': 1.0 + nrm(ks[15], (DEPTH, DA_QK_DIM), 0.05),
        'da_lambda_q1': nrm(ks[16], (DEPTH, DA_QK_DIM), 0.1),
        'da_lambda_k1': nrm(ks[17], (DEPTH, DA_QK_DIM), 0.1),
        'da_lambda_q2': nrm(ks[18], (DEPTH, DA_QK_DIM), 0.1),
        'da_lambda_k2': nrm(ks[19], (DEPTH, DA_QK_DIM), 0.1),
        'da_subln_w': 1.0 + nrm(ks[20], (DEPTH, DA_V_DIM), 0.05),
        'w_out': nrm(ks[21], (DEPTH, D_MIX, D_MODEL), D_MIX ** -0.5),
        'norm2_w': 1.0 + nrm(ks[22], (DEPTH, D_MODEL), 0.05),
        'ffn_w_gate': nrm(ks[23], (DEPTH, D_MODEL, D_FF), D_MODEL ** -0.5),
        'ffn_w_up': nrm(ks[24], (DEPTH, D_MODEL, D_FF), D_MODEL ** -0.5),
        'ffn_w_down': nrm(ks[25], (DEPTH, D_FF, D_MODEL), D_FF ** -0.5),
    }


def reference(x, norm1_w, w_in, gm_norm_w, gm_ws, gm_bs, ssm_conv_w, ssm_conv_b, ssm_dt_bias,
              ssm_a_log, ssm_d, ssm_norm_w, pool_w, pool_scale, da_q_norm_w, da_k_norm_w,
              da_lambda_q1, da_lambda_k1, da_lambda_q2, da_lambda_k2, da_subln_w, w_out,
              norm2_w, ffn_w_gate, ffn_w_up, ffn_w_down):
    for i in range(DEPTH):
        h = rms_norm(x, norm1_w[i])
        proj = h @ w_in[i]
        pa, pb, pc, pd = jnp.split(proj, IN_SPLITS, axis=-1)
        ya = gmlp_mixer(pa, gm_norm_w[i], gm_ws[i], gm_bs[i])
        yb = ssd_mixer(pb, ssm_conv_w[i], ssm_conv_b[i], ssm_dt_bias[i], ssm_a_log[i], ssm_d[i], ssm_norm_w[i])
        yc = pool_mixer(pc, pool_w[i], pool_scale[i])
        yd = diff_attn_mixer(pd, da_q_norm_w[i], da_k_norm_w[i], da_lambda_q1[i], da_lambda_k1[i],
                             da_lambda_q2[i], da_lambda_k2[i], da_subln_w[i], i)
        mix = jnp.concatenate([ya.astype(x.dtype), yb.astype(x.dtype), yc.astype(x.dtype), yd.astype(x.dtype)], axis=-1)
        x = x + mix @ w_out[i]
        h = rms_norm(x, norm2_w[i])
        x = x + (jax.nn.silu(h @ ffn_w_gate[i]) * (h @ ffn_w_up[i])) @ ffn_w_down[i]
    return x
```

```python
import math
from contextlib import ExitStack

import numpy as np
import concourse.bass as bass
import concourse.mybir as mybir
from concourse.bass_utils import run_bass_kernel_spmd

F32 = mybir.dt.float32
BF16 = mybir.dt.bfloat16
U8 = mybir.dt.uint8
AF = mybir.ActivationFunctionType
ALU = mybir.AluOpType
AX = mybir.AxisListType


def _esize(dt):
    if dt == F32:
        return 4
    if dt == BF16:
        return 2
    if dt == U8:
        return 1
    s = str(dt)
    if '32' in s:
        return 4
    if '16' in s:
        return 2
    return 1


def ap_region(ap):
    pat = ap.ap
    pstep, pcnt = pat[0]
    es = _esize(ap.dtype)
    off = ap.offset
    if pstep == 0:
        row = 1
        for d in list(ap.tensor.shape)[1:]:
            row *= int(d)
        pstep = row
    p0 = off // pstep
    f0 = off % pstep
    ext = 0
    for st, cnt in pat[1:]:
        ext += abs(st) * (cnt - 1)
    lo, hi = f0 * es, (f0 + ext + 1) * es
    nm = ap.tensor.name
    if nm in PSUM_NAMES:
        return (nm, 0, 128, (lo // 2048) * 2048, ((hi + 2047) // 2048) * 2048)
    return (nm, p0, p0 + pcnt, lo, hi)


PSUM_NAMES = ('psA', 'pst')


class Prog:
    ENGS = ('pe', 'act', 'dve', 'pool', 'sp')

    def __init__(self, nc, sem_alloc):
        self.nc = nc
        self.sem_alloc = sem_alloc
        self.cnt = {e: 0 for e in self.ENGS}
        self.plan = {e: [] for e in self.ENGS}
        self.waited = {e: {} for e in self.ENGS}
        self.dsem = {}
        self.acc = {}
        self.semh = {}
        for e in ('pe', 'act', 'dve', 'pool'):
            self.semh['c_' + e] = sem_alloc('c_' + e)
        self.n_wait = 0
        self.n_ops = 0
        self.marks = []
        self.K = {}
        self.vc = {}

    def mark(self, name):
        self.marks.append((name, dict(self.cnt)))

    def _deps(self, eng, reads, writes):
        need = {}
        own = 'c_' + eng
        for is_w, aps in ((False, reads), (True, writes)):
            for ap in aps:
                nm, plo, phi, lo, hi = ap_region(ap)
                psum = nm in PSUM_NAMES
                for r in self.acc.get(nm, ()):
                    if r[0] < phi and plo < r[1] and r[2] < hi and lo < r[3]:
                        s, v = r[5]
                        if not (is_w or r[4] or (psum and s != own)):
                            continue
                        if need.get(s, 0) < v:
                            need[s] = v
        return need

    def _record(self, reads, writes, tok):
        for ap in writes:
            nm, plo, phi, lo, hi = ap_region(ap)
            lst = self.acc.setdefault(nm, [])
            lst[:] = [r for r in lst if not (plo <= r[0] and r[1] <= phi and lo <= r[2] and r[3] <= hi)]
            lst.append((plo, phi, lo, hi, True, tok))
        for ap in reads:
            nm, plo, phi, lo, hi = ap_region(ap)
            lst = self.acc.setdefault(nm, [])
            lst[:] = [r for r in lst if not ((not r[4]) and r[5][0] == tok[0]
                                             and plo <= r[0] and r[1] <= phi and lo <= r[2] and r[3] <= hi)]
            lst.append((plo, phi, lo, hi, False, tok))

    def _resolve(self, eng, need):
        waits = []
        own = 'c_' + eng
        K = self.K.setdefault(eng, {})
        for s, v in sorted(need.items(), key=lambda kv: -kv[1]):
            if s.startswith('d_'):
                v = max(v, self.dsem[s][1])
            if s == own and eng == 'pe':
                continue
            if K.get(s, 0) >= v:
                continue
            waits.append((s, v))
            K[s] = v
            snap = self.vc.get((s, v))
            if snap is None and s.startswith('d_'):
                snap = self.vc.get((s, self.dsem[s][1]))
            if snap:
                for s2, v2 in snap.items():
                    if K.get(s2, 0) < v2:
                        K[s2] = v2
        return waits

    def op(self, eng, thunk, reads=(), writes=()):
        reads = [r for r in reads if r is not None and not isinstance(r, (int, float))]
        writes = list(writes)
        waits = self._resolve(eng, self._deps(eng, reads, writes))
        self.cnt[eng] += 1
        tok = ('c_' + eng, self.cnt[eng])
        self.vc[tok] = dict(self.K.get(eng, {}))
        self.plan[eng].append((waits, thunk, tok))
        self._record(reads, writes, tok)
        self.n_wait += len(waits)
        self.n_ops += 1
        return tok

    def dma(self, queue, out, in_, key, reads_sb=(), writes_sb=(), **kw):
        s = 'd_' + key
        if s not in self.dsem:
            h = self.sem_alloc(s)
            self.dsem[s] = [h, 0]
            self.semh[s] = h
        waits = self._resolve(queue, self._deps(queue, list(reads_sb), list(writes_sb)))
        self.dsem[s][1] += 16
        tok = (s, self.dsem[s][1])
        self.vc[tok] = dict(self.K.get(queue, {}))
        self.plan[queue].append((waits, (lambda e, o=out, i=in_, k=kw: e.dma_start(out=o, in_=i, **k)), tok))
        self._record(list(reads_sb), list(writes_sb), tok)
        self.n_ops += 1
        return tok

    def wait_all(self, eng, toks):
        need = {}
        for s, v in toks:
            need[s] = max(need.get(s, 0), v)
        self.plan[eng].append((self._resolve(eng, need), None, None))

    def emit(self, block):
        semh = self.semh
        plan = self.plan

        def run(engname, e):
            for waits, thunk, tok in plan[engname]:
                if thunk is None:
                    standalone = list(waits)
                else:
                    standalone = list(waits[:-1])
                for k in range(0, len(standalone), 2):
                    w_ins = e.wait_ge(semh[standalone[k][0]], standalone[k][1])
                    if k + 1 < len(standalone):
                        w_ins._wait_ge(semh[standalone[k + 1][0]], standalone[k + 1][1])
                if thunk is None:
                    continue
                ins = thunk(e)
                if waits:
                    s, v = waits[-1]
                    ins._wait_ge(semh[s], v)
                if tok is None:
                    pass
                elif tok[0].startswith('d_'):
                    ins.then_inc(semh[tok[0]], 16)
                else:
                    ins.then_inc(semh[tok[0]], 1)

        @block.tensor
        def _(e):
            run('pe', e)

        @block.scalar
        def _(e):
            run('act', e)

        @block.vector
        def _(e):
            run('dve', e)

        @block.gpsimd
        def _(e):
            run('pool', e)

        @block.sync
        def _(e):
            run('sp', e)

    def mm(self, out, lhsT, rhs, start=True, stop=True, **kw):
        return self.op('pe', lambda e: e.matmul(out, lhsT, rhs, start=start, stop=stop, **kw),
                       reads=[lhsT, rhs], writes=[out])

    def transpose(self, out, in_, ident):
        return self.op('pe', lambda e: e.transpose(out, in_, ident), reads=[in_, ident], writes=[out])

    def act(self, out, in_, func, bias=None, scale=None):
        kw = {}
        rd = [in_]
        if bias is not None:
            kw['bias'] = bias
            rd.append(bias)
        if scale is not None:
            kw['scale'] = scale
            rd.append(scale)
        return self.op('act', lambda e: e.activation(out, in_, func, **kw), reads=rd, writes=[out])

    def tt(self, eng, out, in0, in1, op):
        return self.op(eng, lambda e: e.tensor_tensor(out, in0, in1, op), reads=[in0, in1], writes=[out])

    def ts(self, eng, out, in0, s1, s2=None, op0=ALU.mult, op1=None):
        kw = {}
        if op1 is not None:
            kw['op1'] = op1
        return self.op(eng, lambda e: e.tensor_scalar(out, in0, s1, s2, op0, **kw),
                       reads=[in0, s1, s2], writes=[out])

    def stt(self, out, in0, scalar, in1, op0, op1):
        return self.op('dve', lambda e: e.scalar_tensor_tensor(out, in0, scalar, in1, op0, op1),
                       reads=[in0, scalar, in1], writes=[out])

    def copy(self, eng, out, in_):
        if eng == 'act':
            return self.op(eng, lambda e: e.copy(out, in_), reads=[in_], writes=[out])
        return self.op(eng, lambda e: e.tensor_copy(out, in_), reads=[in_], writes=[out])

    def memset(self, eng, out, val):
        return self.op(eng, lambda e: e.memset(out, val), reads=[], writes=[out])

    def rsum(self, out, in_):
        return self.op('dve', lambda e: e.reduce_sum(out, in_, AX.X), reads=[in_], writes=[out])

    def recip(self, out, in_):
        return self.op('dve', lambda e: e.reciprocal(out, in_), reads=[in_], writes=[out])


D = 1024
L = 2048
NT = 16
NS = 4
DIN = 2564
DFF = 2816
NJ = DFF // 128
EPS = 1e-6
SM_SCALE = 32 ** -0.5
SLOPES = [2.0 ** (-8.0 * (h + 1) / 4) for h in range(4)]
POOL_WINDOWS = (2, 4, 8, 16)
NEGV = -30000.0

CP_N1W, CP_N2W, CP_CONVW, CP_CONVB, CP_DCOL, CP_SSMNW, CP_PSCALE, CP_QNW, CP_KNW, CP_BS = 0, 8, 16, 40, 46, 48, 50, 52, 53, 54
NCOLP = 58
RP_GMNW, RP_DTB, RP_ALOG, RP_LQ1, RP_LK1, RP_LQ2, RP_LK2, RP_SUBLN = 0, 256, 260, 264, 296, 328, 360, 392
NROWP = 648
CF_U, CF_SL, CF_NEG, CF_ALIBI, CF_INVWIN, CF_INVC16, CF_IDENT, CF_BD32, CF_MASKC, CF_MASKH = 0, 128, 256, 384, 452, 454, 486, 614, 742, 744
NCF = 746

ARENA = 70 * 1024


def make_consts():
    cf = np.zeros((128, NCF), np.float32)
    j = np.arange(128)[:, None]
    l = np.arange(128)[None, :]
    cf[:, CF_U:CF_U + 128] = (j <= l)
    cf[:, CF_SL:CF_SL + 128] = (j > l)
    cf[:, CF_NEG:CF_NEG + 128] = np.where(j <= l, 0.0, NEGV)
    for h in range(4):
        for e in range(17):
            cf[:, CF_ALIBI + h * 17 + e] = SLOPES[h] * (np.arange(128) - 128.0 * e)
    cf[:, CF_MASKC] = (np.arange(128) % 64 < 32)
    cf[:, CF_MASKC + 1] = (np.arange(128) % 64 >= 32)
    cf[:, CF_MASKH] = (np.arange(128) < 64)
    cf[:, CF_MASKH + 1] = (np.arange(128) >= 64)
    for c in range(2):
        for p in range(128):
            win = POOL_WINDOWS[2 * c + p // 64]
            cf[p, CF_INVWIN + c] = 1.0 / win
            for t in range(16):
                cf[p, CF_INVC16 + c * 16 + t] = 1.0 / min(t + 1, win)
    cf[:, CF_IDENT:CF_IDENT + 128] = np.eye(128)
    cf[:, CF_BD32:CF_BD32 + 128] = (j // 32 == l // 32)
    return cf


class Bump:
    def __init__(self, arena, base, limit):
        self.arena = arena
        self.off = base
        self.limit = limit

    def alloc(self, shape, dt):
        n = 1
        for s in shape[1:]:
            n *= s
        nb = n * _esize(dt)
        off = (self.off + 31) // 32 * 32
        assert off + nb <= self.limit, (off, nb, self.limit)
        self.off = off + nb
        v = self.arena[:, off:off + nb].bitcast(dt)
        if len(shape) == 3:
            v = v.rearrange("p (a b) -> p a b", a=shape[1])
        elif len(shape) == 4:
            v = v.rearrange("p (a b c) -> p a b c", a=shape[1], b=shape[2])
        return v


class _Stop(Exception):
    pass


def build(nlayers=2, dbg=(), stop=None):
    nc = bass.Bass("TRN2", target_bir_lowering=False)

    def din(name, shape):
        return nc.dram_tensor(name, list(shape), F32, kind="ExternalInput").ap()

    x_d = din("x", [L, D])
    w_in_d = din("w_in", [2, D, DIN])
    w_out_d = din("w_out", [2, D, D])
    wg_d = din("ffn_w_gate", [2, D, DFF])
    wu_d = din("ffn_w_up", [2, D, DFF])
    wd_d = din("ffn_w_down", [2, DFF, D])
    wsT_d = din("gm_wsT", [2, 4, 128, 128])
    poolw_d = din("pool_w", [2, 4, 64, 64])
    colp_d = din("colp", [2, 128, NCOLP])
    rowp_d = din("rowp", [2, 1, NROWP])
    cf_d = din("cf", [128, NCF])
    out_d = nc.dram_tensor("out", [L, D], F32, kind="ExternalOutput").ap()
    dbg_d = {}
    for name in dbg:
        dbg_d[name] = nc.dram_tensor("dbg_" + name, [128, 8, L], F32, kind="ExternalOutput").ap()

    es = ExitStack()

    def sb(name, shape, dt):
        return es.enter_context(nc.sbuf_tensor("s_" + name, list(shape), dt))

    def sem(name):
        return es.enter_context(nc.semaphore(name))

    P = Prog(nc, sem)

    xT = sb("xT", [128, 8, L], F32)
    hT = sb("hT", [128, 8, L], BF16)
    mixT = sb("mixT", [128, 8, L], BF16)
    arena = sb("arena", [128, ARENA], U8)
    cF = sb("cF", [128, NCF], F32)
    colp = sb("colp", [128, NCOLP], F32)
    rowp = sb("rowp", [128, NROWP], F32)
    identb = sb("identb", [128, 128], BF16)
    Ub = sb("Ub", [128, 128], BF16)
    NEGb = sb("NEGb", [128, 128], BF16)
    bd32b = sb("bd32b", [128, 128], BF16)
    onesb = sb("onesb", [128, 128], BF16)
    neghalf = sb("neghalf", [128, 1], F32)
    small = sb("small", [128, 64], F32)
    psA = es.enter_context(nc.psum_tensor("psA", [128, 6, 512], F32))
    pst = es.enter_context(nc.psum_tensor("pst", [128, 16, 128], BF16))

    identf = cF[:, CF_IDENT:CF_IDENT + 128]
    Uf = cF[:, CF_U:CF_U + 128]
    SLf = cF[:, CF_SL:CF_SL + 128]
    NEGf = cF[:, CF_NEG:CF_NEG + 128]

    RING_SLOT = 8192

    def dump(name, src):
        if name in dbg_d:
            for c in range(8):
                P.dma('pool', dbg_d[name][:, c, :], src[:, c, :], 'dbg', reads_sb=[src[:, c, :]])

    def chk(phase):
        P.mark(phase)
        if stop == phase:
            dump('mix0', mixT)
            raise _Stop()

    def ring_slot(i, kind="in"):
        v = arena[:, i * RING_SLOT:(i + 1) * RING_SLOT].bitcast(BF16)
        if kind == "in":
            return v.rearrange("p (c n) -> p c n", c=8)
        return v.rearrange("p (j n) -> p j n", j=4)

    def load_piece(slot, src, ncols, kind="in"):
        dst = ring_slot(slot, kind)
        if kind == "in":
            dst = dst[:, :, 0:ncols]
        else:
            dst = dst[:, 0:ncols, :]
        P.dma('pool', dst, src, 'ring%d' % slot, writes_sb=[dst])
        return dst

    P.dma('sp', cF[:], cf_d, 'cf', writes_sb=[cF[:]])
    P.copy('dve', identb[:], identf)
    P.copy('dve', Ub[:], Uf)
    P.copy('dve', NEGb[:], NEGf)
    P.copy('dve', bd32b[:], cF[:, CF_BD32:CF_BD32 + 128])
    P.memset('dve', onesb[:], 1.0)
    P.memset('dve', neghalf[:], -0.5)

    M0 = sb("M0", [128, 512], BF16)
    M1 = sb("M1", [128, 512], BF16)
    m04 = M0[:].rearrange("p (c j q) -> p c j q", c=2, j=2)
    m14 = M1[:].rearrange("p (c j q) -> p c j q", c=2, j=2)
    P.memset('pool', M0[:], 0.0)
    P.memset('pool', M1[:], NEGV)
    for c in range(2):
        P.copy('pool', m04[:, c, 0, :], NEGb[:])
        P.copy('pool', m14[:, c, 1, :], NEGb[:])

    psrot = {}

    def psbank(lo=0, hi=6):
        k = (lo, hi)
        v = psrot.get(k, 0)
        psrot[k] = v + 1
        return lo + v % (hi - lo)

    def load_x(fuse_norm):
        bm = Bump(arena, 16384, ARENA)
        stage = [bm.alloc([128, D], F32) for _ in range(4)]
        nbx = norm_bufs() if fuse_norm else None
        for t in range(NT):
            st = stage[t % 4]
            P.dma('sp', st, x_d[t * 128:(t + 1) * 128, :], 'xin%d' % (t % 4), writes_sb=[st])
            for half in range(2):
                b = psbank()
                for c4 in range(4):
                    c = half * 4 + c4
                    P.transpose(psA[:, b, c4 * 128:(c4 + 1) * 128], st[:, c * 128:(c + 1) * 128], identf)
                src = psA[:, b, :].rearrange("p (c t) -> p c t", c=4)
                dst = xT[:, half * 4:half * 4 + 4, t * 128:(t + 1) * 128]
                P.copy('act' if (t + half) % 2 == 0 else 'dve', dst, src)
            if fuse_norm and t % 4 == 3 and t >= 7:
                norm_slab(t // 4 - 1, CP_N1W, nbx)
        if fuse_norm:
            norm_slab(NS - 1, CP_N1W, nbx)

    def norm_bufs():
        bm = Bump(arena, 50 * 1024, ARENA)
        sq = [bm.alloc([128, 512], BF16) for _ in range(2)]
        ms = bm.alloc([128, 512], F32)
        rs = bm.alloc([128, 512], F32)
        return sq, ms, rs

    def norm_slab(s, nw_off, bufs):
        sq, ms, rs = bufs
        sl = slice(s * 512, (s + 1) * 512)
        b = psbank()
        for c in range(8):
            P.act(sq[c % 2], xT[:, c, sl], AF.Square)
            P.mm(psA[:, b, :], onesb[:], sq[c % 2], start=(c == 0), stop=(c == 7))
        P.act(ms, psA[:, b, :], AF.Ln, scale=1.0 / D, bias=EPS)
        P.act(rs, ms, AF.Exp, scale=-0.5)
        for c in range(8):
            P.stt(hT[:, c, sl], xT[:, c, sl], colp[:, nw_off + c:nw_off + c + 1], rs, ALU.mult, ALU.mult)

    def norm_full(nw_off):
        bufs = norm_bufs()
        for s in range(NS):
            norm_slab(s, nw_off, bufs)

    def proj_fm(wpiece, col0, s, b, ncols=128):
        sl = slice(s * 512, (s + 1) * 512)
        for dc in range(8):
            P.mm(psA[0:ncols, b, :], wpiece[:, dc, col0:col0 + ncols], hT[:, dc, sl], start=(dc == 0), stop=(dc == 7))

    final_tok = [None, None]
    final_done_flag = []

    def final_tile(t, osts, lo=0, hi=6):
        st = osts[t % 2]
        for half in range(2):
            b = psbank(lo, hi)
            for c4 in range(4):
                c = half * 4 + c4
                P.transpose(psA[:, b, c4 * 128:(c4 + 1) * 128], xT[:, c, t * 128:(t + 1) * 128], identf)
            P.copy('act' if half == 0 else 'dve', st[:, half * 512:(half + 1) * 512], psA[:, b, :])
        final_tok[t % 2] = P.dma('sp', out_d[t * 128:(t + 1) * 128, :], st, 'out%d' % (t % 2), reads_sb=[st])

    def load_params(li):
        P.dma('sp', colp[:], colp_d[li], 'params', writes_sb=[colp[:]])
        P.dma('sp', rowp[:], rowp_d[li].partition_broadcast(128), 'params', writes_sb=[rowp[:]])

    def layer(li, pre_normed=False):
        lam_init = 0.8 - 0.6 * math.exp(-0.3 * li)
        win = w_in_d[li]

        def win_piece(c0, n):
            return win[:, c0:c0 + n].rearrange("(c p) n -> p c n", p=128)

        if not pre_normed:
            load_params(li)
        pA = load_piece(0, win_piece(0, 512), 512)
        pC = load_piece(1, win_piece(1540, 256), 256)

        if not pre_normed:
            norm_full(CP_N1W)
        dump('h1_%d' % li, hT)
        chk('norm1')

        W0 = 16384

        bm = Bump(arena, W0, ARENA)
        wsTm = bm.alloc([128, 4, 128], BF16)
        NBA = 7
        gA = [bm.alloc([128, 512], F32) for _ in range(NBA)]
        gB = [bm.alloc([128, 512], F32) for _ in range(NBA)]
        vnb = [bm.alloc([128, 256], BF16) for _ in range(NBA)]
        yab = [bm.alloc([128, 256], BF16) for _ in range(NBA)]
        ssv = [bm.alloc([128, 4], F32) for _ in range(NBA)]
        rsv = [bm.alloc([128, 4], F32) for _ in range(NBA)]
        P.dma('pool', wsTm, wsT_d[li].rearrange("h s t -> s h t"), 'wsT', writes_sb=[wsTm])
        P.tt('dve', wsTm, wsTm, Ub[:].unsqueeze(1).to_broadcast([128, 4, 128]), ALU.mult)
        gmnw = rowp[:, RP_GMNW:RP_GMNW + 256]
        v4 = lambda ap: ap.rearrange("p (h d) -> p h d", h=4)
        pa_of = {}

        def a_st(st, t):
            tl = slice(t * 128, (t + 1) * 128)
            par = t % NBA
            a, bb = gA[par], gB[par]
            if st == 0:
                pa = psA[:, t % 4, :]
                pa_of[t] = pa
                for dc in range(8):
                    P.mm(pa, hT[:, dc, tl], pA[:, dc, :], start=(dc == 0), stop=(dc == 7))
                P.act(a, pa, AF.Square, scale=0.044715 ** 0.5)
            elif st == 1:
                P.stt(bb, a, 1.0, pa_of[t], ALU.add, ALU.mult)
                P.act(a, bb, AF.Sigmoid, scale=1.5957691216057308)
            elif st == 2:
                P.tt('dve', a, a, pa_of[t], ALU.mult)
                P.act(bb[:, 0:256], a[:, 256:512], AF.Square)
            elif st == 3:
                P.rsum(ssv[par], v4(bb[:, 0:256]))
                P.ts('dve', ssv[par], ssv[par], 1.0 / 64, EPS, op0=ALU.mult, op1=ALU.add)
                P.tt('pool', rsv[par], ssv[par], neghalf[:, 0:1].to_broadcast([128, 4]), ALU.pow)
            elif st == 4:
                P.tt('pool', v4(bb[:, 0:256]), v4(a[:, 256:512]), rsv[par].unsqueeze(2).to_broadcast([128, 4, 64]), ALU.mult)
                P.tt('pool', vnb[par], bb[:, 0:256], gmnw, ALU.mult)
            elif st == 5:
                b2 = psbank(4, 6)
                for h in range(4):
                    P.mm(psA[:, b2, h * 64:(h + 1) * 64], wsTm[:, h, :], vnb[par][:, h * 64:(h + 1) * 64], start=True, stop=True)
                for h in range(4):
                    P.stt(yab[par][:, h * 64:(h + 1) * 64], psA[:, b2, h * 64:(h + 1) * 64],
                          colp[:, CP_BS + h:CP_BS + h + 1], a[:, h * 64:(h + 1) * 64], ALU.add, ALU.mult)
            else:
                ts0 = (t % 2) * 8
                for j in range(2):
                    P.transpose(pst[:, ts0 + j, :], yab[par][:, j * 128:(j + 1) * 128], identb[:])
                P.copy('act', mixT[:, 0:2, tl], pst[:, ts0:ts0 + 2, :])

        NST = 7
        for i in range(NT + NST - 1):
            for st in range(NST):
                t = i - st
                if 0 <= t < NT:
                    a_st(st, t)

        chk('A')
        pB1 = load_piece(0, win_piece(512, 512), 512)

        bm = Bump(arena, W0, ARENA)
        wbd = [bm.alloc([128, 128], BF16) for _ in range(2)]
        pcxs = [bm.alloc([128, 16 + L], F32) for _ in range(2)]
        L1 = bm.alloc([128, 16 + L], F32)
        L2 = bm.alloc([128, 16 + L], F32)
        pT = bm.alloc([128, L], BF16)
        t16 = bm.alloc([128, 16], F32)
        for cc in range(2):
            P.memset('pool', wbd[cc], 0.0)
            for gg in range(2):
                dst = wbd[cc][gg * 64:(gg + 1) * 64, gg * 64:(gg + 1) * 64]
                P.dma('pool', dst, poolw_d[li][2 * cc + gg], 'wbd', writes_sb=[dst])
        P.memset('pool', pcxs[0][:, 0:16], 0.0)
        P.memset('pool', pcxs[1][:, 0:16], 0.0)
        P.memset('pool', L1[:, 0:16], 0.0)
        P.memset('pool', L2[:, 0:16], 0.0)
        for cc in range(2):
            for s in range(NS):
                b = psbank()
                proj_fm(pC, cc * 128, s, b)
                P.copy('act', pcxs[cc][:, 16 + s * 512:16 + (s + 1) * 512], psA[:, b, :])
        for cc in range(2):
            pcx = pcxs[cc]

            def shadd(dst, src, k, prt=slice(0, 128), eng='dve'):
                P.tt(eng, dst[prt, 16:16 + L], src[prt, 16:16 + L], src[prt, 16 - k:16 - k + L], ALU.add)

            shadd(L1, pcx, 1)
            hi = slice(64, 128)
            lo = slice(0, 64)
            if cc == 0:
                shadd(L2, L1, 2, hi)
            else:
                shadd(L2, L1, 2)
                shadd(L1, L2, 4)
                shadd(L2, L1, 8, hi)
            for prt, S in ((lo, L1), (hi, L2)):
                P.stt(pT[prt, 16:L], S[prt, 32:16 + L], cF[prt, CF_INVWIN + cc:CF_INVWIN + cc + 1],
                      pcx[prt, 32:16 + L], ALU.mult, ALU.subtract)
                P.tt('dve', t16[prt, :], S[prt, 16:32], cF[prt, CF_INVC16 + cc * 16:CF_INVC16 + (cc + 1) * 16], ALU.mult)
                P.tt('dve', pT[prt, 0:16], t16[prt, :], pcx[prt, 16:32], ALU.subtract)
            for s in range(NS):
                sl = slice(s * 512, (s + 1) * 512)
                b = psbank()
                P.mm(psA[:, b, :], wbd[cc], pT[:, sl], start=True, stop=True)
                P.act(mixT[:, 4 + cc, sl], psA[:, b, :], AF.Identity, scale=colp[:, CP_PSCALE + cc:CP_PSCALE + cc + 1])

        chk('C')
        pB2 = load_piece(1, win_piece(1024, 512), 512)

        bm = Bump(arena, W0, ARENA)
        zs = bm.alloc([128, 2, L], BF16)
        xsT = bm.alloc([128, 2, L], BF16)
        BT = bm.alloc([128, 2, L], BF16)
        CT = bm.alloc([128, 2, L], BF16)
        dtm = bm.alloc([128, 64], F32)
        atm = bm.alloc([128, 64], F32)
        acs = bm.alloc([128, 64], F32)
        dte = bm.alloc([128, 64], F32)
        b1_start = bm.off
        cin = [bm.alloc([128, 4 + 512], BF16) for _ in range(2)]
        bm_cin3 = bm.alloc([128, 4 + 512], BF16)
        accs = [bm.alloc([128, 512], F32) for _ in range(2)]
        dws = [bm.alloc([128, 4, 128], BF16) for _ in range(2)]
        wdt = bm.alloc([128, 8, 4], BF16)
        dtr = bm.alloc([128, 64], F32)
        t64a = bm.alloc([128, 64], F32)
        t64b = bm.alloc([128, 64], F32)
        Ab = bm.alloc([128, 4], F32)
        P.dma('pool', wdt, win_piece(1536, 4), 'wdt', writes_sb=[wdt])

        bD = psbank()
        for t in range(NT):
            for dc in range(8):
                P.mm(psA[:, bD, t * 4:(t + 1) * 4], hT[:, dc, t * 128:(t + 1) * 128], wdt[:, dc, :], start=(dc == 0), stop=(dc == 7))
        v3 = lambda ap: ap.rearrange("p (c h) -> p c h", h=4)
        P.tt('dve', v3(dtr), v3(psA[:, bD, 0:64]), rowp[:, RP_DTB:RP_DTB + 4].unsqueeze(1).to_broadcast([128, 16, 4]), ALU.add)
        P.act(t64a, dtr, AF.Abs)
        P.act(t64a, t64a, AF.Exp, scale=-1.0)
        P.act(t64a, t64a, AF.Ln, bias=1.0)
        P.ts('dve', t64b, dtr, 0.0, None, op0=ALU.max)
        P.tt('dve', dtm, t64a, t64b, ALU.add)
        P.act(Ab, rowp[:, RP_ALOG:RP_ALOG + 4], AF.Exp)
        P.ts('dve', Ab, Ab, -1.0, None, op0=ALU.mult)
        P.tt('dve', v3(atm), v3(dtm), Ab.unsqueeze(1).to_broadcast([128, 16, 4]), ALU.mult)
        bX = psbank()
        P.mm(psA[:, bX, 0:64], Uf, atm, start=True, stop=True)
        P.copy('dve', acs, psA[:, bX, 0:64])
        P.mm(psA[:, bX, 64:128], SLf, atm, start=True, stop=True)
        P.act(dte, psA[:, bX, 64:128], AF.Exp)

        for zc in range(2):
            for s in range(NS):
                b = psbank()
                proj_fm(pB1, zc * 128, s, b)
                P.act(zs[:, zc, s * 512:(s + 1) * 512], psA[:, b, :], AF.Silu)
        dests = [xsT[:, 0, :], xsT[:, 1, :], BT[:, 0, :], BT[:, 1, :], CT[:, 0, :], CT[:, 1, :]]
        cin = cin + [bm_cin3]
        conv_units = [(cb, s_) for cb in range(6) for s_ in range(NS)]

        def conv_proj(i):
            cb, s_ = conv_units[i]
            piece, col0 = (pB1, 256 + cb * 128) if cb < 2 else (pB2, (cb - 2) * 128)
            if s_ == 0:
                dw = dws[cb % 2]
                cw = CP_CONVW + cb * 4
                for kk in range(2):
                    P.ts('dve', dw[:, kk, :], identb[:], colp[:, cw + kk:cw + kk + 1], None, op0=ALU.mult)
            ci = cin[i % 3]
            prev = cin[(i - 1) % 3]
            b = psbank(0, 3)
            proj_fm(piece, col0, s_, b)
            P.copy('act', ci[:, 3:515], psA[:, b, :])
            if s_ == 0:
                P.memset('pool', ci[:, 0:3], 0.0)
            else:
                P.copy('pool', ci[:, 0:3], prev[:, 512:515])

        def conv_mm(i):
            cb, s_ = conv_units[i]
            dw = dws[cb % 2]
            ci = cin[i % 3]
            ac = accs[i % 2]
            cw = CP_CONVW + cb * 4
            b2 = psbank(3, 6)
            for kk in range(2):
                P.mm(psA[:, b2, :], dw[:, kk, :], ci[:, kk:kk + 512], start=(kk == 0), stop=(kk == 1))
            P.stt(ac, ci[:, 2:2 + 512], colp[:, cw + 2:cw + 3], psA[:, b2, :], ALU.mult, ALU.add)
            P.stt(ac, ci[:, 3:3 + 512], colp[:, cw + 3:cw + 4], ac, ALU.mult, ALU.add)
            P.act(dests[cb][:, s_ * 512:(s_ + 1) * 512], ac, AF.Silu, bias=colp[:, CP_CONVB + cb:CP_CONVB + cb + 1])

        conv_proj(0)
        for i in range(len(conv_units)):
            if i + 1 < len(conv_units):
                conv_proj(i + 1)
            conv_mm(i)
        chk('B1')
        pD1 = load_piece(0, win_piece(1796, 512), 512)
        pD2 = load_piece(1, win_piece(2308, 256), 256)

        bm2 = Bump(arena, b1_start, ARENA)
        xdts = [bm2.alloc([128, 256], BF16) for _ in range(2)]
        xdtw = bm2.alloc([128, 256], BF16)
        Btm = bm2.alloc([128, 256], BF16)
        t1 = bm2.alloc([128, 512], F32)
        MTs = [bm2.alloc([128, 512], BF16) for _ in range(2)]
        E = bm2.alloc([128, 512], F32)
        Cdecs = [bm2.alloc([128, 512], BF16) for _ in range(2)]
        cds = [bm2.alloc([128, 4], F32) for _ in range(2)]
        S = bm2.alloc([128, 256], F32)
        Sbf = bm2.alloc([128, 256], BF16)
        yg = bm2.alloc([128, 2, 512], F32)
        sqy = bm2.alloc([128, 512], BF16)
        msy = t1
        rsy = E
        P.memset('pool', S, 0.0)
        P.memset('pool', Sbf, 0.0)
        h4 = lambda ap: ap.rearrange("p (h d) -> p h d", h=4)
        g22 = lambda ap: ap.rearrange("p (g r l) -> p g r l", g=2, r=2)

        def b2_X(c):
            par = c % 2
            cl = slice(c * 128, (c + 1) * 128)
            xdt, MT, Cdec, cd = xdts[par], MTs[par], Cdecs[par], cds[par]
            tb = par * 8
            P.transpose(pst[:, tb + 0, :], xsT[:, 0, cl], identb[:])
            P.transpose(pst[:, tb + 1, :], xsT[:, 1, cl], identb[:])
            P.transpose(pst[:, tb + 2, :], BT[:, 0, cl], identb[:])
            P.transpose(pst[:, tb + 3, :], BT[:, 1, cl], identb[:])
            P.tt('dve', h4(xdt), pst[:, tb:tb + 2, :].rearrange("p a (h d) -> p (a h) d", h=2),
                 dtm[:, c * 4:(c + 1) * 4].unsqueeze(2).to_broadcast([128, 4, 64]), ALU.mult)
            P.tt('pool', h4(xdtw), h4(xdt), dte[:, c * 4:(c + 1) * 4].unsqueeze(2).to_broadcast([128, 4, 64]), ALU.mult)
            P.copy('act', Btm.rearrange("p (a n) -> p a n", a=2), pst[:, tb + 2:tb + 4, :])
            bC = psbank(0, 2)
            for g in range(2):
                P.mm(psA[:, bC, g * 128:(g + 1) * 128], BT[:, g, cl], CT[:, g, cl], start=True, stop=True)
            bR = psbank(2, 4)
            for h in range(4):
                P.mm(psA[:, bR, h * 128:(h + 1) * 128], atm[:, c * 4 + h:c * 4 + h + 1].to_broadcast([128, 128]), Uf,
                     start=True, stop=True)
            R4 = psA[:, bR, :].rearrange("p (h l) -> p h l", h=4)
            t14 = t1.rearrange("p (h l) -> p h l", h=4)
            for h in range(4):
                P.stt(t14[:, h, :], R4[:, h, :], acs[:, c * 4 + h:c * 4 + h + 1], NEGf, ALU.subtract, ALU.add)
            P.act(t1, t1, AF.Exp)
            P.tt('dve', g22(MT), g22(t1),
                 psA[:, bC, 0:256].rearrange("p (g l) -> p g l", g=2).unsqueeze(2).to_broadcast([128, 2, 2, 128]), ALU.mult)
            P.act(E, psA[:, bR, :], AF.Exp)
            P.copy('act', cd, E.rearrange("p (h l) -> p h l", h=4)[:, :, 127])
            P.tt('pool', g22(Cdec), g22(E), CT[:, :, cl].unsqueeze(2).to_broadcast([128, 2, 2, 128]), ALU.mult)
            bY = 4 + par
            for h in range(4):
                P.mm(psA[:, bY, 256 + h * 64:256 + (h + 1) * 64], Btm[:, (h // 2) * 128:(h // 2 + 1) * 128],
                     xdtw[:, h * 64:(h + 1) * 64], start=True, stop=True)

        def b2_Y(c):
            par = c % 2
            cl = slice(c * 128, (c + 1) * 128)
            xdt, MT, Cdec, cd = xdts[par], MTs[par], Cdecs[par], cds[par]
            bY = 4 + par
            for h in range(4):
                o = psA[(h % 2) * 64:(h % 2 + 1) * 64, bY, (h // 2) * 128:(h // 2 + 1) * 128]
                P.mm(o, xdt[:, h * 64:(h + 1) * 64], MT[:, h * 128:(h + 1) * 128], start=True, stop=False)
                P.mm(o, Sbf[:, h * 64:(h + 1) * 64], Cdec[:, h * 128:(h + 1) * 128], start=False, stop=True)
            for h in range(4):
                P.stt(S[:, h * 64:(h + 1) * 64], S[:, h * 64:(h + 1) * 64], cd[:, h:h + 1],
                      psA[:, bY, 256 + h * 64:256 + (h + 1) * 64], ALU.mult, ALU.add)
            P.copy('act', Sbf, S)
            cs = (c % 4) * 128
            for hc in range(2):
                P.stt(yg[:, hc, cs:cs + 128], xsT[:, hc, cl], colp[:, CP_DCOL + hc:CP_DCOL + hc + 1],
                      psA[:, bY, hc * 128:(hc + 1) * 128], ALU.mult, ALU.add)
            if c % 4 == 3:
                s = c // 4
                sl = slice(s * 512, (s + 1) * 512)
                for hc in range(2):
                    P.tt('pool', yg[:, hc, :], yg[:, hc, :], zs[:, hc, sl], ALU.mult)
                    P.act(sqy, yg[:, hc, :], AF.Square)
                    bN = psbank(0, 2)
                    P.mm(psA[:, bN, :], onesb[:], sqy, start=True, stop=True)
                    P.act(msy, psA[:, bN, :], AF.Ln, scale=1.0 / 128, bias=EPS)
                    P.act(rsy, msy, AF.Exp, scale=-0.5)
                    P.stt(mixT[:, 2 + hc, sl], yg[:, hc, :], colp[:, CP_SSMNW + hc:CP_SSMNW + hc + 1], rsy, ALU.mult, ALU.mult)

        b2_X(0)
        for c in range(NT):
            if c + 1 < NT:
                b2_X(c + 1)
            b2_Y(c)

        chk('B2')
        bm = Bump(arena, W0, ARENA)
        qT2 = bm.alloc([128, 2, 2, L], BF16)
        kT2 = bm.alloc([128, 2, 2, L], BF16)
        vflat = bm.alloc([128, NT, 324], BF16)
        vaug = vflat[:, :, 0:260].rearrange("p t (h e) -> p t h e", h=4)
        PT = [bm.alloc([128, 512], BF16) for _ in range(3)]
        oTs = [bm.alloc([128, 512], F32) for _ in range(2)]
        sqb = PT[0]
        msq = oTs[0]
        rsq = oTs[1]
        o0 = bm.alloc([128, 256], F32)
        o1 = bm.alloc([128, 256], F32)
        onbs = [bm.alloc([128, 256], BF16) for _ in range(2)]
        lt = bm.alloc([128, 32], F32)
        s1 = small[:, 0:1]
        s2 = small[:, 1:2]
        neglam = small[:, 2:3]
        qnws = small[:, 4:6]
        rc = small[:, 8:16]
        nr1 = small[:, 16:20]
        ss4 = small[:, 20:24]
        rs4 = small[:, 24:28]
        P.tt('dve', lt, rowp[:, RP_LQ1:RP_LQ1 + 32], rowp[:, RP_LK1:RP_LK1 + 32], ALU.mult)
        P.rsum(s1, lt)
        P.tt('dve', lt, rowp[:, RP_LQ2:RP_LQ2 + 32], rowp[:, RP_LK2:RP_LK2 + 32], ALU.mult)
        P.rsum(s2, lt)
        P.act(small[:, 0:2], small[:, 0:2], AF.Exp)
        P.tt('dve', neglam, s2, s1, ALU.subtract)
        P.ts('dve', neglam, neglam, -lam_init, None, op0=ALU.add)
        P.ts('dve', qnws, cF[:, CF_MASKC:CF_MASKC + 2], colp[:, CP_QNW:CP_QNW + 1], SM_SCALE, op0=ALU.mult, op1=ALU.mult)
        P.memset('pool', vflat[:, :, 260:324], 0.0)
        P.memset('pool', vaug[:, :, :, 64:65], 1.0)
        P.ts('dve', small[:, 6:8], cF[:, CF_MASKH:CF_MASKH + 2], colp[:, CP_KNW:CP_KNW + 1], None, op0=ALU.mult)
        for blk in range(4):
            for s in range(NS):
                sl = slice(s * 512, (s + 1) * 512)
                b = psbank(0, 3)
                proj_fm(pD1, blk * 128, s, b)
                P.act(sqb, psA[:, b, :], AF.Square)
                b2 = psbank(3, 6)
                P.mm(psA[:, b2, :], bd32b[:], sqb, start=True, stop=True)
                P.act(msq, psA[:, b2, :], AF.Ln, scale=1.0 / 32, bias=EPS)
                P.act(rsq, msq, AF.Exp, scale=-0.5)
                if blk < 2:
                    for c in range(2):
                        P.stt(qT2[:, blk, c, sl], psA[:, b, :], qnws[:, c:c + 1], rsq, ALU.mult, ALU.mult)
                else:
                    for hh in range(2):
                        P.stt(kT2[:, blk - 2, hh, sl], psA[:, b, :], small[:, 6 + hh:7 + hh], rsq, ALU.mult, ALU.mult)
        for t in range(NT):
            b = psbank()
            for dc in range(8):
                P.mm(psA[:, b, 0:256], hT[:, dc, t * 128:(t + 1) * 128], pD2[:, dc, :], start=(dc == 0), stop=(dc == 7))
            P.copy('act' if t % 2 else 'dve', vaug[:, t, :, 0:64], psA[:, b, 0:256].rearrange("p (h d) -> p h d", h=4))

        pO = [load_piece(0, w_out_d[li][:, 0:512].rearrange("(c p) n -> p c n", p=128), 512),
              load_piece(1, w_out_d[li][:, 512:1024].rearrange("(c p) n -> p c n", p=128), 512)]

        psSum = pst[:, 4:8, :].rearrange("p a b -> p (a b)").bitcast(F32)
        S3 = pst[:, 8:16, :].rearrange("p a b -> p (a b)").bitcast(F32)
        Sbanks = [psA[:, 0, :], psA[:, 1, :], S3]
        units = []
        for qs in range(8):
            for h in range(4):
                kbs = []
                for kb in range(2 * qs + 2):
                    if SLOPES[h] * (256 * qs - (128 * kb + 127)) > 130.0:
                        continue
                    kbs.append(kb)
                for n_, kb in enumerate(kbs):
                    units.append((qs, h, kb, n_ == 0, n_ == len(kbs) - 1))
        s_rot = [0]

        def emit_S(u):
            qs, h, kb, first, last = u
            kc, hh = h // 2, h % 2
            b = s_rot[0] % 3
            s_rot[0] += 1
            diag = kb >= 2 * qs
            P.mm(Sbanks[b], kT2[:, kc, hh, kb * 128:(kb + 1) * 128],
                 qT2[:, kc, :, qs * 256:(qs + 1) * 256].rearrange("p c q -> p (c q)") if False else qT2[:, kc, :, qs * 256:(qs + 1) * 256],
                 start=True, stop=not diag)
            if diag:
                P.mm(Sbanks[b], identb[:], (M0 if kb == 2 * qs else M1)[:], start=False, stop=True)
            return b

        def emit_exp_pv(u, b):
            qs, h, kb, first, last = u
            e = 2 * qs - kb + 1
            P.act(PT[b], Sbanks[b], AF.Exp, bias=cF[:, CF_ALIBI + h * 17 + e:CF_ALIBI + h * 17 + e + 1])
            ob = 2 + (qs * 4 + h) % 2
            P.mm(psA[:, ob, :], vflat[:, kb, h * 65:h * 65 + 128], PT[b], start=first, stop=last)
            if last:
                ot = oTs[(qs * 4 + h) % 2]
                for d in [d for d in deferred if d[1] == 'T' and (d[2][0] * 4 + d[2][1]) % 2 == (qs * 4 + h) % 2]:
                    deferred.remove(d)
                    fire(d)
                P.copy('dve' if h % 2 else 'act', ot[0:65, :], psA[0:65, ob, :])

        def emit_oT_transposes(qs, h, c, j):
            ot = oTs[(qs * 4 + h) % 2]
            cols = slice(c * 256 + j * 128, c * 256 + (j + 1) * 128)
            P.transpose(psA[:, 4 + j, (c * 4 + h) * 64:(c * 4 + h + 1) * 64], ot[0:64, cols], identf[0:64, 0:64])
            P.transpose(psSum[:, j * 8 + c * 4 + h:j * 8 + c * 4 + h + 1], ot[64:65, cols], identf[64:65, 64:65])

        def epilogue_a(qb):
            j = qb % 2
            onb = onbs[j]
            P.recip(rc, psSum[:, j * 8:(j + 1) * 8])
            P.ts('dve', nr1, rc[:, 4:8], neglam, None, op0=ALU.mult)
            P.tt('dve', h4(o0), h4(psA[:, 4 + j, 0:256]), rc[:, 0:4].unsqueeze(2).to_broadcast([128, 4, 64]), ALU.mult)
            P.tt('dve', h4(o1), h4(psA[:, 4 + j, 256:512]), nr1.unsqueeze(2).to_broadcast([128, 4, 64]), ALU.mult)
            P.tt('pool', o0, o0, o1, ALU.add)
            P.tt('dve', o1, o0, o0, ALU.mult)
            P.rsum(ss4, h4(o1))
            P.ts('dve', ss4, ss4, 1.0 / 64, EPS, op0=ALU.mult, op1=ALU.add)
            P.tt('pool', rs4, ss4, neghalf[:, 0:1].to_broadcast([128, 4]), ALU.pow)
            P.tt('dve', h4(o1), h4(o0), rs4.unsqueeze(2).to_broadcast([128, 4, 64]), ALU.mult)
            P.stt(onb, o1, 1.0 - lam_init, rowp[:, RP_SUBLN:RP_SUBLN + 256], ALU.mult, ALU.mult)

        def epilogue_b(qb):
            ql = slice(qb * 128, (qb + 1) * 128)
            j = qb % 2
            ts0 = j * 2
            for jj in range(2):
                P.transpose(pst[:, ts0 + jj, :], onbs[j][:, jj * 128:(jj + 1) * 128], identb[:])
            P.copy('act', mixT[:, 6:8, ql], pst[:, ts0:ts0 + 2, :])

        pend = []
        deferred = []

        def fire(item):
            _, kind, args = item
            if kind == 'T':
                qs_, h_, c_, j_ = args
                emit_oT_transposes(qs_, h_, c_, j_)
                if h_ == 3 and c_ == 1 and j_ == 1:
                    epilogue_a(2 * qs_)
                    epilogue_a(2 * qs_ + 1)
                    deferred.append([6, 'E', 2 * qs_])
                    deferred.append([6, 'E', 2 * qs_ + 1])
            else:
                epilogue_b(args)

        def retire():
            pu, pb = pend.pop(0)
            emit_exp_pv(pu, pb)
            for d in deferred:
                d[0] -= 1
            ready = [d for d in deferred if d[0] <= 0]
            for d in ready:
                deferred.remove(d)
                fire(d)
            if pu[4]:
                n_ = 0
                for c_ in range(2):
                    for j_ in range(2):
                        deferred.append([2 + n_, 'T', (pu[0], pu[1], c_, j_)])
                        n_ += 1

        for u in units:
            pend.append((u, emit_S(u)))
            if len(pend) > 2:
                retire()
        while pend:
            retire()
        while deferred:
            fire(deferred.pop(0))

        if stop != 'D':
            dump('mix%d' % li, mixT)
        chk('D')

        nb = norm_bufs()
        for s in range(NS):
            sl = slice(s * 512, (s + 1) * 512)
            for c in range(8):
                half, cj = c // 4, c % 4
                b = psbank()
                for mc in range(8):
                    P.mm(psA[:, b, :], pO[half][:, mc, cj * 128:(cj + 1) * 128], mixT[:, mc, sl], start=(mc == 0), stop=(mc == 7))
                P.tt('dve', xT[:, c, sl], xT[:, c, sl], psA[:, b, :], ALU.add)
            if s >= 1:
                norm_slab(s - 1, CP_N2W, nb)
        norm_slab(NS - 1, CP_N2W, nb)

        dump('x1_%d' % li, xT)
        chk('wout')

        groups = []
        j0 = 0
        while j0 < NJ:
            n = min(4, NJ - j0)
            groups.append((j0, n))
            j0 += n

        def load_group(gi):
            j0, n = groups[gi]
            base = 3 * (gi % 2)
            wg = load_piece(base + 0, wg_d[li][:, j0 * 128:(j0 + n) * 128].rearrange("(c p) n -> p c n", p=128), n * 128)
            wu = load_piece(base + 1, wu_d[li][:, j0 * 128:(j0 + n) * 128].rearrange("(c p) n -> p c n", p=128), n * 128)
            wd = load_piece(base + 2, wd_d[li][j0 * 128:(j0 + n) * 128, :].rearrange("(j p) n -> p j n", p=128), n, kind="down")
            return wg, wu, wd

        loaded = {0: load_group(0)}
        if len(groups) > 1:
            loaded[1] = load_group(1)
        dump('h2_%d' % li, hT)
        chk('norm2')
        bmf = Bump(arena, 48 * 1024, ARENA)
        sg = [bmf.alloc([128, 512], BF16) for _ in range(2)]
        actT = [mixT[:, 0:4, 0:512], mixT[:, 4:8, 0:512]]
        fuse_next = (li + 1 < nlayers) and stop is None
        fuse_final = (li + 1 == nlayers) and stop is None and not dbg_d
        if fuse_final:
            bmo = Bump(arena, 0, 24 * 1024)
            fin_osts = [bmo.alloc([128, D], F32) for _ in range(2)]
            assert (len(groups) - 1) % 2 == 1
        steps = [(gi, s_) for gi in range(len(groups)) for s_ in range(NS)]

        def ffn_gu(k):
            gi, s_ = steps[k]
            n = groups[gi][1]
            wg, wu, wd = loaded[gi]
            sl = slice(s_ * 512, (s_ + 1) * 512)
            at = actT[k % 2]
            for jb in range(n):
                bg = psbank(0, 2)
                bu = psbank(2, 4)
                for dc in range(8):
                    P.mm(psA[:, bg, :], wg[:, dc, jb * 128:(jb + 1) * 128], hT[:, dc, sl], start=(dc == 0), stop=(dc == 7))
                for dc in range(8):
                    P.mm(psA[:, bu, :], wu[:, dc, jb * 128:(jb + 1) * 128], hT[:, dc, sl], start=(dc == 0), stop=(dc == 7))
                sgt = sg[(k * 4 + jb) % 2]
                P.act(sgt, psA[:, bg, :], AF.Silu)
                P.tt('dve', at[:, jb, :], sgt, psA[:, bu, :], ALU.mult)

        def ffn_dn(k):
            gi, s_ = steps[k]
            n = groups[gi][1]
            wg, wu, wd = loaded[gi]
            sl = slice(s_ * 512, (s_ + 1) * 512)
            at = actT[k % 2]
            for c in range(8):
                bd = psbank(4, 6)
                for jb in range(n):
                    P.mm(psA[:, bd, :], wd[:, jb, c * 128:(c + 1) * 128], at[:, jb, :], start=(jb == 0), stop=(jb == n - 1))
                P.tt('dve', xT[:, c, sl], xT[:, c, sl], psA[:, bd, :], ALU.add)

        ffn_gu(0)
        for k, (gi, s_) in enumerate(steps):
            lastg = gi == len(groups) - 1
            if lastg and s_ == 0 and fuse_next:
                load_params(li + 1)
            if k + 1 < len(steps):
                ffn_gu(k + 1)
            ffn_dn(k)
            if lastg and fuse_next and s_ >= 1:
                norm_slab(s_ - 1, CP_N1W, nb)
            if lastg and fuse_final and s_ >= 1:
                for t in range(4 * (s_ - 1), 4 * s_):
                    final_tile(t, fin_osts, 4, 6)
            if s_ == NS - 1:
                if lastg and fuse_next:
                    norm_slab(NS - 1, CP_N1W, nb)
                if lastg and fuse_final:
                    for t in range(4 * (NS - 1), 4 * NS):
                        final_tile(t, fin_osts, 4, 6)
                    final_done_flag.append(True)
                if gi + 2 < len(groups):
                    loaded[gi + 2] = load_group(gi + 2)

        dump('x2_%d' % li, xT)
        P.mark('ffn')

    if stop is not None:
        for c in range(8):
            P.memset('pool', mixT[:, c, :], 0.0)
    fuse0 = stop is None
    if fuse0:
        load_params(0)
    load_x(fuse0)
    P.mark('loadx')
    try:
        if stop != 'loadx':
            for li in range(nlayers):
                layer(li, pre_normed=(stop is None))
    except _Stop:
        pass

    final_done = bool(final_done_flag)
    if not final_done:
        bmo = Bump(arena, 0, ARENA)
        osts = [bmo.alloc([128, D], F32) for _ in range(2)]
        for t in range(NT):
            final_tile(t, osts)
    toks = list(final_tok) + [(s, v[1]) for s, v in P.dsem.items() if s == 'd_dbg']
    P.wait_all('sp', toks)

    with nc.Block() as block:
        P.emit(block)
    es.close()
    return nc, P


def host_pack(inputs):
    f = lambda k: np.asarray(inputs[k], dtype=np.float32)
    colp = np.zeros((2, 128, NCOLP), np.float32)
    rowp = np.zeros((2, 1, NROWP), np.float32)
    p = np.arange(128)
    for l in range(2):
        colp[l, :, CP_N1W:CP_N1W + 8] = f('norm1_w')[l].reshape(8, 128).T
        colp[l, :, CP_N2W:CP_N2W + 8] = f('norm2_w')[l].reshape(8, 128).T
        colp[l, :, CP_CONVW:CP_CONVW + 24] = f('ssm_conv_w')[l].reshape(6, 128, 4).transpose(1, 0, 2).reshape(128, 24)
        colp[l, :, CP_CONVB:CP_CONVB + 6] = f('ssm_conv_b')[l].reshape(6, 128).T
        for hc in range(2):
            colp[l, :, CP_DCOL + hc] = f('ssm_d')[l][2 * hc + p // 64]
        colp[l, :, CP_SSMNW:CP_SSMNW + 2] = f('ssm_norm_w')[l].reshape(2, 128).T
        colp[l, :, CP_PSCALE:CP_PSCALE + 2] = f('pool_scale')[l].reshape(2, 128).T
        colp[l, :, CP_QNW] = f('da_q_norm_w')[l][p % 32]
        colp[l, :, CP_KNW] = f('da_k_norm_w')[l][p % 32]
        colp[l, :, CP_BS:CP_BS + 4] = f('gm_bs')[l].T
        rowp[l, 0, RP_GMNW:RP_GMNW + 256] = f('gm_norm_w')[l].reshape(256)
        rowp[l, 0, RP_DTB:RP_DTB + 4] = f('ssm_dt_bias')[l]
        rowp[l, 0, RP_ALOG:RP_ALOG + 4] = f('ssm_a_log')[l]
        rowp[l, 0, RP_LQ1:RP_LQ1 + 32] = f('da_lambda_q1')[l]
        rowp[l, 0, RP_LK1:RP_LK1 + 32] = f('da_lambda_k1')[l]
        rowp[l, 0, RP_LQ2:RP_LQ2 + 32] = f('da_lambda_q2')[l]
        rowp[l, 0, RP_LK2:RP_LK2 + 32] = f('da_lambda_k2')[l]
        rowp[l, 0, RP_SUBLN:RP_SUBLN + 256] = np.tile(f('da_subln_w')[l], 4)
    shared = {
        "w_in": np.ascontiguousarray(f('w_in')),
        "w_out": np.ascontiguousarray(f('w_out')),
        "ffn_w_gate": np.ascontiguousarray(f('ffn_w_gate')),
        "ffn_w_up": np.ascontiguousarray(f('ffn_w_up')),
        "ffn_w_down": np.ascontiguousarray(f('ffn_w_down')),
        "gm_wsT": np.ascontiguousarray(f('gm_ws').transpose(0, 1, 3, 2)),
        "pool_w": np.ascontiguousarray(f('pool_w')),
        "colp": colp,
        "rowp": rowp,
        "cf": make_consts(),
    }
    return shared


_CACHE = {}


def kernel(**inputs):
    if 'nc' not in _CACHE:
        _CACHE['nc'] = build(2)[0]
    nc = _CACHE['nc']
    shared = host_pack(inputs)
    x = np.asarray(inputs['x'], dtype=np.float32)
    in_maps = []
    for b in range(8):
        m = dict(shared)
        m["x"] = np.ascontiguousarray(x[b])
        in_maps.append(m)
    res = run_bass_kernel_spmd(nc, in_maps, core_ids=list(range(8)))
    out = np.stack([np.asarray(res.results[b]["out"], dtype=np.float32) for b in range(8)], axis=0)
    return out
```

```python
import math
from contextlib import ExitStack

import numpy as np
import concourse.bass as bass
import concourse.mybir as mybir
from concourse.bass_utils import run_bass_kernel_spmd

F32 = mybir.dt.float32
BF16 = mybir.dt.bfloat16
U8 = mybir.dt.uint8
AF = mybir.ActivationFunctionType
ALU = mybir.AluOpType
AX = mybir.AxisListType


def _esize(dt):
    if dt == F32:
        return 4
    if dt == BF16:
        return 2
    if dt == U8:
        return 1
    s = str(dt)
    if '32' in s:
        return 4
    if '16' in s:
        return 2
    return 1


def ap_region(ap):
    pat = ap.ap
    pstep, pcnt = pat[0]
    es = _esize(ap.dtype)
    off = ap.offset
    if pstep == 0:
        row = 1
        for d in list(ap.tensor.shape)[1:]:
            row *= int(d)
        pstep = row
    p0 = off // pstep
    f0 = off % pstep
    ext = 0
    for st, cnt in pat[1:]:
        ext += abs(st) * (cnt - 1)
    lo, hi = f0 * es, (f0 + ext + 1) * es
    nm = ap.tensor.name
    if nm in PSUM_NAMES:
        return (nm, 0, 128, (lo // 2048) * 2048, ((hi + 2047) // 2048) * 2048)
    return (nm, p0, p0 + pcnt, lo, hi)


PSUM_NAMES = ('psA', 'pst')


class Prog:
    ENGS = ('pe', 'act', 'dve', 'pool', 'sp')

    def __init__(self, nc, sem_alloc):
        self.nc = nc
        self.sem_alloc = sem_alloc
        self.cnt = {e: 0 for e in self.ENGS}
        self.plan = {e: [] for e in self.ENGS}
        self.waited = {e: {} for e in self.ENGS}
        self.dsem = {}
        self.acc = {}
        self.semh = {}
        for e in ('pe', 'act', 'dve', 'pool'):
            self.semh['c_' + e] = sem_alloc('c_' + e)
        self.n_wait = 0
        self.n_ops = 0
        self.marks = []
        self.K = {}
        self.vc = {}

    def mark(self, name):
        self.marks.append((name, dict(self.cnt)))

    def _deps(self, eng, reads, writes):
        need = {}
        own = 'c_' + eng
        for is_w, aps in ((False, reads), (True, writes)):
            for ap in aps:
                nm, plo, phi, lo, hi = ap_region(ap)
                psum = nm in PSUM_NAMES
                for r in self.acc.get(nm, ()):
                    if r[0] < phi and plo < r[1] and r[2] < hi and lo < r[3]:
                        s, v = r[5]
                        if not (is_w or r[4] or (psum and s != own)):
                            continue
                        if need.get(s, 0) < v:
                            need[s] = v
        return need

    def _record(self, reads, writes, tok):
        for ap in writes:
            nm, plo, phi, lo, hi = ap_region(ap)
            lst = self.acc.setdefault(nm, [])
            lst[:] = [r for r in lst if not (plo <= r[0] and r[1] <= phi and lo <= r[2] and r[3] <= hi)]
            lst.append((plo, phi, lo, hi, True, tok))
        for ap in reads:
            nm, plo, phi, lo, hi = ap_region(ap)
            lst = self.acc.setdefault(nm, [])
            lst[:] = [r for r in lst if not ((not r[4]) and r[5][0] == tok[0]
                                             and plo <= r[0] and r[1] <= phi and lo <= r[2] and r[3] <= hi)]
            lst.append((plo, phi, lo, hi, False, tok))

    def _resolve(self, eng, need):
        waits = []
        own = 'c_' + eng
        K = self.K.setdefault(eng, {})
        for s, v in sorted(need.items(), key=lambda kv: -kv[1]):
            if s.startswith('d_'):
                v = max(v, self.dsem[s][1])
            if s == own and eng == 'pe':
                continue
            if K.get(s, 0) >= v:
                continue
            waits.append((s, v))
            K[s] = v
            snap = self.vc.get((s, v))
            if snap is None and s.startswith('d_'):
                snap = self.vc.get((s, self.dsem[s][1]))
            if snap:
                for s2, v2 in snap.items():
                    if K.get(s2, 0) < v2:
                        K[s2] = v2
        return waits

    def op(self, eng, thunk, reads=(), writes=(), milestone=True):
        reads = [r for r in reads if r is not None and not isinstance(r, (int, float))]
        writes = list(writes)
        waits = self._resolve(eng, self._deps(eng, reads, writes))
        if milestone:
            self.cnt[eng] += 1
            tok = ('c_' + eng, self.cnt[eng])
            self.vc[tok] = dict(self.K.get(eng, {}))
            self.plan[eng].append((waits, thunk, tok))
        else:
            tok = ('c_' + eng, self.cnt[eng] + 1)
            self.plan[eng].append((waits, thunk, None))
        self._record(reads, writes, tok)
        self.n_wait += len(waits)
        self.n_ops += 1
        return tok

    def dma(self, queue, out, in_, key, reads_sb=(), writes_sb=(), **kw):
        s = 'd_' + key
        if s not in self.dsem:
            h = self.sem_alloc(s)
            self.dsem[s] = [h, 0]
            self.semh[s] = h
        waits = self._resolve(queue, self._deps(queue, list(reads_sb), list(writes_sb)))
        self.dsem[s][1] += 16
        tok = (s, self.dsem[s][1])
        self.vc[tok] = dict(self.K.get(queue, {}))
        self.plan[queue].append((waits, (lambda e, o=out, i=in_, k=kw: e.dma_start(out=o, in_=i, **k)), tok))
        self._record(list(reads_sb), list(writes_sb), tok)
        self.n_ops += 1
        return tok

    def wait_all(self, eng, toks):
        need = {}
        for s, v in toks:
            need[s] = max(need.get(s, 0), v)
        self.plan[eng].append((self._resolve(eng, need), None, None))

    def emit(self, block):
        semh = self.semh
        plan = self.plan

        def run(engname, e):
            for waits, thunk, tok in plan[engname]:
                if thunk is None:
                    standalone = list(waits)
                else:
                    standalone = list(waits[:-1])
                for k in range(0, len(standalone), 2):
                    w_ins = e.wait_ge(semh[standalone[k][0]], standalone[k][1])
                    if k + 1 < len(standalone):
                        w_ins._wait_ge(semh[standalone[k + 1][0]], standalone[k + 1][1])
                if thunk is None:
                    continue
                ins = thunk(e)
                if waits:
                    s, v = waits[-1]
                    ins._wait_ge(semh[s], v)
                if tok is None:
                    pass
                elif tok[0].startswith('d_'):
                    ins.then_inc(semh[tok[0]], 16)
                else:
                    ins.then_inc(semh[tok[0]], 1)

        @block.tensor
        def _(e):
            run('pe', e)

        @block.scalar
        def _(e):
            run('act', e)

        @block.vector
        def _(e):
            run('dve', e)

        @block.gpsimd
        def _(e):
            run('pool', e)

        @block.sync
        def _(e):
            run('sp', e)

    def mm(self, out, lhsT, rhs, start=True, stop=True, group_inc=False, **kw):
        return self.op('pe', lambda e: e.matmul(out, lhsT, rhs, start=start, stop=stop, **kw),
                       reads=[lhsT, rhs], writes=[out], milestone=(bool(stop) or not group_inc))

    def transpose(self, out, in_, ident):
        return self.op('pe', lambda e: e.transpose(out, in_, ident), reads=[in_, ident], writes=[out])

    def act(self, out, in_, func, bias=None, scale=None):
        kw = {}
        rd = [in_]
        if bias is not None:
            kw['bias'] = bias
            rd.append(bias)
        if scale is not None:
            kw['scale'] = scale
            rd.append(scale)
        return self.op('act', lambda e: e.activation(out, in_, func, **kw), reads=rd, writes=[out])

    def tt(self, eng, out, in0, in1, op):
        return self.op(eng, lambda e: e.tensor_tensor(out, in0, in1, op), reads=[in0, in1], writes=[out])

    def ts(self, eng, out, in0, s1, s2=None, op0=ALU.mult, op1=None):
        kw = {}
        if op1 is not None:
            kw['op1'] = op1
        return self.op(eng, lambda e: e.tensor_scalar(out, in0, s1, s2, op0, **kw),
                       reads=[in0, s1, s2], writes=[out])

    def stt(self, out, in0, scalar, in1, op0, op1):
        return self.op('dve', lambda e: e.scalar_tensor_tensor(out, in0, scalar, in1, op0, op1),
                       reads=[in0, scalar, in1], writes=[out])

    def copy(self, eng, out, in_):
        if eng == 'act':
            return self.op(eng, lambda e: e.copy(out, in_), reads=[in_], writes=[out])
        return self.op(eng, lambda e: e.tensor_copy(out, in_), reads=[in_], writes=[out])

    def memset(self, eng, out, val):
        return self.op(eng, lambda e: e.memset(out, val), reads=[], writes=[out])

    def rsum(self, out, in_):
        return self.op('dve', lambda e: e.reduce_sum(out, in_, AX.X), reads=[in_], writes=[out])

    def recip(self, out, in_):
        return self.op('dve', lambda e: e.reciprocal(out, in_), reads=[in_], writes=[out])


D = 1024
L = 2048
NT = 16
NS = 4
DIN = 2564
DFF = 2816
NJ = DFF // 128
EPS = 1e-6
SM_SCALE = 32 ** -0.5
SLOPES = [2.0 ** (-8.0 * (h + 1) / 4) for h in range(4)]
POOL_WINDOWS = (2, 4, 8, 16)
NEGV = -30000.0

CP_N1W, CP_N2W, CP_CONVW, CP_CONVB, CP_DCOL, CP_SSMNW, CP_PSCALE, CP_QNW, CP_KNW, CP_BS = 0, 8, 16, 40, 46, 48, 50, 52, 53, 54
NCOLP = 58
RP_GMNW, RP_DTB, RP_ALOG, RP_LQ1, RP_LK1, RP_LQ2, RP_LK2, RP_SUBLN = 0, 256, 260, 264, 296, 328, 360, 392
NROWP = 648
CF_U, CF_SL, CF_NEG, CF_ALIBI, CF_INVWIN, CF_INVC16, CF_IDENT, CF_BD32, CF_MASKC, CF_MASKH = 0, 128, 256, 384, 452, 454, 486, 614, 742, 744
NCF = 746

ARENA = 70 * 1024


def make_consts():
    cf = np.zeros((128, NCF), np.float32)
    j = np.arange(128)[:, None]
    l = np.arange(128)[None, :]
    cf[:, CF_U:CF_U + 128] = (j <= l)
    cf[:, CF_SL:CF_SL + 128] = (j > l)
    cf[:, CF_NEG:CF_NEG + 128] = np.where(j <= l, 0.0, NEGV)
    for h in range(4):
        for e in range(17):
            cf[:, CF_ALIBI + h * 17 + e] = SLOPES[h] * (np.arange(128) - 128.0 * e)
    cf[:, CF_MASKC] = (np.arange(128) % 64 < 32)
    cf[:, CF_MASKC + 1] = (np.arange(128) % 64 >= 32)
    cf[:, CF_MASKH] = (np.arange(128) < 64)
    cf[:, CF_MASKH + 1] = (np.arange(128) >= 64)
    for c in range(2):
        for p in range(128):
            win = POOL_WINDOWS[2 * c + p // 64]
            cf[p, CF_INVWIN + c] = 1.0 / win
            for t in range(16):
                cf[p, CF_INVC16 + c * 16 + t] = 1.0 / min(t + 1, win)
    cf[:, CF_IDENT:CF_IDENT + 128] = np.eye(128)
    cf[:, CF_BD32:CF_BD32 + 128] = (j // 32 == l // 32)
    return cf


class Bump:
    def __init__(self, arena, base, limit):
        self.arena = arena
        self.off = base
        self.limit = limit

    def alloc(self, shape, dt):
        n = 1
        for s in shape[1:]:
            n *= s
        nb = n * _esize(dt)
        off = (self.off + 31) // 32 * 32
        assert off + nb <= self.limit, (off, nb, self.limit)
        self.off = off + nb
        v = self.arena[:, off:off + nb].bitcast(dt)
        if len(shape) == 3:
            v = v.rearrange("p (a b) -> p a b", a=shape[1])
        elif len(shape) == 4:
            v = v.rearrange("p (a b c) -> p a b c", a=shape[1], b=shape[2])
        return v


class _Stop(Exception):
    pass


def build(nlayers=2, dbg=(), stop=None):
    nc = bass.Bass("TRN2", target_bir_lowering=False)

    def din(name, shape):
        return nc.dram_tensor(name, list(shape), F32, kind="ExternalInput").ap()

    x_d = din("x", [L, D])
    w_in_d = din("w_in", [2, D, DIN])
    w_out_d = din("w_out", [2, D, D])
    wg_d = din("ffn_w_gate", [2, D, DFF])
    wu_d = din("ffn_w_up", [2, D, DFF])
    wd_d = din("ffn_w_down", [2, DFF, D])
    wsT_d = din("gm_wsT", [2, 4, 128, 128])
    poolw_d = din("pool_w", [2, 4, 64, 64])
    colp_d = din("colp", [2, 128, NCOLP])
    rowp_d = din("rowp", [2, 1, NROWP])
    cf_d = din("cf", [128, NCF])
    out_d = nc.dram_tensor("out", [L, D], F32, kind="ExternalOutput").ap()
    dbg_d = {}
    for name in dbg:
        dbg_d[name] = nc.dram_tensor("dbg_" + name, [128, 8, L], F32, kind="ExternalOutput").ap()

    es = ExitStack()

    def sb(name, shape, dt):
        return es.enter_context(nc.sbuf_tensor("s_" + name, list(shape), dt))

    def sem(name):
        return es.enter_context(nc.semaphore(name))

    P = Prog(nc, sem)

    xT = sb("xT", [128, 8, L], F32)
    hT = sb("hT", [128, 8, L], BF16)
    mixT = sb("mixT", [128, 8, L], BF16)
    arena = sb("arena", [128, ARENA], U8)
    cF = sb("cF", [128, NCF], F32)
    colp = sb("colp", [128, NCOLP], F32)
    rowp = sb("rowp", [128, NROWP], F32)
    identb = sb("identb", [128, 128], BF16)
    Ub = sb("Ub", [128, 128], BF16)
    NEGb = sb("NEGb", [128, 128], BF16)
    bd32b = sb("bd32b", [128, 128], BF16)
    onesb = sb("onesb", [128, 128], BF16)
    neghalf = sb("neghalf", [128, 1], F32)
    small = sb("small", [128, 64], F32)
    psA = es.enter_context(nc.psum_tensor("psA", [128, 6, 512], F32))
    pst = es.enter_context(nc.psum_tensor("pst", [128, 16, 128], BF16))

    identf = cF[:, CF_IDENT:CF_IDENT + 128]
    Uf = cF[:, CF_U:CF_U + 128]
    SLf = cF[:, CF_SL:CF_SL + 128]
    NEGf = cF[:, CF_NEG:CF_NEG + 128]

    RING_SLOT = 8192

    def dump(name, src):
        if name in dbg_d:
            for c in range(8):
                P.dma('pool', dbg_d[name][:, c, :], src[:, c, :], 'dbg', reads_sb=[src[:, c, :]])

    def chk(phase):
        P.mark(phase)
        if stop == phase:
            dump('mix0', mixT)
            raise _Stop()

    def ring_slot(i, kind="in"):
        v = arena[:, i * RING_SLOT:(i + 1) * RING_SLOT].bitcast(BF16)
        if kind == "in":
            return v.rearrange("p (c n) -> p c n", c=8)
        return v.rearrange("p (j n) -> p j n", j=4)

    def load_piece(slot, src, ncols, kind="in"):
        dst = ring_slot(slot, kind)
        if kind == "in":
            dst = dst[:, :, 0:ncols]
        else:
            dst = dst[:, 0:ncols, :]
        P.dma('pool', dst, src, 'ring%d' % slot, writes_sb=[dst])
        return dst

    P.dma('sp', cF[:], cf_d, 'cf', writes_sb=[cF[:]])
    P.copy('dve', identb[:], identf)
    P.copy('dve', Ub[:], Uf)
    P.copy('dve', NEGb[:], NEGf)
    P.copy('dve', bd32b[:], cF[:, CF_BD32:CF_BD32 + 128])
    P.memset('dve', onesb[:], 1.0)
    P.memset('dve', neghalf[:], -0.5)

    M0 = sb("M0", [128, 512], BF16)
    M1 = sb("M1", [128, 512], BF16)
    m04 = M0[:].rearrange("p (c j q) -> p c j q", c=2, j=2)
    m14 = M1[:].rearrange("p (c j q) -> p c j q", c=2, j=2)
    P.memset('pool', M0[:], 0.0)
    P.memset('pool', M1[:], NEGV)
    for c in range(2):
        P.copy('pool', m04[:, c, 0, :], NEGb[:])
        P.copy('pool', m14[:, c, 1, :], NEGb[:])

    psrot = {}

    def psbank(lo=0, hi=6):
        k = (lo, hi)
        v = psrot.get(k, 0)
        psrot[k] = v + 1
        return lo + v % (hi - lo)

    def load_x(fuse_norm):
        bm = Bump(arena, 16384, ARENA)
        stage = [bm.alloc([128, D], F32) for _ in range(4)]
        nbx = norm_bufs() if fuse_norm else None
        for t in range(NT):
            st = stage[t % 4]
            P.dma('sp', st, x_d[t * 128:(t + 1) * 128, :], 'xin%d' % (t % 4), writes_sb=[st])
            for half in range(2):
                b = psbank()
                for c4 in range(4):
                    c = half * 4 + c4
                    P.transpose(psA[:, b, c4 * 128:(c4 + 1) * 128], st[:, c * 128:(c + 1) * 128], identf)
                src = psA[:, b, :].rearrange("p (c t) -> p c t", c=4)
                dst = xT[:, half * 4:half * 4 + 4, t * 128:(t + 1) * 128]
                P.copy('act' if (t + half) % 2 == 0 else 'dve', dst, src)
            if fuse_norm and t % 4 == 3 and t >= 7:
                norm_slab(t // 4 - 1, CP_N1W, nbx)
        if fuse_norm:
            norm_slab(NS - 1, CP_N1W, nbx)

    def norm_bufs():
        bm = Bump(arena, 50 * 1024, ARENA)
        sq = [bm.alloc([128, 512], BF16) for _ in range(2)]
        ms = bm.alloc([128, 512], F32)
        rs = bm.alloc([128, 512], F32)
        return sq, ms, rs

    def norm_slab(s, nw_off, bufs):
        sq, ms, rs = bufs
        sl = slice(s * 512, (s + 1) * 512)
        b = psbank()
        for c in range(8):
            P.act(sq[c % 2], xT[:, c, sl], AF.Square)
            P.mm(psA[:, b, :], onesb[:], sq[c % 2], start=(c == 0), stop=(c == 7))
        P.act(ms, psA[:, b, :], AF.Ln, scale=1.0 / D, bias=EPS)
        P.act(rs, ms, AF.Exp, scale=-0.5)
        for c in range(8):
            P.stt(hT[:, c, sl], xT[:, c, sl], colp[:, nw_off + c:nw_off + c + 1], rs, ALU.mult, ALU.mult)

    def norm_full(nw_off):
        bufs = norm_bufs()
        for s in range(NS):
            norm_slab(s, nw_off, bufs)

    def proj_fm(wpiece, col0, s, b, ncols=128):
        sl = slice(s * 512, (s + 1) * 512)
        for dc in range(8):
            P.mm(psA[0:ncols, b, :], wpiece[:, dc, col0:col0 + ncols], hT[:, dc, sl], start=(dc == 0), stop=(dc == 7), group_inc=True)

    final_tok = [None, None]
    final_done_flag = []

    def final_tile(t, osts, lo=0, hi=6):
        st = osts[t % 2]
        for half in range(2):
            b = psbank(lo, hi)
            for c4 in range(4):
                c = half * 4 + c4
                P.transpose(psA[:, b, c4 * 128:(c4 + 1) * 128], xT[:, c, t * 128:(t + 1) * 128], identf)
            P.copy('act' if half == 0 else 'dve', st[:, half * 512:(half + 1) * 512], psA[:, b, :])
        final_tok[t % 2] = P.dma('sp', out_d[t * 128:(t + 1) * 128, :], st, 'out%d' % (t % 2), reads_sb=[st])

    def load_params(li):
        P.dma('sp', colp[:], colp_d[li], 'params', writes_sb=[colp[:]])
        P.dma('sp', rowp[:], rowp_d[li].partition_broadcast(128), 'params', writes_sb=[rowp[:]])

    def layer(li, pre_normed=False):
        lam_init = 0.8 - 0.6 * math.exp(-0.3 * li)
        win = w_in_d[li]

        def win_piece(c0, n):
            return win[:, c0:c0 + n].rearrange("(c p) n -> p c n", p=128)

        if not pre_normed:
            load_params(li)
        pA = load_piece(0, win_piece(0, 512), 512)
        pC = load_piece(1, win_piece(1540, 256), 256)

        if not pre_normed:
            norm_full(CP_N1W)
        dump('h1_%d' % li, hT)
        chk('norm1')

        W0 = 16384

        bm = Bump(arena, W0, ARENA)
        wsTm = bm.alloc([128, 4, 128], BF16)
        NBA = 7
        gA = [bm.alloc([128, 512], F32) for _ in range(NBA)]
        gB = [bm.alloc([128, 512], F32) for _ in range(NBA)]
        vnb = [bm.alloc([128, 256], BF16) for _ in range(NBA)]
        yab = [bm.alloc([128, 256], BF16) for _ in range(NBA)]
        ssv = [bm.alloc([128, 4], F32) for _ in range(NBA)]
        rsv = [bm.alloc([128, 4], F32) for _ in range(NBA)]
        P.dma('pool', wsTm, wsT_d[li].rearrange("h s t -> s h t"), 'wsT', writes_sb=[wsTm])
        P.tt('dve', wsTm, wsTm, Ub[:].unsqueeze(1).to_broadcast([128, 4, 128]), ALU.mult)
        gmnw = rowp[:, RP_GMNW:RP_GMNW + 256]
        v4 = lambda ap: ap.rearrange("p (h d) -> p h d", h=4)
        pa_of = {}

        def a_st(st, t):
            tl = slice(t * 128, (t + 1) * 128)
            par = t % NBA
            a, bb = gA[par], gB[par]
            if st == 0:
                pa = psA[:, t % 4, :]
                pa_of[t] = pa
                for dc in range(8):
                    P.mm(pa, hT[:, dc, tl], pA[:, dc, :], start=(dc == 0), stop=(dc == 7))
                P.act(a, pa, AF.Square, scale=0.044715 ** 0.5)
            elif st == 1:
                P.stt(bb, a, 1.0, pa_of[t], ALU.add, ALU.mult)
                P.act(a, bb, AF.Sigmoid, scale=1.5957691216057308)
            elif st == 2:
                P.tt('dve', a, a, pa_of[t], ALU.mult)
                P.act(bb[:, 0:256], a[:, 256:512], AF.Square)
            elif st == 3:
                P.rsum(ssv[par], v4(bb[:, 0:256]))
                P.ts('dve', ssv[par], ssv[par], 1.0 / 64, EPS, op0=ALU.mult, op1=ALU.add)
                P.tt('pool', rsv[par], ssv[par], neghalf[:, 0:1].to_broadcast([128, 4]), ALU.pow)
            elif st == 4:
                P.tt('pool', v4(bb[:, 0:256]), v4(a[:, 256:512]), rsv[par].unsqueeze(2).to_broadcast([128, 4, 64]), ALU.mult)
                P.tt('pool', vnb[par], bb[:, 0:256], gmnw, ALU.mult)
            elif st == 5:
                b2 = psbank(4, 6)
                for h in range(4):
                    P.mm(psA[:, b2, h * 64:(h + 1) * 64], wsTm[:, h, :], vnb[par][:, h * 64:(h + 1) * 64], start=True, stop=True)
                for h in range(4):
                    P.stt(yab[par][:, h * 64:(h + 1) * 64], psA[:, b2, h * 64:(h + 1) * 64],
                          colp[:, CP_BS + h:CP_BS + h + 1], a[:, h * 64:(h + 1) * 64], ALU.add, ALU.mult)
            else:
                ts0 = (t % 2) * 8
                for j in range(2):
                    P.transpose(pst[:, ts0 + j, :], yab[par][:, j * 128:(j + 1) * 128], identb[:])
                P.copy('act', mixT[:, 0:2, tl], pst[:, ts0:ts0 + 2, :])

        NST = 7
        for i in range(NT + NST - 1):
            for st in range(NST):
                t = i - st
                if 0 <= t < NT:
                    a_st(st, t)

        chk('A')
        pB1 = load_piece(0, win_piece(512, 512), 512)

        bm = Bump(arena, W0, ARENA)
        wbd = [bm.alloc([128, 128], BF16) for _ in range(2)]
        pcxs = [bm.alloc([128, 16 + L], F32) for _ in range(2)]
        L1 = bm.alloc([128, 16 + L], F32)
        L2 = bm.alloc([128, 16 + L], F32)
        pT = bm.alloc([128, L], BF16)
        t16 = bm.alloc([128, 16], F32)
        for cc in range(2):
            P.memset('pool', wbd[cc], 0.0)
            for gg in range(2):
                dst = wbd[cc][gg * 64:(gg + 1) * 64, gg * 64:(gg + 1) * 64]
                P.dma('pool', dst, poolw_d[li][2 * cc + gg], 'wbd', writes_sb=[dst])
        P.memset('pool', pcxs[0][:, 0:16], 0.0)
        P.memset('pool', pcxs[1][:, 0:16], 0.0)
        P.memset('pool', L1[:, 0:16], 0.0)
        P.memset('pool', L2[:, 0:16], 0.0)
        for cc in range(2):
            for s in range(NS):
                b = psbank()
                proj_fm(pC, cc * 128, s, b)
                P.copy('act', pcxs[cc][:, 16 + s * 512:16 + (s + 1) * 512], psA[:, b, :])
        for cc in range(2):
            pcx = pcxs[cc]

            def shadd(dst, src, k, prt=slice(0, 128), eng='dve'):
                P.tt(eng, dst[prt, 16:16 + L], src[prt, 16:16 + L], src[prt, 16 - k:16 - k + L], ALU.add)

            shadd(L1, pcx, 1)
            hi = slice(64, 128)
            lo = slice(0, 64)
            if cc == 0:
                shadd(L2, L1, 2, hi)
            else:
                shadd(L2, L1, 2)
                shadd(L1, L2, 4)
                shadd(L2, L1, 8, hi)
            for prt, S in ((lo, L1), (hi, L2)):
                P.stt(pT[prt, 16:L], S[prt, 32:16 + L], cF[prt, CF_INVWIN + cc:CF_INVWIN + cc + 1],
                      pcx[prt, 32:16 + L], ALU.mult, ALU.subtract)
                P.tt('dve', t16[prt, :], S[prt, 16:32], cF[prt, CF_INVC16 + cc * 16:CF_INVC16 + (cc + 1) * 16], ALU.mult)
                P.tt('dve', pT[prt, 0:16], t16[prt, :], pcx[prt, 16:32], ALU.subtract)
            for s in range(NS):
                sl = slice(s * 512, (s + 1) * 512)
                b = psbank()
                P.mm(psA[:, b, :], wbd[cc], pT[:, sl], start=True, stop=True)
                P.act(mixT[:, 4 + cc, sl], psA[:, b, :], AF.Identity, scale=colp[:, CP_PSCALE + cc:CP_PSCALE + cc + 1])

        chk('C')
        pB2 = load_piece(1, win_piece(1024, 512), 512)

        bm = Bump(arena, W0, ARENA)
        zs = bm.alloc([128, 2, L], BF16)
        xsT = bm.alloc([128, 2, L], BF16)
        BT = bm.alloc([128, 2, L], BF16)
        CT = bm.alloc([128, 2, L], BF16)
        dtm = bm.alloc([128, 64], F32)
        atm = bm.alloc([128, 64], F32)
        acs = bm.alloc([128, 64], F32)
        dte = bm.alloc([128, 64], F32)
        b1_start = bm.off
        cin = [bm.alloc([128, 4 + 512], BF16) for _ in range(2)]
        bm_cin3 = bm.alloc([128, 4 + 512], BF16)
        accs = [bm.alloc([128, 512], F32) for _ in range(2)]
        dws = [bm.alloc([128, 4, 128], BF16) for _ in range(2)]
        wdt = bm.alloc([128, 8, 4], BF16)
        dtr = bm.alloc([128, 64], F32)
        t64a = bm.alloc([128, 64], F32)
        t64b = bm.alloc([128, 64], F32)
        Ab = bm.alloc([128, 4], F32)
        P.dma('pool', wdt, win_piece(1536, 4), 'wdt', writes_sb=[wdt])

        bD = psbank()
        for t in range(NT):
            for dc in range(8):
                P.mm(psA[:, bD, t * 4:(t + 1) * 4], hT[:, dc, t * 128:(t + 1) * 128], wdt[:, dc, :], start=(dc == 0), stop=(dc == 7))
        v3 = lambda ap: ap.rearrange("p (c h) -> p c h", h=4)
        P.tt('dve', v3(dtr), v3(psA[:, bD, 0:64]), rowp[:, RP_DTB:RP_DTB + 4].unsqueeze(1).to_broadcast([128, 16, 4]), ALU.add)
        P.act(t64a, dtr, AF.Abs)
        P.act(t64a, t64a, AF.Exp, scale=-1.0)
        P.act(t64a, t64a, AF.Ln, bias=1.0)
        P.ts('dve', t64b, dtr, 0.0, None, op0=ALU.max)
        P.tt('dve', dtm, t64a, t64b, ALU.add)
        P.act(Ab, rowp[:, RP_ALOG:RP_ALOG + 4], AF.Exp)
        P.ts('dve', Ab, Ab, -1.0, None, op0=ALU.mult)
        P.tt('dve', v3(atm), v3(dtm), Ab.unsqueeze(1).to_broadcast([128, 16, 4]), ALU.mult)
        bX = psbank()
        P.mm(psA[:, bX, 0:64], Uf, atm, start=True, stop=True)
        P.copy('dve', acs, psA[:, bX, 0:64])
        P.mm(psA[:, bX, 64:128], SLf, atm, start=True, stop=True)
        P.act(dte, psA[:, bX, 64:128], AF.Exp)

        for zc in range(2):
            for s in range(NS):
                b = psbank()
                proj_fm(pB1, zc * 128, s, b)
                P.act(zs[:, zc, s * 512:(s + 1) * 512], psA[:, b, :], AF.Silu)
        dests = [xsT[:, 0, :], xsT[:, 1, :], BT[:, 0, :], BT[:, 1, :], CT[:, 0, :], CT[:, 1, :]]
        cin = cin + [bm_cin3]
        conv_units = [(cb, s_) for cb in range(6) for s_ in range(NS)]

        def conv_proj(i):
            cb, s_ = conv_units[i]
            piece, col0 = (pB1, 256 + cb * 128) if cb < 2 else (pB2, (cb - 2) * 128)
            if s_ == 0:
                dw = dws[cb % 2]
                cw = CP_CONVW + cb * 4
                for kk in range(2):
                    P.ts('dve', dw[:, kk, :], identb[:], colp[:, cw + kk:cw + kk + 1], None, op0=ALU.mult)
            ci = cin[i % 3]
            prev = cin[(i - 1) % 3]
            b = psbank(0, 3)
            proj_fm(piece, col0, s_, b)
            P.copy('act', ci[:, 3:515], psA[:, b, :])
            if s_ == 0:
                P.memset('pool', ci[:, 0:3], 0.0)
            else:
                P.copy('pool', ci[:, 0:3], prev[:, 512:515])

        def conv_mm(i):
            cb, s_ = conv_units[i]
            dw = dws[cb % 2]
            ci = cin[i % 3]
            ac = accs[i % 2]
            cw = CP_CONVW + cb * 4
            b2 = psbank(3, 6)
            for kk in range(2):
                P.mm(psA[:, b2, :], dw[:, kk, :], ci[:, kk:kk + 512], start=(kk == 0), stop=(kk == 1))
            P.stt(ac, ci[:, 2:2 + 512], colp[:, cw + 2:cw + 3], psA[:, b2, :], ALU.mult, ALU.add)
            P.stt(ac, ci[:, 3:3 + 512], colp[:, cw + 3:cw + 4], ac, ALU.mult, ALU.add)
            P.act(dests[cb][:, s_ * 512:(s_ + 1) * 512], ac, AF.Silu, bias=colp[:, CP_CONVB + cb:CP_CONVB + cb + 1])

        conv_proj(0)
        for i in range(len(conv_units)):
            if i + 1 < len(conv_units):
                conv_proj(i + 1)
            conv_mm(i)
        chk('B1')
        pD1 = load_piece(0, win_piece(1796, 512), 512)
        pD2 = load_piece(1, win_piece(2308, 256), 256)

        bm2 = Bump(arena, b1_start, ARENA)
        xdts = [bm2.alloc([128, 256], BF16) for _ in range(2)]
        xdtw = bm2.alloc([128, 256], BF16)
        Btm = bm2.alloc([128, 256], BF16)
        t1 = bm2.alloc([128, 512], F32)
        MTs = [bm2.alloc([128, 512], BF16) for _ in range(2)]
        E = bm2.alloc([128, 512], F32)
        Cdecs = [bm2.alloc([128, 512], BF16) for _ in range(2)]
        cds = [bm2.alloc([128, 4], F32) for _ in range(2)]
        S = bm2.alloc([128, 256], F32)
        Sbf = bm2.alloc([128, 256], BF16)
        yg = bm2.alloc([128, 2, 512], F32)
        sqy = bm2.alloc([128, 512], BF16)
        msy = t1
        rsy = E
        P.memset('pool', S, 0.0)
        P.memset('pool', Sbf, 0.0)
        h4 = lambda ap: ap.rearrange("p (h d) -> p h d", h=4)
        g22 = lambda ap: ap.rearrange("p (g r l) -> p g r l", g=2, r=2)

        def b2_X(c):
            par = c % 2
            cl = slice(c * 128, (c + 1) * 128)
            xdt, MT, Cdec, cd = xdts[par], MTs[par], Cdecs[par], cds[par]
            tb = par * 8
            P.transpose(pst[:, tb + 0, :], xsT[:, 0, cl], identb[:])
            P.transpose(pst[:, tb + 1, :], xsT[:, 1, cl], identb[:])
            P.transpose(pst[:, tb + 2, :], BT[:, 0, cl], identb[:])
            P.transpose(pst[:, tb + 3, :], BT[:, 1, cl], identb[:])
            P.tt('dve', h4(xdt), pst[:, tb:tb + 2, :].rearrange("p a (h d) -> p (a h) d", h=2),
                 dtm[:, c * 4:(c + 1) * 4].unsqueeze(2).to_broadcast([128, 4, 64]), ALU.mult)
            P.tt('pool', h4(xdtw), h4(xdt), dte[:, c * 4:(c + 1) * 4].unsqueeze(2).to_broadcast([128, 4, 64]), ALU.mult)
            P.copy('act', Btm.rearrange("p (a n) -> p a n", a=2), pst[:, tb + 2:tb + 4, :])
            bC = psbank(0, 2)
            for g in range(2):
                P.mm(psA[:, bC, g * 128:(g + 1) * 128], BT[:, g, cl], CT[:, g, cl], start=True, stop=True)
            bR = psbank(2, 4)
            for h in range(4):
                P.mm(psA[:, bR, h * 128:(h + 1) * 128], atm[:, c * 4 + h:c * 4 + h + 1].to_broadcast([128, 128]), Uf,
                     start=True, stop=True)
            R4 = psA[:, bR, :].rearrange("p (h l) -> p h l", h=4)
            t14 = t1.rearrange("p (h l) -> p h l", h=4)
            for h in range(4):
                P.stt(t14[:, h, :], R4[:, h, :], acs[:, c * 4 + h:c * 4 + h + 1], NEGf, ALU.subtract, ALU.add)
            P.act(t1, t1, AF.Exp)
            P.tt('dve', g22(MT), g22(t1),
                 psA[:, bC, 0:256].rearrange("p (g l) -> p g l", g=2).unsqueeze(2).to_broadcast([128, 2, 2, 128]), ALU.mult)
            P.act(E, psA[:, bR, :], AF.Exp)
            P.copy('act', cd, E.rearrange("p (h l) -> p h l", h=4)[:, :, 127])
            P.tt('pool', g22(Cdec), g22(E), CT[:, :, cl].unsqueeze(2).to_broadcast([128, 2, 2, 128]), ALU.mult)
            bY = 4 + par
            for h in range(4):
                P.mm(psA[:, bY, 256 + h * 64:256 + (h + 1) * 64], Btm[:, (h // 2) * 128:(h // 2 + 1) * 128],
                     xdtw[:, h * 64:(h + 1) * 64], start=True, stop=True)

        def b2_Y(c):
            par = c % 2
            cl = slice(c * 128, (c + 1) * 128)
            xdt, MT, Cdec, cd = xdts[par], MTs[par], Cdecs[par], cds[par]
            bY = 4 + par
            for h in range(4):
                o = psA[(h % 2) * 64:(h % 2 + 1) * 64, bY, (h // 2) * 128:(h // 2 + 1) * 128]
                P.mm(o, xdt[:, h * 64:(h + 1) * 64], MT[:, h * 128:(h + 1) * 128], start=True, stop=False)
                P.mm(o, Sbf[:, h * 64:(h + 1) * 64], Cdec[:, h * 128:(h + 1) * 128], start=False, stop=True)
            for h in range(4):
                P.stt(S[:, h * 64:(h + 1) * 64], S[:, h * 64:(h + 1) * 64], cd[:, h:h + 1],
                      psA[:, bY, 256 + h * 64:256 + (h + 1) * 64], ALU.mult, ALU.add)
            P.copy('act', Sbf, S)
            cs = (c % 4) * 128
            for hc in range(2):
                P.stt(yg[:, hc, cs:cs + 128], xsT[:, hc, cl], colp[:, CP_DCOL + hc:CP_DCOL + hc + 1],
                      psA[:, bY, hc * 128:(hc + 1) * 128], ALU.mult, ALU.add)
            if c % 4 == 3:
                s = c // 4
                sl = slice(s * 512, (s + 1) * 512)
                for hc in range(2):
                    P.tt('pool', yg[:, hc, :], yg[:, hc, :], zs[:, hc, sl], ALU.mult)
                    P.act(sqy, yg[:, hc, :], AF.Square)
                    bN = psbank(0, 2)
                    P.mm(psA[:, bN, :], onesb[:], sqy, start=True, stop=True)
                    P.act(msy, psA[:, bN, :], AF.Ln, scale=1.0 / 128, bias=EPS)
                    P.act(rsy, msy, AF.Exp, scale=-0.5)
                    P.stt(mixT[:, 2 + hc, sl], yg[:, hc, :], colp[:, CP_SSMNW + hc:CP_SSMNW + hc + 1], rsy, ALU.mult, ALU.mult)

        b2_X(0)
        for c in range(NT):
            if c + 1 < NT:
                b2_X(c + 1)
            b2_Y(c)

        chk('B2')
        bm = Bump(arena, W0, ARENA)
        qT2 = bm.alloc([128, 2, 2, L], BF16)
        kT2 = bm.alloc([128, 2, 2, L], BF16)
        vflat = bm.alloc([128, NT, 324], BF16)
        vaug = vflat[:, :, 0:260].rearrange("p t (h e) -> p t h e", h=4)
        PT = [bm.alloc([128, 512], BF16) for _ in range(3)]
        oTs = [bm.alloc([128, 512], F32) for _ in range(2)]
        sqb = PT[0]
        msq = oTs[0]
        rsq = oTs[1]
        o0 = bm.alloc([128, 256], F32)
        o1 = bm.alloc([128, 256], F32)
        onbs = [bm.alloc([128, 256], BF16) for _ in range(2)]
        lt = bm.alloc([128, 32], F32)
        s1 = small[:, 0:1]
        s2 = small[:, 1:2]
        neglam = small[:, 2:3]
        qnws = small[:, 4:6]
        rc = small[:, 8:16]
        nr1 = small[:, 16:20]
        ss4 = small[:, 20:24]
        rs4 = small[:, 24:28]
        P.tt('dve', lt, rowp[:, RP_LQ1:RP_LQ1 + 32], rowp[:, RP_LK1:RP_LK1 + 32], ALU.mult)
        P.rsum(s1, lt)
        P.tt('dve', lt, rowp[:, RP_LQ2:RP_LQ2 + 32], rowp[:, RP_LK2:RP_LK2 + 32], ALU.mult)
        P.rsum(s2, lt)
        P.act(small[:, 0:2], small[:, 0:2], AF.Exp)
        P.tt('dve', neglam, s2, s1, ALU.subtract)
        P.ts('dve', neglam, neglam, -lam_init, None, op0=ALU.add)
        P.ts('dve', qnws, cF[:, CF_MASKC:CF_MASKC + 2], colp[:, CP_QNW:CP_QNW + 1], SM_SCALE, op0=ALU.mult, op1=ALU.mult)
        P.memset('pool', vflat[:, :, 260:324], 0.0)
        P.memset('pool', vaug[:, :, :, 64:65], 1.0)
        P.ts('dve', small[:, 6:8], cF[:, CF_MASKH:CF_MASKH + 2], colp[:, CP_KNW:CP_KNW + 1], None, op0=ALU.mult)
        for blk in range(4):
            for s in range(NS):
                sl = slice(s * 512, (s + 1) * 512)
                b = psbank(0, 3)
                proj_fm(pD1, blk * 128, s, b)
                P.act(sqb, psA[:, b, :], AF.Square)
                b2 = psbank(3, 6)
                P.mm(psA[:, b2, :], bd32b[:], sqb, start=True, stop=True)
                P.act(msq, psA[:, b2, :], AF.Ln, scale=1.0 / 32, bias=EPS)
                P.act(rsq, msq, AF.Exp, scale=-0.5)
                if blk < 2:
                    for c in range(2):
                        P.stt(qT2[:, blk, c, sl], psA[:, b, :], qnws[:, c:c + 1], rsq, ALU.mult, ALU.mult)
                else:
                    for hh in range(2):
                        P.stt(kT2[:, blk - 2, hh, sl], psA[:, b, :], small[:, 6 + hh:7 + hh], rsq, ALU.mult, ALU.mult)
        for t in range(NT):
            b = psbank()
            for dc in range(8):
                P.mm(psA[:, b, 0:256], hT[:, dc, t * 128:(t + 1) * 128], pD2[:, dc, :], start=(dc == 0), stop=(dc == 7))
            P.copy('act' if t % 2 else 'dve', vaug[:, t, :, 0:64], psA[:, b, 0:256].rearrange("p (h d) -> p h d", h=4))

        pO = [load_piece(0, w_out_d[li][:, 0:512].rearrange("(c p) n -> p c n", p=128), 512),
              load_piece(1, w_out_d[li][:, 512:1024].rearrange("(c p) n -> p c n", p=128), 512)]

        psSum = pst[:, 4:8, :].rearrange("p a b -> p (a b)").bitcast(F32)
        S3 = pst[:, 8:16, :].rearrange("p a b -> p (a b)").bitcast(F32)
        Sbanks = [psA[:, 0, :], psA[:, 1, :], S3]
        units = []
        for qs in range(8):
            for h in range(4):
                kbs = []
                for kb in range(2 * qs + 2):
                    if SLOPES[h] * (256 * qs - (128 * kb + 127)) > 130.0:
                        continue
                    kbs.append(kb)
                for n_, kb in enumerate(kbs):
                    units.append((qs, h, kb, n_ == 0, n_ == len(kbs) - 1))
        s_rot = [0]

        def emit_S(u):
            qs, h, kb, first, last = u
            kc, hh = h // 2, h % 2
            b = s_rot[0] % 3
            s_rot[0] += 1
            diag = kb >= 2 * qs
            P.mm(Sbanks[b], kT2[:, kc, hh, kb * 128:(kb + 1) * 128],
                 qT2[:, kc, :, qs * 256:(qs + 1) * 256].rearrange("p c q -> p (c q)") if False else qT2[:, kc, :, qs * 256:(qs + 1) * 256],
                 start=True, stop=not diag)
            if diag:
                P.mm(Sbanks[b], identb[:], (M0 if kb == 2 * qs else M1)[:], start=False, stop=True)
            return b

        def emit_exp_pv(u, b):
            qs, h, kb, first, last = u
            e = 2 * qs - kb + 1
            P.act(PT[b], Sbanks[b], AF.Exp, bias=cF[:, CF_ALIBI + h * 17 + e:CF_ALIBI + h * 17 + e + 1])
            ob = 2 + (qs * 4 + h) % 2
            P.mm(psA[:, ob, :], vflat[:, kb, h * 65:h * 65 + 128], PT[b], start=first, stop=last)
            if last:
                ot = oTs[(qs * 4 + h) % 2]
                for d in [d for d in deferred if d[1] == 'T' and (d[2][0] * 4 + d[2][1]) % 2 == (qs * 4 + h) % 2]:
                    deferred.remove(d)
                    fire(d)
                P.copy('dve' if h % 2 else 'act', ot[0:65, :], psA[0:65, ob, :])

        def emit_oT_transposes(qs, h, c, j):
            ot = oTs[(qs * 4 + h) % 2]
            cols = slice(c * 256 + j * 128, c * 256 + (j + 1) * 128)
            P.transpose(psA[:, 4 + j, (c * 4 + h) * 64:(c * 4 + h + 1) * 64], ot[0:64, cols], identf[0:64, 0:64])
            P.transpose(psSum[:, j * 8 + c * 4 + h:j * 8 + c * 4 + h + 1], ot[64:65, cols], identf[64:65, 64:65])

        def epilogue_a(qb):
            j = qb % 2
            onb = onbs[j]
            P.recip(rc, psSum[:, j * 8:(j + 1) * 8])
            P.ts('dve', nr1, rc[:, 4:8], neglam, None, op0=ALU.mult)
            P.tt('dve', h4(o0), h4(psA[:, 4 + j, 0:256]), rc[:, 0:4].unsqueeze(2).to_broadcast([128, 4, 64]), ALU.mult)
            P.tt('dve', h4(o1), h4(psA[:, 4 + j, 256:512]), nr1.unsqueeze(2).to_broadcast([128, 4, 64]), ALU.mult)
            P.tt('pool', o0, o0, o1, ALU.add)
            P.tt('dve', o1, o0, o0, ALU.mult)
            P.rsum(ss4, h4(o1))
            P.ts('dve', ss4, ss4, 1.0 / 64, EPS, op0=ALU.mult, op1=ALU.add)
            P.tt('pool', rs4, ss4, neghalf[:, 0:1].to_broadcast([128, 4]), ALU.pow)
            P.tt('dve', h4(o1), h4(o0), rs4.unsqueeze(2).to_broadcast([128, 4, 64]), ALU.mult)
            P.stt(onb, o1, 1.0 - lam_init, rowp[:, RP_SUBLN:RP_SUBLN + 256], ALU.mult, ALU.mult)

        def epilogue_b(qb):
            ql = slice(qb * 128, (qb + 1) * 128)
            j = qb % 2
            ts0 = j * 2
            for jj in range(2):
                P.transpose(pst[:, ts0 + jj, :], onbs[j][:, jj * 128:(jj + 1) * 128], identb[:])
            P.copy('act', mixT[:, 6:8, ql], pst[:, ts0:ts0 + 2, :])

        pend = []
        deferred = []

        def fire(item):
            _, kind, args = item
            if kind == 'T':
                qs_, h_, c_, j_ = args
                emit_oT_transposes(qs_, h_, c_, j_)
                if h_ == 3 and c_ == 1 and j_ == 1:
                    epilogue_a(2 * qs_)
                    epilogue_a(2 * qs_ + 1)
                    deferred.append([6, 'E', 2 * qs_])
                    deferred.append([6, 'E', 2 * qs_ + 1])
            else:
                epilogue_b(args)

        def retire():
            pu, pb = pend.pop(0)
            emit_exp_pv(pu, pb)
            for d in deferred:
                d[0] -= 1
            ready = [d for d in deferred if d[0] <= 0]
            for d in ready:
                deferred.remove(d)
                fire(d)
            if pu[4]:
                n_ = 0
                for c_ in range(2):
                    for j_ in range(2):
                        deferred.append([2 + n_, 'T', (pu[0], pu[1], c_, j_)])
                        n_ += 1

        for u in units:
            pend.append((u, emit_S(u)))
            if len(pend) > 2:
                retire()
        while pend:
            retire()
        while deferred:
            fire(deferred.pop(0))

        if stop != 'D':
            dump('mix%d' % li, mixT)
        chk('D')

        nb = norm_bufs()
        for s in range(NS):
            sl = slice(s * 512, (s + 1) * 512)
            for c in range(8):
                half, cj = c // 4, c % 4
                b = psbank()
                for mc in range(8):
                    P.mm(psA[:, b, :], pO[half][:, mc, cj * 128:(cj + 1) * 128], mixT[:, mc, sl], start=(mc == 0), stop=(mc == 7), group_inc=True)
                P.tt('dve', xT[:, c, sl], xT[:, c, sl], psA[:, b, :], ALU.add)
            if s >= 1:
                norm_slab(s - 1, CP_N2W, nb)
        norm_slab(NS - 1, CP_N2W, nb)

        dump('x1_%d' % li, xT)
        chk('wout')

        groups = []
        j0 = 0
        while j0 < NJ:
            n = min(4, NJ - j0)
            groups.append((j0, n))
            j0 += n

        def load_group(gi):
            j0, n = groups[gi]
            base = 3 * (gi % 2)
            wg = load_piece(base + 0, wg_d[li][:, j0 * 128:(j0 + n) * 128].rearrange("(c p) n -> p c n", p=128), n * 128)
            wu = load_piece(base + 1, wu_d[li][:, j0 * 128:(j0 + n) * 128].rearrange("(c p) n -> p c n", p=128), n * 128)
            wd = load_piece(base + 2, wd_d[li][j0 * 128:(j0 + n) * 128, :].rearrange("(j p) n -> p j n", p=128), n, kind="down")
            return wg, wu, wd

        loaded = {0: load_group(0)}
        if len(groups) > 1:
            loaded[1] = load_group(1)
        dump('h2_%d' % li, hT)
        chk('norm2')
        bmf = Bump(arena, 48 * 1024, ARENA)
        sg = [bmf.alloc([128, 512], BF16) for _ in range(2)]
        actT = [mixT[:, 0:4, 0:512], mixT[:, 4:8, 0:512]]
        fuse_next = (li + 1 < nlayers) and stop is None
        fuse_final = (li + 1 == nlayers) and stop is None and not dbg_d
        if fuse_final:
            bmo = Bump(arena, 0, 24 * 1024)
            fin_osts = [bmo.alloc([128, D], F32) for _ in range(2)]
            assert (len(groups) - 1) % 2 == 1
        steps = [(gi, s_) for gi in range(len(groups)) for s_ in range(NS)]

        def ffn_gu(k):
            gi, s_ = steps[k]
            n = groups[gi][1]
            wg, wu, wd = loaded[gi]
            sl = slice(s_ * 512, (s_ + 1) * 512)
            at = actT[k % 2]
            for jb in range(n):
                bg = psbank(0, 2)
                bu = psbank(2, 4)
                for dc in range(8):
                    P.mm(psA[:, bg, :], wg[:, dc, jb * 128:(jb + 1) * 128], hT[:, dc, sl], start=(dc == 0), stop=(dc == 7), group_inc=True)
                for dc in range(8):
                    P.mm(psA[:, bu, :], wu[:, dc, jb * 128:(jb + 1) * 128], hT[:, dc, sl], start=(dc == 0), stop=(dc == 7), group_inc=True)
                sgt = sg[(k * 4 + jb) % 2]
                P.act(sgt, psA[:, bg, :], AF.Silu)
                P.tt('dve', at[:, jb, :], sgt, psA[:, bu, :], ALU.mult)

        def ffn_dn(k):
            gi, s_ = steps[k]
            n = groups[gi][1]
            wg, wu, wd = loaded[gi]
            sl = slice(s_ * 512, (s_ + 1) * 512)
            at = actT[k % 2]
            for c in range(8):
                bd = psbank(4, 6)
                for jb in range(n):
                    P.mm(psA[:, bd, :], wd[:, jb, c * 128:(c + 1) * 128], at[:, jb, :], start=(jb == 0), stop=(jb == n - 1), group_inc=True)
                P.tt('dve', xT[:, c, sl], xT[:, c, sl], psA[:, bd, :], ALU.add)

        ffn_gu(0)
        for k, (gi, s_) in enumerate(steps):
            lastg = gi == len(groups) - 1
            if lastg and s_ == 0 and fuse_next:
                load_params(li + 1)
            if k + 1 < len(steps):
                ffn_gu(k + 1)
            ffn_dn(k)
            if lastg and fuse_next and s_ >= 1:
                norm_slab(s_ - 1, CP_N1W, nb)
            if lastg and fuse_final and s_ >= 1:
                for t in range(4 * (s_ - 1), 4 * s_):
                    final_tile(t, fin_osts, 4, 6)
            if s_ == NS - 1:
                if lastg and fuse_next:
                    norm_slab(NS - 1, CP_N1W, nb)
                if lastg and fuse_final:
                    for t in range(4 * (NS - 1), 4 * NS):
                        final_tile(t, fin_osts, 4, 6)
                    final_done_flag.append(True)
                if gi + 2 < len(groups):
                    loaded[gi + 2] = load_group(gi + 2)

        dump('x2_%d' % li, xT)
        P.mark('ffn')

    if stop is not None:
        for c in range(8):
            P.memset('pool', mixT[:, c, :], 0.0)
    fuse0 = stop is None
    if fuse0:
        load_params(0)
    load_x(fuse0)
    P.mark('loadx')
    try:
        if stop != 'loadx':
            for li in range(nlayers):
                layer(li, pre_normed=(stop is None))
    except _Stop:
        pass

    final_done = bool(final_done_flag)
    if not final_done:
        bmo = Bump(arena, 0, ARENA)
        osts = [bmo.alloc([128, D], F32) for _ in range(2)]
        for t in range(NT):
            final_tile(t, osts)
    toks = list(final_tok) + [(s, v[1]) for s, v in P.dsem.items() if s == 'd_dbg']
    P.wait_all('sp', toks)

    with nc.Block() as block:
        P.emit(block)
    es.close()
    return nc, P


def host_pack(inputs):
    f = lambda k: np.asarray(inputs[k], dtype=np.float32)
    colp = np.zeros((2, 128, NCOLP), np.float32)
    rowp = np.zeros((2, 1, NROWP), np.float32)
    p = np.arange(128)
    for l in range(2):
        colp[l, :, CP_N1W:CP_N1W + 8] = f('norm1_w')[l].reshape(8, 128).T
        colp[l, :, CP_N2W:CP_N2W + 8] = f('norm2_w')[l].reshape(8, 128).T
        colp[l, :, CP_CONVW:CP_CONVW + 24] = f('ssm_conv_w')[l].reshape(6, 128, 4).transpose(1, 0, 2).reshape(128, 24)
        colp[l, :, CP_CONVB:CP_CONVB + 6] = f('ssm_conv_b')[l].reshape(6, 128).T
        for hc in range(2):
            colp[l, :, CP_DCOL + hc] = f('ssm_d')[l][2 * hc + p // 64]
        colp[l, :, CP_SSMNW:CP_SSMNW + 2] = f('ssm_norm_w')[l].reshape(2, 128).T
        colp[l, :, CP_PSCALE:CP_PSCALE + 2] = f('pool_scale')[l].reshape(2, 128).T
        colp[l, :, CP_QNW] = f('da_q_norm_w')[l][p % 32]
        colp[l, :, CP_KNW] = f('da_k_norm_w')[l][p % 32]
        colp[l, :, CP_BS:CP_BS + 4] = f('gm_bs')[l].T
        rowp[l, 0, RP_GMNW:RP_GMNW + 256] = f('gm_norm_w')[l].reshape(256)
        rowp[l, 0, RP_DTB:RP_DTB + 4] = f('ssm_dt_bias')[l]
        rowp[l, 0, RP_ALOG:RP_ALOG + 4] = f('ssm_a_log')[l]
        rowp[l, 0, RP_LQ1:RP_LQ1 + 32] = f('da_lambda_q1')[l]
        rowp[l, 0, RP_LK1:RP_LK1 + 32] = f('da_lambda_k1')[l]
        rowp[l, 0, RP_LQ2:RP_LQ2 + 32] = f('da_lambda_q2')[l]
        rowp[l, 0, RP_LK2:RP_LK2 + 32] = f('da_lambda_k2')[l]
        rowp[l, 0, RP_SUBLN:RP_SUBLN + 256] = np.tile(f('da_subln_w')[l], 4)
    shared = {
        "w_in": np.ascontiguousarray(f('w_in')),
        "w_out": np.ascontiguousarray(f('w_out')),
        "ffn_w_gate": np.ascontiguousarray(f('ffn_w_gate')),
        "ffn_w_up": np.ascontiguousarray(f('ffn_w_up')),
        "ffn_w_down": np.ascontiguousarray(f('ffn_w_down')),
        "gm_wsT": np.ascontiguousarray(f('gm_ws').transpose(0, 1, 3, 2)),
        "pool_w": np.ascontiguousarray(f('pool_w')),
        "colp": colp,
        "rowp": rowp,
        "cf": make_consts(),
    }
    return shared


_CACHE = {}


def kernel(**inputs):
    if 'nc' not in _CACHE:
        _CACHE['nc'] = build(2)[0]
    nc = _CACHE['nc']
    shared = host_pack(inputs)
    x = np.asarray(inputs['x'], dtype=np.float32)
    in_maps = []
    for b in range(8):
        m = dict(shared)
        m["x"] = np.ascontiguousarray(x[b])
        in_maps.append(m)
    res = run_bass_kernel_spmd(nc, in_maps, core_ids=list(range(8)))
    out = np.stack([np.asarray(res.results[b]["out"], dtype=np.float32) for b in range(8)], axis=0)
    return out
```

```python
import math
from contextlib import ExitStack

import numpy as np
import concourse.bass as bass
import concourse.mybir as mybir
from concourse.bass_utils import run_bass_kernel_spmd

F32 = mybir.dt.float32
BF16 = mybir.dt.bfloat16
U8 = mybir.dt.uint8
AF = mybir.ActivationFunctionType
ALU = mybir.AluOpType
AX = mybir.AxisListType


def _esize(dt):
    if dt == F32:
        return 4
    if dt == BF16:
        return 2
    if dt == U8:
        return 1
    s = str(dt)
    if '32' in s:
        return 4
    if '16' in s:
        return 2
    return 1


def ap_region(ap):
    pat = ap.ap
    pstep, pcnt = pat[0]
    es = _esize(ap.dtype)
    off = ap.offset
    if pstep == 0:
        row = 1
        for d in list(ap.tensor.shape)[1:]:
            row *= int(d)
        pstep = row
    p0 = off // pstep
    f0 = off % pstep
    ext = 0
    for st, cnt in pat[1:]:
        ext += abs(st) * (cnt - 1)
    lo, hi = f0 * es, (f0 + ext + 1) * es
    nm = ap.tensor.name
    if nm in PSUM_NAMES:
        return (nm, 0, 128, (lo // 2048) * 2048, ((hi + 2047) // 2048) * 2048)
    return (nm, p0, p0 + pcnt, lo, hi)


PSUM_NAMES = ('psA', 'pst')


class Prog:
    ENGS = ('pe', 'act', 'dve', 'pool', 'sp')

    def __init__(self, nc, sem_alloc):
        self.nc = nc
        self.sem_alloc = sem_alloc
        self.cnt = {e: 0 for e in self.ENGS}
        self.plan = {e: [] for e in self.ENGS}
        self.waited = {e: {} for e in self.ENGS}
        self.dsem = {}
        self.acc = {}
        self.semh = {}
        for e in ('pe', 'act', 'dve', 'pool'):
            self.semh['c_' + e] = sem_alloc('c_' + e)
        self.n_wait = 0
        self.n_ops = 0
        self.marks = []
        self.K = {}
        self.vc = {}

    def mark(self, name):
        self.marks.append((name, dict(self.cnt)))

    def _deps(self, eng, reads, writes):
        need = {}
        own = 'c_' + eng
        for is_w, aps in ((False, reads), (True, writes)):
            for ap in aps:
                nm, plo, phi, lo, hi = ap_region(ap)
                psum = nm in PSUM_NAMES
                for r in self.acc.get(nm, ()):
                    if r[0] < phi and plo < r[1] and r[2] < hi and lo < r[3]:
                        s, v = r[5]
                        if not (is_w or r[4] or (psum and s != own)):
                            continue
                        if need.get(s, 0) < v:
                            need[s] = v
        return need

    def _record(self, reads, writes, tok):
        for ap in writes:
            nm, plo, phi, lo, hi = ap_region(ap)
            lst = self.acc.setdefault(nm, [])
            lst[:] = [r for r in lst if not (plo <= r[0] and r[1] <= phi and lo <= r[2] and r[3] <= hi)]
            lst.append((plo, phi, lo, hi, True, tok))
        for ap in reads:
            nm, plo, phi, lo, hi = ap_region(ap)
            lst = self.acc.setdefault(nm, [])
            lst[:] = [r for r in lst if not ((not r[4]) and r[5][0] == tok[0]
                                             and plo <= r[0] and r[1] <= phi and lo <= r[2] and r[3] <= hi)]
            lst.append((plo, phi, lo, hi, False, tok))

    def _resolve(self, eng, need):
        waits = []
        own = 'c_' + eng
        K = self.K.setdefault(eng, {})
        for s, v in sorted(need.items(), key=lambda kv: -kv[1]):
            if s.startswith('d_'):
                v = max(v, self.dsem[s][1])
            if s == own and eng == 'pe':
                continue
            if K.get(s, 0) >= v:
                continue
            waits.append((s, v))
            K[s] = v
            snap = self.vc.get((s, v))
            if snap is None and s.startswith('d_'):
                snap = self.vc.get((s, self.dsem[s][1]))
            if snap:
                for s2, v2 in snap.items():
                    if K.get(s2, 0) < v2:
                        K[s2] = v2
        return waits

    def op(self, eng, thunk, reads=(), writes=()):
        reads = [r for r in reads if r is not None and not isinstance(r, (int, float))]
        writes = list(writes)
        waits = self._resolve(eng, self._deps(eng, reads, writes))
        self.cnt[eng] += 1
        tok = ('c_' + eng, self.cnt[eng])
        self.vc[tok] = dict(self.K.get(eng, {}))
        self.plan[eng].append((waits, thunk, tok))
        self._record(reads, writes, tok)
        self.n_wait += len(waits)
        self.n_ops += 1
        return tok

    def dma(self, queue, out, in_, key, reads_sb=(), writes_sb=(), **kw):
        s = 'd_' + key
        if s not in self.dsem:
            h = self.sem_alloc(s)
            self.dsem[s] = [h, 0]
            self.semh[s] = h
        waits = self._resolve(queue, self._deps(queue, list(reads_sb), list(writes_sb)))
        self.dsem[s][1] += 16
        tok = (s, self.dsem[s][1])
        self.vc[tok] = dict(self.K.get(queue, {}))
        self.plan[queue].append((waits, (lambda e, o=out, i=in_, k=kw: e.dma_start(out=o, in_=i, **k)), tok))
        self._record(list(reads_sb), list(writes_sb), tok)
        self.n_ops += 1
        return tok

    def wait_all(self, eng, toks):
        need = {}
        for s, v in toks:
            need[s] = max(need.get(s, 0), v)
        self.plan[eng].append((self._resolve(eng, need), None, None))

    def emit(self, block):
        semh = self.semh
        plan = self.plan

        def run(engname, e):
            for waits, thunk, tok in plan[engname]:
                if thunk is None:
                    standalone = list(waits)
                else:
                    standalone = list(waits[:-1])
                for k in range(0, len(standalone), 2):
                    w_ins = e.wait_ge(semh[standalone[k][0]], standalone[k][1])
                    if k + 1 < len(standalone):
                        w_ins._wait_ge(semh[standalone[k + 1][0]], standalone[k + 1][1])
                if thunk is None:
                    continue
                ins = thunk(e)
                if waits:
                    s, v = waits[-1]
                    ins._wait_ge(semh[s], v)
                if tok is None:
                    pass
                elif tok[0].startswith('d_'):
                    ins.then_inc(semh[tok[0]], 16)
                else:
                    ins.then_inc(semh[tok[0]], 1)

        @block.tensor
        def _(e):
            run('pe', e)

        @block.scalar
        def _(e):
            run('act', e)

        @block.vector
        def _(e):
            run('dve', e)

        @block.gpsimd
        def _(e):
            run('pool', e)

        @block.sync
        def _(e):
            run('sp', e)

    def mm(self, out, lhsT, rhs, start=True, stop=True, **kw):
        return self.op('pe', lambda e: e.matmul(out, lhsT, rhs, start=start, stop=stop, **kw),
                       reads=[lhsT, rhs], writes=[out])

    def transpose(self, out, in_, ident):
        return self.op('pe', lambda e: e.transpose(out, in_, ident), reads=[in_, ident], writes=[out])

    def act(self, out, in_, func, bias=None, scale=None):
        kw = {}
        rd = [in_]
        if bias is not None:
            kw['bias'] = bias
            rd.append(bias)
        if scale is not None:
            kw['scale'] = scale
            rd.append(scale)
        return self.op('act', lambda e: e.activation(out, in_, func, **kw), reads=rd, writes=[out])

    def tt(self, eng, out, in0, in1, op):
        return self.op(eng, lambda e: e.tensor_tensor(out, in0, in1, op), reads=[in0, in1], writes=[out])

    def ts(self, eng, out, in0, s1, s2=None, op0=ALU.mult, op1=None):
        kw = {}
        if op1 is not None:
            kw['op1'] = op1
        return self.op(eng, lambda e: e.tensor_scalar(out, in0, s1, s2, op0, **kw),
                       reads=[in0, s1, s2], writes=[out])

    def stt(self, out, in0, scalar, in1, op0, op1):
        return self.op('dve', lambda e: e.scalar_tensor_tensor(out, in0, scalar, in1, op0, op1),
                       reads=[in0, scalar, in1], writes=[out])

    def copy(self, eng, out, in_):
        if eng == 'act':
            return self.op(eng, lambda e: e.copy(out, in_), reads=[in_], writes=[out])
        return self.op(eng, lambda e: e.tensor_copy(out, in_), reads=[in_], writes=[out])

    def memset(self, eng, out, val):
        return self.op(eng, lambda e: e.memset(out, val), reads=[], writes=[out])

    def rsum(self, out, in_):
        return self.op('dve', lambda e: e.reduce_sum(out, in_, AX.X), reads=[in_], writes=[out])

    def recip(self, out, in_):
        return self.op('dve', lambda e: e.reciprocal(out, in_), reads=[in_], writes=[out])


D = 1024
L = 2048
NT = 16
NS = 4
DIN = 2564
DFF = 2816
NJ = DFF // 128
EPS = 1e-6
SM_SCALE = 32 ** -0.5
SLOPES = [2.0 ** (-8.0 * (h + 1) / 4) for h in range(4)]
POOL_WINDOWS = (2, 4, 8, 16)
NEGV = -30000.0

CP_N1W, CP_N2W, CP_CONVW, CP_CONVB, CP_DCOL, CP_SSMNW, CP_PSCALE, CP_QNW, CP_KNW, CP_BS = 0, 8, 16, 40, 46, 48, 50, 52, 53, 54
NCOLP = 58
RP_GMNW, RP_DTB, RP_ALOG, RP_LQ1, RP_LK1, RP_LQ2, RP_LK2, RP_SUBLN = 0, 256, 260, 264, 296, 328, 360, 392
NROWP = 648
CF_U, CF_SL, CF_NEG, CF_ALIBI, CF_INVWIN, CF_INVC16, CF_IDENT, CF_BD32, CF_MASKC, CF_MASKH = 0, 128, 256, 384, 452, 454, 486, 614, 742, 744
NCF = 746

ARENA = 70 * 1024


def make_consts():
    cf = np.zeros((128, NCF), np.float32)
    j = np.arange(128)[:, None]
    l = np.arange(128)[None, :]
    cf[:, CF_U:CF_U + 128] = (j <= l)
    cf[:, CF_SL:CF_SL + 128] = (j > l)
    cf[:, CF_NEG:CF_NEG + 128] = np.where(j <= l, 0.0, NEGV)
    for h in range(4):
        for e in range(17):
            cf[:, CF_ALIBI + h * 17 + e] = SLOPES[h] * (np.arange(128) - 128.0 * e)
    cf[:, CF_MASKC] = (np.arange(128) % 64 < 32)
    cf[:, CF_MASKC + 1] = (np.arange(128) % 64 >= 32)
    cf[:, CF_MASKH] = (np.arange(128) < 64)
    cf[:, CF_MASKH + 1] = (np.arange(128) >= 64)
    for c in range(2):
        for p in range(128):
            win = POOL_WINDOWS[2 * c + p // 64]
            cf[p, CF_INVWIN + c] = 1.0 / win
            for t in range(16):
                cf[p, CF_INVC16 + c * 16 + t] = 1.0 / min(t + 1, win)
    cf[:, CF_IDENT:CF_IDENT + 128] = np.eye(128)
    cf[:, CF_BD32:CF_BD32 + 128] = (j // 32 == l // 32)
    return cf


class Bump:
    def __init__(self, arena, base, limit):
        self.arena = arena
        self.off = base
        self.limit = limit

    def alloc(self, shape, dt):
        n = 1
        for s in shape[1:]:
            n *= s
        nb = n * _esize(dt)
        off = (self.off + 31) // 32 * 32
        assert off + nb <= self.limit, (off, nb, self.limit)
        self.off = off + nb
        v = self.arena[:, off:off + nb].bitcast(dt)
        if len(shape) == 3:
            v = v.rearrange("p (a b) -> p a b", a=shape[1])
        elif len(shape) == 4:
            v = v.rearrange("p (a b c) -> p a b c", a=shape[1], b=shape[2])
        return v


class _Stop(Exception):
    pass


def build(nlayers=2, dbg=(), stop=None):
    nc = bass.Bass("TRN2", target_bir_lowering=False)

    def din(name, shape):
        return nc.dram_tensor(name, list(shape), F32, kind="ExternalInput").ap()

    x_d = din("x", [L, D])
    w_in_d = din("w_in", [2, D, DIN])
    w_out_d = din("w_out", [2, D, D])
    wg_d = din("ffn_w_gate", [2, D, DFF])
    wu_d = din("ffn_w_up", [2, D, DFF])
    wd_d = din("ffn_w_down", [2, DFF, D])
    wsT_d = din("gm_wsT", [2, 4, 128, 128])
    poolw_d = din("pool_w", [2, 4, 64, 64])
    colp_d = din("colp", [2, 128, NCOLP])
    rowp_d = din("rowp", [2, 1, NROWP])
    cf_d = din("cf", [128, NCF])
    out_d = nc.dram_tensor("out", [L, D], F32, kind="ExternalOutput").ap()
    dbg_d = {}
    for name in dbg:
        dbg_d[name] = nc.dram_tensor("dbg_" + name, [128, 8, L], F32, kind="ExternalOutput").ap()

    es = ExitStack()

    def sb(name, shape, dt):
        return es.enter_context(nc.sbuf_tensor("s_" + name, list(shape), dt))

    def sem(name):
        return es.enter_context(nc.semaphore(name))

    P = Prog(nc, sem)

    xT = sb("xT", [128, 8, L], F32)
    hT = sb("hT", [128, 8, L], BF16)
    mixT = sb("mixT", [128, 8, L], BF16)
    arena = sb("arena", [128, ARENA], U8)
    cF = sb("cF", [128, NCF], F32)
    colp = sb("colp", [128, NCOLP], F32)
    rowp = sb("rowp", [128, NROWP], F32)
    identb = sb("identb", [128, 128], BF16)
    Ub = sb("Ub", [128, 128], BF16)
    NEGb = sb("NEGb", [128, 128], BF16)
    bd32b = sb("bd32b", [128, 128], BF16)
    onesb = sb("onesb", [128, 128], BF16)
    neghalf = sb("neghalf", [128, 1], F32)
    small = sb("small", [128, 64], F32)
    psA = es.enter_context(nc.psum_tensor("psA", [128, 6, 512], F32))
    pst = es.enter_context(nc.psum_tensor("pst", [128, 16, 128], BF16))

    identf = cF[:, CF_IDENT:CF_IDENT + 128]
    Uf = cF[:, CF_U:CF_U + 128]
    SLf = cF[:, CF_SL:CF_SL + 128]
    NEGf = cF[:, CF_NEG:CF_NEG + 128]

    RING_SLOT = 8192

    def dump(name, src):
        if name in dbg_d:
            for c in range(8):
                P.dma('pool', dbg_d[name][:, c, :], src[:, c, :], 'dbg', reads_sb=[src[:, c, :]])

    def chk(phase):
        P.mark(phase)
        if stop == phase:
            dump('mix0', mixT)
            raise _Stop()

    def ring_slot(i, kind="in"):
        v = arena[:, i * RING_SLOT:(i + 1) * RING_SLOT].bitcast(BF16)
        if kind == "in":
            return v.rearrange("p (c n) -> p c n", c=8)
        return v.rearrange("p (j n) -> p j n", j=4)

    def load_piece(slot, src, ncols, kind="in"):
        dst = ring_slot(slot, kind)
        if kind == "in":
            dst = dst[:, :, 0:ncols]
        else:
            dst = dst[:, 0:ncols, :]
        P.dma('pool', dst, src, 'ring%d' % slot, writes_sb=[dst])
        return dst

    P.dma('sp', cF[:], cf_d, 'cf', writes_sb=[cF[:]])
    P.copy('dve', identb[:], identf)
    P.copy('dve', Ub[:], Uf)
    P.copy('dve', NEGb[:], NEGf)
    P.copy('dve', bd32b[:], cF[:, CF_BD32:CF_BD32 + 128])
    P.memset('dve', onesb[:], 1.0)
    P.memset('dve', neghalf[:], -0.5)

    M0 = sb("M0", [128, 512], BF16)
    M1 = sb("M1", [128, 512], BF16)
    m04 = M0[:].rearrange("p (c j q) -> p c j q", c=2, j=2)
    m14 = M1[:].rearrange("p (c j q) -> p c j q", c=2, j=2)
    P.memset('pool', M0[:], 0.0)
    P.memset('pool', M1[:], NEGV)
    for c in range(2):
        P.copy('pool', m04[:, c, 0, :], NEGb[:])
        P.copy('pool', m14[:, c, 1, :], NEGb[:])

    psrot = {}

    def psbank(lo=0, hi=6):
        k = (lo, hi)
        v = psrot.get(k, 0)
        psrot[k] = v + 1
        return lo + v % (hi - lo)

    def load_x(fuse_norm):
        bm = Bump(arena, 16384, ARENA)
        stage = [bm.alloc([128, D], F32) for _ in range(4)]
        nbx = norm_bufs() if fuse_norm else None
        for t in range(NT):
            st = stage[t % 4]
            P.dma('sp', st, x_d[t * 128:(t + 1) * 128, :], 'xin%d' % (t % 4), writes_sb=[st])
            for half in range(2):
                b = psbank()
                for c4 in range(4):
                    c = half * 4 + c4
                    P.transpose(psA[:, b, c4 * 128:(c4 + 1) * 128], st[:, c * 128:(c + 1) * 128], identf)
                src = psA[:, b, :].rearrange("p (c t) -> p c t", c=4)
                dst = xT[:, half * 4:half * 4 + 4, t * 128:(t + 1) * 128]
                P.copy('act' if (t + half) % 2 == 0 else 'dve', dst, src)
            if fuse_norm and t % 4 == 3 and t >= 7:
                norm_slab(t // 4 - 1, CP_N1W, nbx)
        if fuse_norm:
            norm_slab(NS - 1, CP_N1W, nbx)

    def norm_bufs():
        bm = Bump(arena, 50 * 1024, ARENA)
        sq = [bm.alloc([128, 512], BF16) for _ in range(2)]
        ms = bm.alloc([128, 512], F32)
        rs = bm.alloc([128, 512], F32)
        return sq, ms, rs

    def norm_slab(s, nw_off, bufs):
        sq, ms, rs = bufs
        sl = slice(s * 512, (s + 1) * 512)
        b = psbank()
        for c in range(8):
            P.act(sq[c % 2], xT[:, c, sl], AF.Square)
            P.mm(psA[:, b, :], onesb[:], sq[c % 2], start=(c == 0), stop=(c == 7))
        P.act(ms, psA[:, b, :], AF.Ln, scale=1.0 / D, bias=EPS)
        P.act(rs, ms, AF.Exp, scale=-0.5)
        for c in range(8):
            P.stt(hT[:, c, sl], xT[:, c, sl], colp[:, nw_off + c:nw_off + c + 1], rs, ALU.mult, ALU.mult)

    def norm_full(nw_off):
        bufs = norm_bufs()
        for s in range(NS):
            norm_slab(s, nw_off, bufs)

    def proj_fm(wpiece, col0, s, b, ncols=128):
        sl = slice(s * 512, (s + 1) * 512)
        for dc in range(8):
            P.mm(psA[0:ncols, b, :], wpiece[:, dc, col0:col0 + ncols], hT[:, dc, sl], start=(dc == 0), stop=(dc == 7))

    final_tok = [None, None]
    final_done_flag = []

    def final_tile(t, osts, lo=0, hi=6):
        st = osts[t % 2]
        for half in range(2):
            b = psbank(lo, hi)
            for c4 in range(4):
                c = half * 4 + c4
                P.transpose(psA[:, b, c4 * 128:(c4 + 1) * 128], xT[:, c, t * 128:(t + 1) * 128], identf)
            P.copy('act' if half == 0 else 'dve', st[:, half * 512:(half + 1) * 512], psA[:, b, :])
        final_tok[t % 2] = P.dma('sp', out_d[t * 128:(t + 1) * 128, :], st, 'out%d' % (t % 2), reads_sb=[st])

    def load_params(li):
        P.dma('sp', colp[:], colp_d[li], 'params', writes_sb=[colp[:]])
        P.dma('sp', rowp[:], rowp_d[li].partition_broadcast(128), 'params', writes_sb=[rowp[:]])

    def layer(li, pre_normed=False):
        lam_init = 0.8 - 0.6 * math.exp(-0.3 * li)
        win = w_in_d[li]

        def win_piece(c0, n):
            return win[:, c0:c0 + n].rearrange("(c p) n -> p c n", p=128)

        if not pre_normed:
            load_params(li)
        pA = load_piece(0, win_piece(0, 512), 512)
        pC = load_piece(1, win_piece(1540, 256), 256)

        if not pre_normed:
            norm_full(CP_N1W)
        dump('h1_%d' % li, hT)
        chk('norm1')

        W0 = 16384

        bm = Bump(arena, W0, ARENA)
        wsTm = bm.alloc([128, 4, 128], BF16)
        NBA = 7
        gA = [bm.alloc([128, 512], F32) for _ in range(NBA)]
        gB = [bm.alloc([128, 512], F32) for _ in range(NBA)]
        vnb = [bm.alloc([128, 256], BF16) for _ in range(NBA)]
        yab = [bm.alloc([128, 256], BF16) for _ in range(NBA)]
        ssv = [bm.alloc([128, 4], F32) for _ in range(NBA)]
        rsv = [bm.alloc([128, 4], F32) for _ in range(NBA)]
        P.dma('pool', wsTm, wsT_d[li].rearrange("h s t -> s h t"), 'wsT', writes_sb=[wsTm])
        P.tt('dve', wsTm, wsTm, Ub[:].unsqueeze(1).to_broadcast([128, 4, 128]), ALU.mult)
        gmnw = rowp[:, RP_GMNW:RP_GMNW + 256]
        v4 = lambda ap: ap.rearrange("p (h d) -> p h d", h=4)
        pa_of = {}

        def a_st(st, t):
            tl = slice(t * 128, (t + 1) * 128)
            par = t % NBA
            a, bb = gA[par], gB[par]
            if st == 0:
                pa = psA[:, t % 4, :]
                pa_of[t] = pa
                for dc in range(8):
                    P.mm(pa, hT[:, dc, tl], pA[:, dc, :], start=(dc == 0), stop=(dc == 7))
                P.act(a, pa, AF.Square, scale=0.044715 ** 0.5)
            elif st == 1:
                P.stt(bb, a, 1.0, pa_of[t], ALU.add, ALU.mult)
                P.act(a, bb, AF.Sigmoid, scale=1.5957691216057308)
            elif st == 2:
                P.tt('dve', a, a, pa_of[t], ALU.mult)
                P.act(bb[:, 0:256], a[:, 256:512], AF.Square)
            elif st == 3:
                P.rsum(ssv[par], v4(bb[:, 0:256]))
                P.ts('dve', ssv[par], ssv[par], 1.0 / 64, EPS, op0=ALU.mult, op1=ALU.add)
                P.tt('pool', rsv[par], ssv[par], neghalf[:, 0:1].to_broadcast([128, 4]), ALU.pow)
            elif st == 4:
                P.tt('pool', v4(bb[:, 0:256]), v4(a[:, 256:512]), rsv[par].unsqueeze(2).to_broadcast([128, 4, 64]), ALU.mult)
                P.tt('pool', vnb[par], bb[:, 0:256], gmnw, ALU.mult)
            elif st == 5:
                b2 = psbank(4, 6)
                for h in range(4):
                    P.mm(psA[:, b2, h * 64:(h + 1) * 64], wsTm[:, h, :], vnb[par][:, h * 64:(h + 1) * 64], start=True, stop=True)
                for h in range(4):
                    P.stt(yab[par][:, h * 64:(h + 1) * 64], psA[:, b2, h * 64:(h + 1) * 64],
                          colp[:, CP_BS + h:CP_BS + h + 1], a[:, h * 64:(h + 1) * 64], ALU.add, ALU.mult)
            else:
                ts0 = (t % 2) * 8
                for j in range(2):
                    P.transpose(pst[:, ts0 + j, :], yab[par][:, j * 128:(j + 1) * 128], identb[:])
                P.copy('act', mixT[:, 0:2, tl], pst[:, ts0:ts0 + 2, :])

        NST = 7
        for i in range(NT + NST - 1):
            for st in range(NST):
                t = i - st
                if 0 <= t < NT:
                    a_st(st, t)

        chk('A')
        pB1 = load_piece(0, win_piece(512, 512), 512)

        bm = Bump(arena, W0, ARENA)
        wbd = [bm.alloc([128, 128], BF16) for _ in range(2)]
        pcxs = [bm.alloc([128, 16 + L], F32) for _ in range(2)]
        L1 = bm.alloc([128, 16 + L], F32)
        L2 = bm.alloc([128, 16 + L], F32)
        pT = bm.alloc([128, L], BF16)
        t16 = bm.alloc([128, 16], F32)
        for cc in range(2):
            P.memset('pool', wbd[cc], 0.0)
            for gg in range(2):
                dst = wbd[cc][gg * 64:(gg + 1) * 64, gg * 64:(gg + 1) * 64]
                P.dma('pool', dst, poolw_d[li][2 * cc + gg], 'wbd', writes_sb=[dst])
        P.memset('pool', pcxs[0][:, 0:16], 0.0)
        P.memset('pool', pcxs[1][:, 0:16], 0.0)
        P.memset('pool', L1[:, 0:16], 0.0)
        P.memset('pool', L2[:, 0:16], 0.0)
        for cc in range(2):
            for s in range(NS):
                b = psbank()
                proj_fm(pC, cc * 128, s, b)
                P.copy('act', pcxs[cc][:, 16 + s * 512:16 + (s + 1) * 512], psA[:, b, :])
        for cc in range(2):
            pcx = pcxs[cc]

            def shadd(dst, src, k, prt=slice(0, 128), eng='dve'):
                P.tt(eng, dst[prt, 16:16 + L], src[prt, 16:16 + L], src[prt, 16 - k:16 - k + L], ALU.add)

            shadd(L1, pcx, 1)
            hi = slice(64, 128)
            lo = slice(0, 64)
            if cc == 0:
                shadd(L2, L1, 2, hi)
            else:
                shadd(L2, L1, 2)
                shadd(L1, L2, 4)
                shadd(L2, L1, 8, hi)
            for prt, S in ((lo, L1), (hi, L2)):
                P.stt(pT[prt, 16:L], S[prt, 32:16 + L], cF[prt, CF_INVWIN + cc:CF_INVWIN + cc + 1],
                      pcx[prt, 32:16 + L], ALU.mult, ALU.subtract)
                P.tt('dve', t16[prt, :], S[prt, 16:32], cF[prt, CF_INVC16 + cc * 16:CF_INVC16 + (cc + 1) * 16], ALU.mult)
                P.tt('dve', pT[prt, 0:16], t16[prt, :], pcx[prt, 16:32], ALU.subtract)
            for s in range(NS):
                sl = slice(s * 512, (s + 1) * 512)
                b = psbank()
                P.mm(psA[:, b, :], wbd[cc], pT[:, sl], start=True, stop=True)
                P.act(mixT[:, 4 + cc, sl], psA[:, b, :], AF.Identity, scale=colp[:, CP_PSCALE + cc:CP_PSCALE + cc + 1])

        chk('C')
        pB2 = load_piece(1, win_piece(1024, 512), 512)

        bm = Bump(arena, W0, ARENA)
        zs = bm.alloc([128, 2, L], BF16)
        xsT = bm.alloc([128, 2, L], BF16)
        BT = bm.alloc([128, 2, L], BF16)
        CT = bm.alloc([128, 2, L], BF16)
        dtm = bm.alloc([128, 64], F32)
        atm = bm.alloc([128, 64], F32)
        acs = bm.alloc([128, 64], F32)
        dte = bm.alloc([128, 64], F32)
        b1_start = bm.off
        cin = [bm.alloc([128, 4 + 512], BF16) for _ in range(2)]
        bm_cin3 = bm.alloc([128, 4 + 512], BF16)
        accs = [bm.alloc([128, 512], F32) for _ in range(2)]
        dws = [bm.alloc([128, 4, 128], BF16) for _ in range(2)]
        wdt = bm.alloc([128, 8, 4], BF16)
        dtr = bm.alloc([128, 64], F32)
        t64a = bm.alloc([128, 64], F32)
        t64b = bm.alloc([128, 64], F32)
        Ab = bm.alloc([128, 4], F32)
        P.dma('pool', wdt, win_piece(1536, 4), 'wdt', writes_sb=[wdt])

        bD = psbank()
        for t in range(NT):
            for dc in range(8):
                P.mm(psA[:, bD, t * 4:(t + 1) * 4], hT[:, dc, t * 128:(t + 1) * 128], wdt[:, dc, :], start=(dc == 0), stop=(dc == 7))
        v3 = lambda ap: ap.rearrange("p (c h) -> p c h", h=4)
        P.tt('dve', v3(dtr), v3(psA[:, bD, 0:64]), rowp[:, RP_DTB:RP_DTB + 4].unsqueeze(1).to_broadcast([128, 16, 4]), ALU.add)
        P.act(t64a, dtr, AF.Abs)
        P.act(t64a, t64a, AF.Exp, scale=-1.0)
        P.act(t64a, t64a, AF.Ln, bias=1.0)
        P.ts('dve', t64b, dtr, 0.0, None, op0=ALU.max)
        P.tt('dve', dtm, t64a, t64b, ALU.add)
        P.act(Ab, rowp[:, RP_ALOG:RP_ALOG + 4], AF.Exp)
        P.ts('dve', Ab, Ab, -1.0, None, op0=ALU.mult)
        P.tt('dve', v3(atm), v3(dtm), Ab.unsqueeze(1).to_broadcast([128, 16, 4]), ALU.mult)
        bX = psbank()
        P.mm(psA[:, bX, 0:64], Uf, atm, start=True, stop=True)
        P.copy('dve', acs, psA[:, bX, 0:64])
        P.mm(psA[:, bX, 64:128], SLf, atm, start=True, stop=True)
        P.act(dte, psA[:, bX, 64:128], AF.Exp)

        for zc in range(2):
            for s in range(NS):
                b = psbank()
                proj_fm(pB1, zc * 128, s, b)
                P.act(zs[:, zc, s * 512:(s + 1) * 512], psA[:, b, :], AF.Silu)
        dests = [xsT[:, 0, :], xsT[:, 1, :], BT[:, 0, :], BT[:, 1, :], CT[:, 0, :], CT[:, 1, :]]
        cin = cin + [bm_cin3]
        conv_units = [(cb, s_) for cb in range(6) for s_ in range(NS)]

        def conv_proj(i):
            cb, s_ = conv_units[i]
            piece, col0 = (pB1, 256 + cb * 128) if cb < 2 else (pB2, (cb - 2) * 128)
            if s_ == 0:
                dw = dws[cb % 2]
                cw = CP_CONVW + cb * 4
                for kk in range(2):
                    P.ts('dve', dw[:, kk, :], identb[:], colp[:, cw + kk:cw + kk + 1], None, op0=ALU.mult)
            ci = cin[i % 3]
            prev = cin[(i - 1) % 3]
            b = psbank(0, 3)
            proj_fm(piece, col0, s_, b)
            P.copy('act', ci[:, 3:515], psA[:, b, :])
            if s_ == 0:
                P.memset('pool', ci[:, 0:3], 0.0)
            else:
                P.copy('pool', ci[:, 0:3], prev[:, 512:515])

        def conv_mm(i):
            cb, s_ = conv_units[i]
            dw = dws[cb % 2]
            ci = cin[i % 3]
            ac = accs[i % 2]
            cw = CP_CONVW + cb * 4
            b2 = psbank(3, 6)
            for kk in range(2):
                P.mm(psA[:, b2, :], dw[:, kk, :], ci[:, kk:kk + 512], start=(kk == 0), stop=(kk == 1))
            P.stt(ac, ci[:, 2:2 + 512], colp[:, cw + 2:cw + 3], psA[:, b2, :], ALU.mult, ALU.add)
            P.stt(ac, ci[:, 3:3 + 512], colp[:, cw + 3:cw + 4], ac, ALU.mult, ALU.add)
            P.act(dests[cb][:, s_ * 512:(s_ + 1) * 512], ac, AF.Silu, bias=colp[:, CP_CONVB + cb:CP_CONVB + cb + 1])

        conv_proj(0)
        for i in range(len(conv_units)):
            if i + 1 < len(conv_units):
                conv_proj(i + 1)
            conv_mm(i)
        chk('B1')
        pD1 = load_piece(0, win_piece(1796, 512), 512)
        pD2 = load_piece(1, win_piece(2308, 256), 256)

        bm2 = Bump(arena, b1_start, ARENA)
        xdts = [bm2.alloc([128, 256], BF16) for _ in range(2)]
        xdtw = bm2.alloc([128, 256], BF16)
        Btm = bm2.alloc([128, 256], BF16)
        t1 = bm2.alloc([128, 512], F32)
        MTs = [bm2.alloc([128, 512], BF16) for _ in range(2)]
        E = bm2.alloc([128, 512], F32)
        Cdecs = [bm2.alloc([128, 512], BF16) for _ in range(2)]
        cds = [bm2.alloc([128, 4], F32) for _ in range(2)]
        S = bm2.alloc([128, 256], F32)
        Sbf = bm2.alloc([128, 256], BF16)
        yg = bm2.alloc([128, 2, 512], F32)
        sqy = bm2.alloc([128, 512], BF16)
        msy = t1
        rsy = E
        P.memset('pool', S, 0.0)
        P.memset('pool', Sbf, 0.0)
        h4 = lambda ap: ap.rearrange("p (h d) -> p h d", h=4)
        g22 = lambda ap: ap.rearrange("p (g r l) -> p g r l", g=2, r=2)

        def b2_X(c):
            par = c % 2
            cl = slice(c * 128, (c + 1) * 128)
            xdt, MT, Cdec, cd = xdts[par], MTs[par], Cdecs[par], cds[par]
            tb = par * 8
            P.transpose(pst[:, tb + 0, :], xsT[:, 0, cl], identb[:])
            P.transpose(pst[:, tb + 1, :], xsT[:, 1, cl], identb[:])
            P.transpose(pst[:, tb + 2, :], BT[:, 0, cl], identb[:])
            P.transpose(pst[:, tb + 3, :], BT[:, 1, cl], identb[:])
            P.tt('dve', h4(xdt), pst[:, tb:tb + 2, :].rearrange("p a (h d) -> p (a h) d", h=2),
                 dtm[:, c * 4:(c + 1) * 4].unsqueeze(2).to_broadcast([128, 4, 64]), ALU.mult)
            P.tt('pool', h4(xdtw), h4(xdt), dte[:, c * 4:(c + 1) * 4].unsqueeze(2).to_broadcast([128, 4, 64]), ALU.mult)
            P.copy('act', Btm.rearrange("p (a n) -> p a n", a=2), pst[:, tb + 2:tb + 4, :])
            bC = psbank(0, 2)
            for g in range(2):
                P.mm(psA[:, bC, g * 128:(g + 1) * 128], BT[:, g, cl], CT[:, g, cl], start=True, stop=True)
            bR = psbank(2, 4)
            for h in range(4):
                P.mm(psA[:, bR, h * 128:(h + 1) * 128], atm[:, c * 4 + h:c * 4 + h + 1].to_broadcast([128, 128]), Uf,
                     start=True, stop=True)
            R4 = psA[:, bR, :].rearrange("p (h l) -> p h l", h=4)
            t14 = t1.rearrange("p (h l) -> p h l", h=4)
            for h in range(4):
                P.stt(t14[:, h, :], R4[:, h, :], acs[:, c * 4 + h:c * 4 + h + 1], NEGf, ALU.subtract, ALU.add)
            P.act(t1, t1, AF.Exp)
            P.tt('dve', g22(MT), g22(t1),
                 psA[:, bC, 0:256].rearrange("p (g l) -> p g l", g=2).unsqueeze(2).to_broadcast([128, 2, 2, 128]), ALU.mult)
            P.act(E, psA[:, bR, :], AF.Exp)
            P.copy('act', cd, E.rearrange("p (h l) -> p h l", h=4)[:, :, 127])
            P.tt('pool', g22(Cdec), g22(E), CT[:, :, cl].unsqueeze(2).to_broadcast([128, 2, 2, 128]), ALU.mult)
            bY = 4 + par
            for h in range(4):
                P.mm(psA[:, bY, 256 + h * 64:256 + (h + 1) * 64], Btm[:, (h // 2) * 128:(h // 2 + 1) * 128],
                     xdtw[:, h * 64:(h + 1) * 64], start=True, stop=True)

        def b2_Y(c):
            par = c % 2
            cl = slice(c * 128, (c + 1) * 128)
            xdt, MT, Cdec, cd = xdts[par], MTs[par], Cdecs[par], cds[par]
            bY = 4 + par
            for h in range(4):
                o = psA[(h % 2) * 64:(h % 2 + 1) * 64, bY, (h // 2) * 128:(h // 2 + 1) * 128]
                P.mm(o, xdt[:, h * 64:(h + 1) * 64], MT[:, h * 128:(h + 1) * 128], start=True, stop=False)
                P.mm(o, Sbf[:, h * 64:(h + 1) * 64], Cdec[:, h * 128:(h + 1) * 128], start=False, stop=True)
            for h in range(4):
                P.stt(S[:, h * 64:(h + 1) * 64], S[:, h * 64:(h + 1) * 64], cd[:, h:h + 1],
                      psA[:, bY, 256 + h * 64:256 + (h + 1) * 64], ALU.mult, ALU.add)
            P.copy('act', Sbf, S)
            cs = (c % 4) * 128
            for hc in range(2):
                P.stt(yg[:, hc, cs:cs + 128], xsT[:, hc, cl], colp[:, CP_DCOL + hc:CP_DCOL + hc + 1],
                      psA[:, bY, hc * 128:(hc + 1) * 128], ALU.mult, ALU.add)
            if c % 4 == 3:
                s = c // 4
                sl = slice(s * 512, (s + 1) * 512)
                for hc in range(2):
                    P.tt('pool', yg[:, hc, :], yg[:, hc, :], zs[:, hc, sl], ALU.mult)
                    P.act(sqy, yg[:, hc, :], AF.Square)
                    bN = psbank(0, 2)
                    P.mm(psA[:, bN, :], onesb[:], sqy, start=True, stop=True)
                    P.act(msy, psA[:, bN, :], AF.Ln, scale=1.0 / 128, bias=EPS)
                    P.act(rsy, msy, AF.Exp, scale=-0.5)
                    P.stt(mixT[:, 2 + hc, sl], yg[:, hc, :], colp[:, CP_SSMNW + hc:CP_SSMNW + hc + 1], rsy, ALU.mult, ALU.mult)

        b2_X(0)
        for c in range(NT):
            if c + 1 < NT:
                b2_X(c + 1)
            b2_Y(c)

        chk('B2')
        bm = Bump(arena, W0, ARENA)
        qT2 = bm.alloc([128, 2, 2, L], BF16)
        kT2 = bm.alloc([128, 2, 2, L], BF16)
        vflat = bm.alloc([128, NT, 324], BF16)
        vaug = vflat[:, :, 0:260].rearrange("p t (h e) -> p t h e", h=4)
        PT = [bm.alloc([128, 512], BF16) for _ in range(3)]
        oTs = [bm.alloc([128, 512], F32) for _ in range(2)]
        sqb = PT[0]
        msq = oTs[0]
        rsq = oTs[1]
        o0 = bm.alloc([128, 256], F32)
        o1 = bm.alloc([128, 256], F32)
        onbs = [bm.alloc([128, 256], BF16) for _ in range(2)]
        lt = bm.alloc([128, 32], F32)
        s1 = small[:, 0:1]
        s2 = small[:, 1:2]
        neglam = small[:, 2:3]
        qnws = small[:, 4:6]
        rc = small[:, 8:16]
        nr1 = small[:, 16:20]
        ss4 = small[:, 20:24]
        rs4 = small[:, 24:28]
        P.tt('dve', lt, rowp[:, RP_LQ1:RP_LQ1 + 32], rowp[:, RP_LK1:RP_LK1 + 32], ALU.mult)
        P.rsum(s1, lt)
        P.tt('dve', lt, rowp[:, RP_LQ2:RP_LQ2 + 32], rowp[:, RP_LK2:RP_LK2 + 32], ALU.mult)
        P.rsum(s2, lt)
        P.act(small[:, 0:2], small[:, 0:2], AF.Exp)
        P.tt('dve', neglam, s2, s1, ALU.subtract)
        P.ts('dve', neglam, neglam, -lam_init, None, op0=ALU.add)
        P.ts('dve', qnws, cF[:, CF_MASKC:CF_MASKC + 2], colp[:, CP_QNW:CP_QNW + 1], SM_SCALE, op0=ALU.mult, op1=ALU.mult)
        P.memset('pool', vflat[:, :, 260:324], 0.0)
        P.memset('pool', vaug[:, :, :, 64:65], 1.0)
        P.ts('dve', small[:, 6:8], cF[:, CF_MASKH:CF_MASKH + 2], colp[:, CP_KNW:CP_KNW + 1], None, op0=ALU.mult)
        for blk in range(4):
            for s in range(NS):
                sl = slice(s * 512, (s + 1) * 512)
                b = psbank(0, 3)
                proj_fm(pD1, blk * 128, s, b)
                P.act(sqb, psA[:, b, :], AF.Square)
                b2 = psbank(3, 6)
                P.mm(psA[:, b2, :], bd32b[:], sqb, start=True, stop=True)
                P.act(msq, psA[:, b2, :], AF.Ln, scale=1.0 / 32, bias=EPS)
                P.act(rsq, msq, AF.Exp, scale=-0.5)
                if blk < 2:
                    for c in range(2):
                        P.stt(qT2[:, blk, c, sl], psA[:, b, :], qnws[:, c:c + 1], rsq, ALU.mult, ALU.mult)
                else:
                    for hh in range(2):
                        P.stt(kT2[:, blk - 2, hh, sl], psA[:, b, :], small[:, 6 + hh:7 + hh], rsq, ALU.mult, ALU.mult)
        for t in range(NT):
            b = psbank()
            for dc in range(8):
                P.mm(psA[:, b, 0:256], hT[:, dc, t * 128:(t + 1) * 128], pD2[:, dc, :], start=(dc == 0), stop=(dc == 7))
            P.copy('act' if t % 2 else 'dve', vaug[:, t, :, 0:64], psA[:, b, 0:256].rearrange("p (h d) -> p h d", h=4))

        pO = [load_piece(0, w_out_d[li][:, 0:512].rearrange("(c p) n -> p c n", p=128), 512),
              load_piece(1, w_out_d[li][:, 512:1024].rearrange("(c p) n -> p c n", p=128), 512)]

        psSum = pst[:, 4:8, :].rearrange("p a b -> p (a b)").bitcast(F32)
        S3 = pst[:, 8:16, :].rearrange("p a b -> p (a b)").bitcast(F32)
        Sbanks = [psA[:, 0, :], psA[:, 1, :], S3]
        units = []
        for qs in range(8):
            for h in range(4):
                kbs = []
                for kb in range(2 * qs + 2):
                    if SLOPES[h] * (256 * qs - (128 * kb + 127)) > 130.0:
                        continue
                    kbs.append(kb)
                for n_, kb in enumerate(kbs):
                    units.append((qs, h, kb, n_ == 0, n_ == len(kbs) - 1))
        s_rot = [0]

        def emit_S(u):
            qs, h, kb, first, last = u
            kc, hh = h // 2, h % 2
            b = s_rot[0] % 3
            s_rot[0] += 1
            diag = kb >= 2 * qs
            P.mm(Sbanks[b], kT2[:, kc, hh, kb * 128:(kb + 1) * 128],
                 qT2[:, kc, :, qs * 256:(qs + 1) * 256].rearrange("p c q -> p (c q)") if False else qT2[:, kc, :, qs * 256:(qs + 1) * 256],
                 start=True, stop=not diag)
            if diag:
                P.mm(Sbanks[b], identb[:], (M0 if kb == 2 * qs else M1)[:], start=False, stop=True)
            return b

        def emit_exp_pv(u, b):
            qs, h, kb, first, last = u
            e = 2 * qs - kb + 1
            P.act(PT[b], Sbanks[b], AF.Exp, bias=cF[:, CF_ALIBI + h * 17 + e:CF_ALIBI + h * 17 + e + 1])
            ob = 2 + (qs * 4 + h) % 2
            P.mm(psA[:, ob, :], vflat[:, kb, h * 65:h * 65 + 128], PT[b], start=first, stop=last)
            if last:
                ot = oTs[(qs * 4 + h) % 2]
                for d in [d for d in deferred if d[1] == 'T' and (d[2][0] * 4 + d[2][1]) % 2 == (qs * 4 + h) % 2]:
                    deferred.remove(d)
                    fire(d)
                P.copy('dve' if h % 2 else 'act', ot[0:65, :], psA[0:65, ob, :])

        def emit_oT_transposes(qs, h, c, j):
            ot = oTs[(qs * 4 + h) % 2]
            cols = slice(c * 256 + j * 128, c * 256 + (j + 1) * 128)
            P.transpose(psA[:, 4 + j, (c * 4 + h) * 64:(c * 4 + h + 1) * 64], ot[0:64, cols], identf[0:64, 0:64])
            P.transpose(psSum[:, j * 8 + c * 4 + h:j * 8 + c * 4 + h + 1], ot[64:65, cols], identf[64:65, 64:65])

        def epilogue_a(qb):
            j = qb % 2
            onb = onbs[j]
            P.recip(rc, psSum[:, j * 8:(j + 1) * 8])
            P.ts('dve', nr1, rc[:, 4:8], neglam, None, op0=ALU.mult)
            P.tt('dve', h4(o0), h4(psA[:, 4 + j, 0:256]), rc[:, 0:4].unsqueeze(2).to_broadcast([128, 4, 64]), ALU.mult)
            P.tt('dve', h4(o1), h4(psA[:, 4 + j, 256:512]), nr1.unsqueeze(2).to_broadcast([128, 4, 64]), ALU.mult)
            P.tt('dve', o0, o0, o1, ALU.add)
            P.tt('dve', o1, o0, o0, ALU.mult)
            P.rsum(ss4, h4(o1))
            P.ts('dve', ss4, ss4, 1.0 / 64, EPS, op0=ALU.mult, op1=ALU.add)
            P.tt('pool', rs4, ss4, neghalf[:, 0:1].to_broadcast([128, 4]), ALU.pow)
            P.tt('dve', h4(o1), h4(o0), rs4.unsqueeze(2).to_broadcast([128, 4, 64]), ALU.mult)
            P.stt(onb, o1, 1.0 - lam_init, rowp[:, RP_SUBLN:RP_SUBLN + 256], ALU.mult, ALU.mult)

        def epilogue_b(qb):
            ql = slice(qb * 128, (qb + 1) * 128)
            j = qb % 2
            ts0 = j * 2
            for jj in range(2):
                P.transpose(pst[:, ts0 + jj, :], onbs[j][:, jj * 128:(jj + 1) * 128], identb[:])
            P.copy('act', mixT[:, 6:8, ql], pst[:, ts0:ts0 + 2, :])

        pend = []
        deferred = []

        def fire(item):
            _, kind, args = item
            if kind == 'T':
                qs_, h_, c_, j_ = args
                emit_oT_transposes(qs_, h_, c_, j_)
                if h_ == 3 and c_ == 1 and j_ == 1:
                    epilogue_a(2 * qs_)
                    epilogue_a(2 * qs_ + 1)
                    deferred.append([6, 'E', 2 * qs_])
                    deferred.append([6, 'E', 2 * qs_ + 1])
            else:
                epilogue_b(args)

        def retire():
            pu, pb = pend.pop(0)
            emit_exp_pv(pu, pb)
            for d in deferred:
                d[0] -= 1
            ready = [d for d in deferred if d[0] <= 0]
            for d in ready:
                deferred.remove(d)
                fire(d)
            if pu[4]:
                n_ = 0
                for c_ in range(2):
                    for j_ in range(2):
                        deferred.append([2 + n_, 'T', (pu[0], pu[1], c_, j_)])
                        n_ += 1

        for u in units:
            pend.append((u, emit_S(u)))
            if len(pend) > 2:
                retire()
        while pend:
            retire()
        while deferred:
            fire(deferred.pop(0))

        if stop != 'D':
            dump('mix%d' % li, mixT)
        chk('D')

        nb = norm_bufs()
        for s in range(NS):
            sl = slice(s * 512, (s + 1) * 512)
            for c in range(8):
                half, cj = c // 4, c % 4
                b = psbank()
                for mc in range(8):
                    P.mm(psA[:, b, :], pO[half][:, mc, cj * 128:(cj + 1) * 128], mixT[:, mc, sl], start=(mc == 0), stop=(mc == 7))
                P.tt('dve', xT[:, c, sl], xT[:, c, sl], psA[:, b, :], ALU.add)
            if s >= 1:
                norm_slab(s - 1, CP_N2W, nb)
        norm_slab(NS - 1, CP_N2W, nb)

        dump('x1_%d' % li, xT)
        chk('wout')

        groups = []
        j0 = 0
        while j0 < NJ:
            n = min(4, NJ - j0)
            groups.append((j0, n))
            j0 += n

        def load_group(gi):
            j0, n = groups[gi]
            base = 3 * (gi % 2)
            wg = load_piece(base + 0, wg_d[li][:, j0 * 128:(j0 + n) * 128].rearrange("(c p) n -> p c n", p=128), n * 128)
            wu = load_piece(base + 1, wu_d[li][:, j0 * 128:(j0 + n) * 128].rearrange("(c p) n -> p c n", p=128), n * 128)
            wd = load_piece(base + 2, wd_d[li][j0 * 128:(j0 + n) * 128, :].rearrange("(j p) n -> p j n", p=128), n, kind="down")
            return wg, wu, wd

        loaded = {0: load_group(0)}
        if len(groups) > 1:
            loaded[1] = load_group(1)
        dump('h2_%d' % li, hT)
        chk('norm2')
        bmf = Bump(arena, 48 * 1024, ARENA)
        sg = [bmf.alloc([128, 512], BF16) for _ in range(2)]
        actT = [mixT[:, 0:4, 0:512], mixT[:, 4:8, 0:512]]
        fuse_next = (li + 1 < nlayers) and stop is None
        fuse_final = (li + 1 == nlayers) and stop is None and not dbg_d
        if fuse_final:
            bmo = Bump(arena, 0, 24 * 1024)
            fin_osts = [bmo.alloc([128, D], F32) for _ in range(2)]
            assert (len(groups) - 1) % 2 == 1
        steps = [(gi, s_) for gi in range(len(groups)) for s_ in range(NS)]

        def ffn_gu(k):
            gi, s_ = steps[k]
            n = groups[gi][1]
            wg, wu, wd = loaded[gi]
            sl = slice(s_ * 512, (s_ + 1) * 512)
            at = actT[k % 2]
            for jb in range(n):
                bg = psbank(0, 2)
                bu = psbank(2, 4)
                for dc in range(8):
                    P.mm(psA[:, bg, :], wg[:, dc, jb * 128:(jb + 1) * 128], hT[:, dc, sl], start=(dc == 0), stop=(dc == 7))
                for dc in range(8):
                    P.mm(psA[:, bu, :], wu[:, dc, jb * 128:(jb + 1) * 128], hT[:, dc, sl], start=(dc == 0), stop=(dc == 7))
                sgt = sg[(k * 4 + jb) % 2]
                P.act(sgt, psA[:, bg, :], AF.Silu)
                P.tt('dve', at[:, jb, :], sgt, psA[:, bu, :], ALU.mult)

        def ffn_dn(k):
            gi, s_ = steps[k]
            n = groups[gi][1]
            wg, wu, wd = loaded[gi]
            sl = slice(s_ * 512, (s_ + 1) * 512)
            at = actT[k % 2]
            for c in range(8):
                bd = psbank(4, 6)
                for jb in range(n):
                    P.mm(psA[:, bd, :], wd[:, jb, c * 128:(c + 1) * 128], at[:, jb, :], start=(jb == 0), stop=(jb == n - 1))
                P.tt('dve', xT[:, c, sl], xT[:, c, sl], psA[:, bd, :], ALU.add)

        ffn_gu(0)
        for k, (gi, s_) in enumerate(steps):
            lastg = gi == len(groups) - 1
            if lastg and s_ == 0 and fuse_next:
                load_params(li + 1)
            if k + 1 < len(steps):
                ffn_gu(k + 1)
            ffn_dn(k)
            if lastg and fuse_next and s_ >= 1:
                norm_slab(s_ - 1, CP_N1W, nb)
            if lastg and fuse_final and s_ >= 1:
                for t in range(4 * (s_ - 1), 4 * s_):
                    final_tile(t, fin_osts, 4, 6)
            if s_ == NS - 1:
                if lastg and fuse_next:
                    norm_slab(NS - 1, CP_N1W, nb)
                if lastg and fuse_final:
                    for t in range(4 * (NS - 1), 4 * NS):
                        final_tile(t, fin_osts, 4, 6)
                    final_done_flag.append(True)
                if gi + 2 < len(groups):
                    loaded[gi + 2] = load_group(gi + 2)

        dump('x2_%d' % li, xT)
        P.mark('ffn')

    if stop is not None:
        for c in range(8):
            P.memset('pool', mixT[:, c, :], 0.0)
    fuse0 = stop is None
    if fuse0:
        load_params(0)
    load_x(fuse0)
    P.mark('loadx')
    try:
        if stop != 'loadx':
            for li in range(nlayers):
                layer(li, pre_normed=(stop is None))
    except _Stop:
        pass

    final_done = bool(final_done_flag)
    if not final_done:
        bmo = Bump(arena, 0, ARENA)
        osts = [bmo.alloc([128, D], F32) for _ in range(2)]
        for t in range(NT):
            final_tile(t, osts)
    toks = list(final_tok) + [(s, v[1]) for s, v in P.dsem.items() if s == 'd_dbg']
    P.wait_all('sp', toks)

    with nc.Block() as block:
        P.emit(block)
    es.close()
    return nc, P


def host_pack(inputs):
    f = lambda k: np.asarray(inputs[k], dtype=np.float32)
    colp = np.zeros((2, 128, NCOLP), np.float32)
    rowp = np.zeros((2, 1, NROWP), np.float32)
    p = np.arange(128)
    for l in range(2):
        colp[l, :, CP_N1W:CP_N1W + 8] = f('norm1_w')[l].reshape(8, 128).T
        colp[l, :, CP_N2W:CP_N2W + 8] = f('norm2_w')[l].reshape(8, 128).T
        colp[l, :, CP_CONVW:CP_CONVW + 24] = f('ssm_conv_w')[l].reshape(6, 128, 4).transpose(1, 0, 2).reshape(128, 24)
        colp[l, :, CP_CONVB:CP_CONVB + 6] = f('ssm_conv_b')[l].reshape(6, 128).T
        for hc in range(2):
            colp[l, :, CP_DCOL + hc] = f('ssm_d')[l][2 * hc + p // 64]
        colp[l, :, CP_SSMNW:CP_SSMNW + 2] = f('ssm_norm_w')[l].reshape(2, 128).T
        colp[l, :, CP_PSCALE:CP_PSCALE + 2] = f('pool_scale')[l].reshape(2, 128).T
        colp[l, :, CP_QNW] = f('da_q_norm_w')[l][p % 32]
        colp[l, :, CP_KNW] = f('da_k_norm_w')[l][p % 32]
        colp[l, :, CP_BS:CP_BS + 4] = f('gm_bs')[l].T
        rowp[l, 0, RP_GMNW:RP_GMNW + 256] = f('gm_norm_w')[l].reshape(256)
        rowp[l, 0, RP_DTB:RP_DTB + 4] = f('ssm_dt_bias')[l]
        rowp[l, 0, RP_ALOG:RP_ALOG + 4] = f('ssm_a_log')[l]
        rowp[l, 0, RP_LQ1:RP_LQ1 + 32] = f('da_lambda_q1')[l]
        rowp[l, 0, RP_LK1:RP_LK1 + 32] = f('da_lambda_k1')[l]
        rowp[l, 0, RP_LQ2:RP_LQ2 + 32] = f('da_lambda_q2')[l]
        rowp[l, 0, RP_LK2:RP_LK2 + 32] = f('da_lambda_k2')[l]
        rowp[l, 0, RP_SUBLN:RP_SUBLN + 256] = np.tile(f('da_subln_w')[l], 4)
    shared = {
        "w_in": np.ascontiguousarray(f('w_in')),
        "w_out": np.ascontiguousarray(f('w_out')),
        "ffn_w_gate": np.ascontiguousarray(f('ffn_w_gate')),
        "ffn_w_up": np.ascontiguousarray(f('ffn_w_up')),
        "ffn_w_down": np.ascontiguousarray(f('ffn_w_down')),
        "gm_wsT": np.ascontiguousarray(f('gm_ws').transpose(0, 1, 3, 2)),
        "pool_w": np.ascontiguousarray(f('pool_w')),
        "colp": colp,
        "rowp": rowp,
        "cf": make_consts(),
    }
    return shared


_CACHE = {}


def kernel(**inputs):
    if 'nc' not in _CACHE:
        _CACHE['nc'] = build(2)[0]
    nc = _CACHE['nc']
    shared = host_pack(inputs)
    x = np.asarray(inputs['x'], dtype=np.float32)
    in_maps = []
    for b in range(8):
        m = dict(shared)
        m["x"] = np.ascontiguousarray(x[b])
        in_maps.append(m)
    res = run_bass_kernel_spmd(nc, in_maps, core_ids=list(range(8)))
    out = np.stack([np.asarray(res.results[b]["out"], dtype=np.float32) for b in range(8)], axis=0)
    return out
```

```python
import math
from contextlib import ExitStack

import numpy as np
import concourse.bass as bass
import concourse.mybir as mybir
from concourse.bass_utils import run_bass_kernel_spmd

F32 = mybir.dt.float32
BF16 = mybir.dt.bfloat16
U8 = mybir.dt.uint8
AF = mybir.ActivationFunctionType
ALU = mybir.AluOpType
AX = mybir.AxisListType


def _esize(dt):
    if dt == F32:
        return 4
    if dt == BF16:
        return 2
    if dt == U8:
        return 1
    s = str(dt)
    if '32' in s:
        return 4
    if '16' in s:
        return 2
    return 1


def ap_region(ap):
    pat = ap.ap
    pstep, pcnt = pat[0]
    es = _esize(ap.dtype)
    off = ap.offset
    if pstep == 0:
        row = 1
        for d in list(ap.tensor.shape)[1:]:
            row *= int(d)
        pstep = row
    p0 = off // pstep
    f0 = off % pstep
    ext = 0
    for st, cnt in pat[1:]:
        ext += abs(st) * (cnt - 1)
    lo, hi = f0 * es, (f0 + ext + 1) * es
    nm = ap.tensor.name
    if nm in PSUM_NAMES:
        return (nm, 0, 128, (lo // 2048) * 2048, ((hi + 2047) // 2048) * 2048)
    return (nm, p0, p0 + pcnt, lo, hi)


PSUM_NAMES = ('psA', 'pst')


class Prog:
    ENGS = ('pe', 'act', 'dve', 'pool', 'sp')

    def __init__(self, nc, sem_alloc):
        self.nc = nc
        self.sem_alloc = sem_alloc
        self.cnt = {e: 0 for e in self.ENGS}
        self.plan = {e: [] for e in self.ENGS}
        self.waited = {e: {} for e in self.ENGS}
        self.dsem = {}
        self.acc = {}
        self.semh = {}
        for e in ('pe', 'act', 'dve', 'pool'):
            self.semh['c_' + e] = sem_alloc('c_' + e)
        self.n_wait = 0
        self.n_ops = 0
        self.marks = []
        self.K = {}
        self.vc = {}

    def mark(self, name):
        self.marks.append((name, dict(self.cnt)))

    def _deps(self, eng, reads, writes):
        need = {}
        own = 'c_' + eng
        for is_w, aps in ((False, reads), (True, writes)):
            for ap in aps:
                nm, plo, phi, lo, hi = ap_region(ap)
                psum = nm in PSUM_NAMES
                for r in self.acc.get(nm, ()):
                    if r[0] < phi and plo < r[1] and r[2] < hi and lo < r[3]:
                        s, v = r[5]
                        if not (is_w or r[4] or (psum and s != own)):
                            continue
                        if need.get(s, 0) < v:
                            need[s] = v
        return need

    def _record(self, reads, writes, tok):
        for ap in writes:
            nm, plo, phi, lo, hi = ap_region(ap)
            lst = self.acc.setdefault(nm, [])
            lst[:] = [r for r in lst if not (plo <= r[0] and r[1] <= phi and lo <= r[2] and r[3] <= hi)]
            lst.append((plo, phi, lo, hi, True, tok))
        for ap in reads:
            nm, plo, phi, lo, hi = ap_region(ap)
            lst = self.acc.setdefault(nm, [])
            lst[:] = [r for r in lst if not ((not r[4]) and r[5][0] == tok[0]
                                             and plo <= r[0] and r[1] <= phi and lo <= r[2] and r[3] <= hi)]
            lst.append((plo, phi, lo, hi, False, tok))

    def _resolve(self, eng, need):
        waits = []
        own = 'c_' + eng
        K = self.K.setdefault(eng, {})
        for s, v in sorted(need.items(), key=lambda kv: -kv[1]):
            if s.startswith('d_'):
                v = max(v, self.dsem[s][1])
            if s == own and eng == 'pe':
                continue
            if K.get(s, 0) >= v:
                continue
            waits.append((s, v))
            K[s] = v
            snap = self.vc.get((s, v))
            if snap is None and s.startswith('d_'):
                snap = self.vc.get((s, self.dsem[s][1]))
            if snap:
                for s2, v2 in snap.items():
                    if K.get(s2, 0) < v2:
                        K[s2] = v2
        return waits

    def op(self, eng, thunk, reads=(), writes=()):
        reads = [r for r in reads if r is not None and not isinstance(r, (int, float))]
        writes = list(writes)
        waits = self._resolve(eng, self._deps(eng, reads, writes))
        self.cnt[eng] += 1
        tok = ('c_' + eng, self.cnt[eng])
        self.vc[tok] = dict(self.K.get(eng, {}))
        self.plan[eng].append((waits, thunk, tok))
        self._record(reads, writes, tok)
        self.n_wait += len(waits)
        self.n_ops += 1
        return tok

    def dma(self, queue, out, in_, key, reads_sb=(), writes_sb=(), **kw):
        s = 'd_' + key
        if s not in self.dsem:
            h = self.sem_alloc(s)
            self.dsem[s] = [h, 0]
            self.semh[s] = h
        waits = self._resolve(queue, self._deps(queue, list(reads_sb), list(writes_sb)))
        self.dsem[s][1] += 16
        tok = (s, self.dsem[s][1])
        self.vc[tok] = dict(self.K.get(queue, {}))
        self.plan[queue].append((waits, (lambda e, o=out, i=in_, k=kw: e.dma_start(out=o, in_=i, **k)), tok))
        self._record(list(reads_sb), list(writes_sb), tok)
        self.n_ops += 1
        return tok

    def wait_all(self, eng, toks):
        need = {}
        for s, v in toks:
            need[s] = max(need.get(s, 0), v)
        self.plan[eng].append((self._resolve(eng, need), None, None))

    def emit(self, block):
        semh = self.semh
        plan = self.plan

        def run(engname, e):
            for waits, thunk, tok in plan[engname]:
                if thunk is None:
                    standalone = list(waits)
                else:
                    standalone = list(waits[:-1])
                for k in range(0, len(standalone), 2):
                    w_ins = e.wait_ge(semh[standalone[k][0]], standalone[k][1])
                    if k + 1 < len(standalone):
                        w_ins._wait_ge(semh[standalone[k + 1][0]], standalone[k + 1][1])
                if thunk is None:
                    continue
                ins = thunk(e)
                if waits:
                    s, v = waits[-1]
                    ins._wait_ge(semh[s], v)
                if tok is None:
                    pass
                elif tok[0].startswith('d_'):
                    ins.then_inc(semh[tok[0]], 16)
                else:
                    ins.then_inc(semh[tok[0]], 1)

        @block.tensor
        def _(e):
            run('pe', e)

        @block.scalar
        def _(e):
            run('act', e)

        @block.vector
        def _(e):
            run('dve', e)

        @block.gpsimd
        def _(e):
            run('pool', e)

        @block.sync
        def _(e):
            run('sp', e)

    def mm(self, out, lhsT, rhs, start=True, stop=True, **kw):
        return self.op('pe', lambda e: e.matmul(out, lhsT, rhs, start=start, stop=stop, **kw),
                       reads=[lhsT, rhs], writes=[out])

    def transpose(self, out, in_, ident):
        return self.op('pe', lambda e: e.transpose(out, in_, ident), reads=[in_, ident], writes=[out])

    def act(self, out, in_, func, bias=None, scale=None):
        kw = {}
        rd = [in_]
        if bias is not None:
            kw['bias'] = bias
            rd.append(bias)
        if scale is not None:
            kw['scale'] = scale
            rd.append(scale)
        return self.op('act', lambda e: e.activation(out, in_, func, **kw), reads=rd, writes=[out])

    def tt(self, eng, out, in0, in1, op):
        return self.op(eng, lambda e: e.tensor_tensor(out, in0, in1, op), reads=[in0, in1], writes=[out])

    def ts(self, eng, out, in0, s1, s2=None, op0=ALU.mult, op1=None):
        kw = {}
        if op1 is not None:
            kw['op1'] = op1
        return self.op(eng, lambda e: e.tensor_scalar(out, in0, s1, s2, op0, **kw),
                       reads=[in0, s1, s2], writes=[out])

    def stt(self, out, in0, scalar, in1, op0, op1):
        return self.op('dve', lambda e: e.scalar_tensor_tensor(out, in0, scalar, in1, op0, op1),
                       reads=[in0, scalar, in1], writes=[out])

    def copy(self, eng, out, in_):
        if eng == 'act':
            return self.op(eng, lambda e: e.copy(out, in_), reads=[in_], writes=[out])
        return self.op(eng, lambda e: e.tensor_copy(out, in_), reads=[in_], writes=[out])

    def memset(self, eng, out, val):
        return self.op(eng, lambda e: e.memset(out, val), reads=[], writes=[out])

    def rsum(self, out, in_):
        return self.op('dve', lambda e: e.reduce_sum(out, in_, AX.X), reads=[in_], writes=[out])

    def recip(self, out, in_):
        return self.op('dve', lambda e: e.reciprocal(out, in_), reads=[in_], writes=[out])


D = 1024
L = 2048
NT = 16
NS = 4
DIN = 2564
DFF = 2816
NJ = DFF // 128
EPS = 1e-6
SM_SCALE = 32 ** -0.5
SLOPES = [2.0 ** (-8.0 * (h + 1) / 4) for h in range(4)]
POOL_WINDOWS = (2, 4, 8, 16)
NEGV = -30000.0

CP_N1W, CP_N2W, CP_CONVW, CP_CONVB, CP_DCOL, CP_SSMNW, CP_PSCALE, CP_QNW, CP_KNW, CP_BS = 0, 8, 16, 40, 46, 48, 50, 52, 53, 54
NCOLP = 58
RP_GMNW, RP_DTB, RP_ALOG, RP_LQ1, RP_LK1, RP_LQ2, RP_LK2, RP_SUBLN = 0, 256, 260, 264, 296, 328, 360, 392
NROWP = 648
CF_U, CF_SL, CF_NEG, CF_ALIBI, CF_INVWIN, CF_INVC16, CF_IDENT, CF_BD32, CF_MASKC, CF_MASKH = 0, 128, 256, 384, 452, 454, 486, 614, 742, 744
NCF = 746

ARENA = 70 * 1024


def make_consts():
    cf = np.zeros((128, NCF), np.float32)
    j = np.arange(128)[:, None]
    l = np.arange(128)[None, :]
    cf[:, CF_U:CF_U + 128] = (j <= l)
    cf[:, CF_SL:CF_SL + 128] = (j > l)
    cf[:, CF_NEG:CF_NEG + 128] = np.where(j <= l, 0.0, NEGV)
    for h in range(4):
        for e in range(17):
            cf[:, CF_ALIBI + h * 17 + e] = SLOPES[h] * (np.arange(128) - 128.0 * e)
    cf[:, CF_MASKC] = (np.arange(128) % 64 < 32)
    cf[:, CF_MASKC + 1] = (np.arange(128) % 64 >= 32)
    cf[:, CF_MASKH] = (np.arange(128) < 64)
    cf[:, CF_MASKH + 1] = (np.arange(128) >= 64)
    for c in range(2):
        for p in range(128):
            win = POOL_WINDOWS[2 * c + p // 64]
            cf[p, CF_INVWIN + c] = 1.0 / win
            for t in range(16):
                cf[p, CF_INVC16 + c * 16 + t] = 1.0 / min(t + 1, win)
    cf[:, CF_IDENT:CF_IDENT + 128] = np.eye(128)
    cf[:, CF_BD32:CF_BD32 + 128] = (j // 32 == l // 32)
    return cf


class Bump:
    def __init__(self, arena, base, limit):
        self.arena = arena
        self.off = base
        self.limit = limit

    def alloc(self, shape, dt):
        n = 1
        for s in shape[1:]:
            n *= s
        nb = n * _esize(dt)
        off = (self.off + 31) // 32 * 32
        assert off + nb <= self.limit, (off, nb, self.limit)
        self.off = off + nb
        v = self.arena[:, off:off + nb].bitcast(dt)
        if len(shape) == 3:
            v = v.rearrange("p (a b) -> p a b", a=shape[1])
        elif len(shape) == 4:
            v = v.rearrange("p (a b c) -> p a b c", a=shape[1], b=shape[2])
        return v


class _Stop(Exception):
    pass


def build(nlayers=2, dbg=(), stop=None):
    nc = bass.Bass("TRN2", target_bir_lowering=False)

    def din(name, shape):
        return nc.dram_tensor(name, list(shape), F32, kind="ExternalInput").ap()

    x_d = din("x", [L, D])
    w_in_d = din("w_in", [2, D, DIN])
    w_out_d = din("w_out", [2, D, D])
    wg_d = din("ffn_w_gate", [2, D, DFF])
    wu_d = din("ffn_w_up", [2, D, DFF])
    wd_d = din("ffn_w_down", [2, DFF, D])
    wsT_d = din("gm_wsT", [2, 4, 128, 128])
    poolw_d = din("pool_w", [2, 4, 64, 64])
    colp_d = din("colp", [2, 128, NCOLP])
    rowp_d = din("rowp", [2, 1, NROWP])
    cf_d = din("cf", [128, NCF])
    out_d = nc.dram_tensor("out", [L, D], F32, kind="ExternalOutput").ap()
    dbg_d = {}
    for name in dbg:
        dbg_d[name] = nc.dram_tensor("dbg_" + name, [128, 8, L], F32, kind="ExternalOutput").ap()

    es = ExitStack()

    def sb(name, shape, dt):
        return es.enter_context(nc.sbuf_tensor("s_" + name, list(shape), dt))

    def sem(name):
        return es.enter_context(nc.semaphore(name))

    P = Prog(nc, sem)

    xT = sb("xT", [128, 8, L], F32)
    hT = sb("hT", [128, 8, L], BF16)
    mixT = sb("mixT", [128, 8, L], BF16)
    arena = sb("arena", [128, ARENA], U8)
    cF = sb("cF", [128, NCF], F32)
    colp = sb("colp", [128, NCOLP], F32)
    rowp = sb("rowp", [128, NROWP], F32)
    identb = sb("identb", [128, 128], BF16)
    Ub = sb("Ub", [128, 128], BF16)
    NEGb = sb("NEGb", [128, 128], BF16)
    bd32b = sb("bd32b", [128, 128], BF16)
    onesb = sb("onesb", [128, 128], BF16)
    neghalf = sb("neghalf", [128, 1], F32)
    small = sb("small", [128, 64], F32)
    psA = es.enter_context(nc.psum_tensor("psA", [128, 6, 512], F32))
    pst = es.enter_context(nc.psum_tensor("pst", [128, 16, 128], BF16))

    identf = cF[:, CF_IDENT:CF_IDENT + 128]
    Uf = cF[:, CF_U:CF_U + 128]
    SLf = cF[:, CF_SL:CF_SL + 128]
    NEGf = cF[:, CF_NEG:CF_NEG + 128]

    RING_SLOT = 8192

    def dump(name, src):
        if name in dbg_d:
            for c in range(8):
                P.dma('pool', dbg_d[name][:, c, :], src[:, c, :], 'dbg', reads_sb=[src[:, c, :]])

    def chk(phase):
        P.mark(phase)
        if stop == phase:
            dump('mix0', mixT)
            raise _Stop()

    def ring_slot(i, kind="in"):
        v = arena[:, i * RING_SLOT:(i + 1) * RING_SLOT].bitcast(BF16)
        if kind == "in":
            return v.rearrange("p (c n) -> p c n", c=8)
        return v.rearrange("p (j n) -> p j n", j=4)

    def load_piece(slot, src, ncols, kind="in"):
        dst = ring_slot(slot, kind)
        if kind == "in":
            dst = dst[:, :, 0:ncols]
        else:
            dst = dst[:, 0:ncols, :]
        P.dma('pool', dst, src, 'ring%d' % slot, writes_sb=[dst])
        return dst

    P.dma('sp', cF[:], cf_d, 'cf', writes_sb=[cF[:]])
    P.copy('dve', identb[:], identf)
    P.copy('dve', Ub[:], Uf)
    P.copy('dve', NEGb[:], NEGf)
    P.copy('dve', bd32b[:], cF[:, CF_BD32:CF_BD32 + 128])
    P.memset('dve', onesb[:], 1.0)
    P.memset('dve', neghalf[:], -0.5)

    M0 = sb("M0", [128, 512], BF16)
    M1 = sb("M1", [128, 512], BF16)
    m04 = M0[:].rearrange("p (c j q) -> p c j q", c=2, j=2)
    m14 = M1[:].rearrange("p (c j q) -> p c j q", c=2, j=2)
    P.memset('pool', M0[:], 0.0)
    P.memset('pool', M1[:], NEGV)
    for c in range(2):
        P.copy('pool', m04[:, c, 0, :], NEGb[:])
        P.copy('pool', m14[:, c, 1, :], NEGb[:])

    psrot = {}

    def psbank(lo=0, hi=6):
        k = (lo, hi)
        v = psrot.get(k, 0)
        psrot[k] = v + 1
        return lo + v % (hi - lo)

    def load_x(fuse_norm):
        bm = Bump(arena, 16384, ARENA)
        stage = [bm.alloc([128, D], F32) for _ in range(4)]
        nbx = norm_bufs() if fuse_norm else None
        for t in range(NT):
            st = stage[t % 4]
            P.dma('sp', st, x_d[t * 128:(t + 1) * 128, :], 'xin%d' % (t % 4), writes_sb=[st])
            for half in range(2):
                b = psbank()
                for c4 in range(4):
                    c = half * 4 + c4
                    P.transpose(psA[:, b, c4 * 128:(c4 + 1) * 128], st[:, c * 128:(c + 1) * 128], identf)
                src = psA[:, b, :].rearrange("p (c t) -> p c t", c=4)
                dst = xT[:, half * 4:half * 4 + 4, t * 128:(t + 1) * 128]
                P.copy('act' if (t + half) % 2 == 0 else 'dve', dst, src)
            if fuse_norm and t % 4 == 3 and t >= 7:
                norm_slab(t // 4 - 1, CP_N1W, nbx)
        if fuse_norm:
            norm_slab(NS - 1, CP_N1W, nbx)

    def norm_bufs():
        bm = Bump(arena, 50 * 1024, ARENA)
        sq = [bm.alloc([128, 512], BF16) for _ in range(2)]
        ms = bm.alloc([128, 512], F32)
        rs = bm.alloc([128, 512], F32)
        return sq, ms, rs

    def norm_slab(s, nw_off, bufs):
        sq, ms, rs = bufs
        sl = slice(s * 512, (s + 1) * 512)
        b = psbank()
        for c in range(8):
            P.act(sq[c % 2], xT[:, c, sl], AF.Square)
            P.mm(psA[:, b, :], onesb[:], sq[c % 2], start=(c == 0), stop=(c == 7))
        P.act(ms, psA[:, b, :], AF.Ln, scale=1.0 / D, bias=EPS)
        P.act(rs, ms, AF.Exp, scale=-0.5)
        for c in range(8):
            P.stt(hT[:, c, sl], xT[:, c, sl], colp[:, nw_off + c:nw_off + c + 1], rs, ALU.mult, ALU.mult)

    def norm_full(nw_off):
        bufs = norm_bufs()
        for s in range(NS):
            norm_slab(s, nw_off, bufs)

    def proj_fm(wpiece, col0, s, b, ncols=128):
        sl = slice(s * 512, (s + 1) * 512)
        for dc in range(8):
            P.mm(psA[0:ncols, b, :], wpiece[:, dc, col0:col0 + ncols], hT[:, dc, sl], start=(dc == 0), stop=(dc == 7))

    final_tok = [None, None]
    final_done_flag = []

    def final_tile(t, osts, lo=0, hi=6):
        st = osts[t % 2]
        for half in range(2):
            b = psbank(lo, hi)
            for c4 in range(4):
                c = half * 4 + c4
                P.transpose(psA[:, b, c4 * 128:(c4 + 1) * 128], xT[:, c, t * 128:(t + 1) * 128], identf)
            P.copy('act' if half == 0 else 'dve', st[:, half * 512:(half + 1) * 512], psA[:, b, :])
        final_tok[t % 2] = P.dma('sp', out_d[t * 128:(t + 1) * 128, :], st, 'out%d' % (t % 2), reads_sb=[st])

    def load_params(li):
        P.dma('sp', colp[:], colp_d[li], 'params', writes_sb=[colp[:]])
        P.dma('sp', rowp[:], rowp_d[li].partition_broadcast(128), 'params', writes_sb=[rowp[:]])

    def layer(li, pre_normed=False):
        lam_init = 0.8 - 0.6 * math.exp(-0.3 * li)
        win = w_in_d[li]

        def win_piece(c0, n):
            return win[:, c0:c0 + n].rearrange("(c p) n -> p c n", p=128)

        if not pre_normed:
            load_params(li)
        pA = load_piece(0, win_piece(0, 512), 512)
        pC = load_piece(1, win_piece(1540, 256), 256)

        if not pre_normed:
            norm_full(CP_N1W)
        dump('h1_%d' % li, hT)
        chk('norm1')

        W0 = 16384

        bm = Bump(arena, W0, ARENA)
        wsTm = bm.alloc([128, 4, 128], BF16)
        NBA = 7
        gA = [bm.alloc([128, 512], F32) for _ in range(NBA)]
        gB = [bm.alloc([128, 512], F32) for _ in range(NBA)]
        vnb = [bm.alloc([128, 256], BF16) for _ in range(NBA)]
        yab = [bm.alloc([128, 256], BF16) for _ in range(NBA)]
        ssv = [bm.alloc([128, 4], F32) for _ in range(NBA)]
        rsv = [bm.alloc([128, 4], F32) for _ in range(NBA)]
        P.dma('pool', wsTm, wsT_d[li].rearrange("h s t -> s h t"), 'wsT', writes_sb=[wsTm])
        P.tt('dve', wsTm, wsTm, Ub[:].unsqueeze(1).to_broadcast([128, 4, 128]), ALU.mult)
        gmnw = rowp[:, RP_GMNW:RP_GMNW + 256]
        v4 = lambda ap: ap.rearrange("p (h d) -> p h d", h=4)
        pa_of = {}

        def a_st(st, t):
            tl = slice(t * 128, (t + 1) * 128)
            par = t % NBA
            a, bb = gA[par], gB[par]
            if st == 0:
                pa = psA[:, t % 4, :]
                pa_of[t] = pa
                for dc in range(8):
                    P.mm(pa, hT[:, dc, tl], pA[:, dc, :], start=(dc == 0), stop=(dc == 7))
                P.act(a, pa, AF.Square, scale=0.044715 ** 0.5)
            elif st == 1:
                P.stt(bb, a, 1.0, pa_of[t], ALU.add, ALU.mult)
                P.act(a, bb, AF.Sigmoid, scale=1.5957691216057308)
            elif st == 2:
                P.tt('dve', a, a, pa_of[t], ALU.mult)
                P.act(bb[:, 0:256], a[:, 256:512], AF.Square)
            elif st == 3:
                P.rsum(ssv[par], v4(bb[:, 0:256]))
                P.ts('dve', ssv[par], ssv[par], 1.0 / 64, EPS, op0=ALU.mult, op1=ALU.add)
                P.tt('pool', rsv[par], ssv[par], neghalf[:, 0:1].to_broadcast([128, 4]), ALU.pow)
            elif st == 4:
                P.tt('pool', v4(bb[:, 0:256]), v4(a[:, 256:512]), rsv[par].unsqueeze(2).to_broadcast([128, 4, 64]), ALU.mult)
                P.tt('pool', vnb[par], bb[:, 0:256], gmnw, ALU.mult)
            elif st == 5:
                b2 = psbank(4, 6)
                for h in range(4):
                    P.mm(psA[:, b2, h * 64:(h + 1) * 64], wsTm[:, h, :], vnb[par][:, h * 64:(h + 1) * 64], start=True, stop=True)
                for h in range(4):
                    P.stt(yab[par][:, h * 64:(h + 1) * 64], psA[:, b2, h * 64:(h + 1) * 64],
                          colp[:, CP_BS + h:CP_BS + h + 1], a[:, h * 64:(h + 1) * 64], ALU.add, ALU.mult)
            else:
                ts0 = (t % 2) * 8
                for j in range(2):
                    P.transpose(pst[:, ts0 + j, :], yab[par][:, j * 128:(j + 1) * 128], identb[:])
                P.copy('act', mixT[:, 0:2, tl], pst[:, ts0:ts0 + 2, :])

        NST = 7
        for i in range(NT + NST - 1):
            for st in range(NST):
                t = i - st
                if 0 <= t < NT:
                    a_st(st, t)

        chk('A')
        pB1 = load_piece(0, win_piece(512, 512), 512)

        bm = Bump(arena, W0, ARENA)
        wbd = [bm.alloc([128, 128], BF16) for _ in range(2)]
        pcxs = [bm.alloc([128, 16 + L], F32) for _ in range(2)]
        L1 = bm.alloc([128, 16 + L], F32)
        L2 = bm.alloc([128, 16 + L], F32)
        pT = bm.alloc([128, L], BF16)
        t16 = bm.alloc([128, 16], F32)
        for cc in range(2):
            P.memset('pool', wbd[cc], 0.0)
            for gg in range(2):
                dst = wbd[cc][gg * 64:(gg + 1) * 64, gg * 64:(gg + 1) * 64]
                P.dma('pool', dst, poolw_d[li][2 * cc + gg], 'wbd', writes_sb=[dst])
        P.memset('pool', pcxs[0][:, 0:16], 0.0)
        P.memset('pool', pcxs[1][:, 0:16], 0.0)
        P.memset('pool', L1[:, 0:16], 0.0)
        P.memset('pool', L2[:, 0:16], 0.0)
        for cc in range(2):
            for s in range(NS):
                b = psbank()
                proj_fm(pC, cc * 128, s, b)
                P.copy('act', pcxs[cc][:, 16 + s * 512:16 + (s + 1) * 512], psA[:, b, :])
        for cc in range(2):
            pcx = pcxs[cc]

            def shadd(dst, src, k, prt=slice(0, 128), eng='dve'):
                P.tt(eng, dst[prt, 16:16 + L], src[prt, 16:16 + L], src[prt, 16 - k:16 - k + L], ALU.add)

            shadd(L1, pcx, 1)
            hi = slice(64, 128)
            lo = slice(0, 64)
            if cc == 0:
                shadd(L2, L1, 2, hi)
            else:
                shadd(L2, L1, 2)
                shadd(L1, L2, 4)
                shadd(L2, L1, 8, hi)
            for prt, S in ((lo, L1), (hi, L2)):
                P.stt(pT[prt, 16:L], S[prt, 32:16 + L], cF[prt, CF_INVWIN + cc:CF_INVWIN + cc + 1],
                      pcx[prt, 32:16 + L], ALU.mult, ALU.subtract)
                P.tt('dve', t16[prt, :], S[prt, 16:32], cF[prt, CF_INVC16 + cc * 16:CF_INVC16 + (cc + 1) * 16], ALU.mult)
                P.tt('dve', pT[prt, 0:16], t16[prt, :], pcx[prt, 16:32], ALU.subtract)
            for s in range(NS):
                sl = slice(s * 512, (s + 1) * 512)
                b = psbank()
                P.mm(psA[:, b, :], wbd[cc], pT[:, sl], start=True, stop=True)
                P.act(mixT[:, 4 + cc, sl], psA[:, b, :], AF.Identity, scale=colp[:, CP_PSCALE + cc:CP_PSCALE + cc + 1])

        chk('C')
        pB2 = load_piece(1, win_piece(1024, 512), 512)

        bm = Bump(arena, W0, ARENA)
        zs = bm.alloc([128, 2, L], BF16)
        xsT = bm.alloc([128, 2, L], BF16)
        BT = bm.alloc([128, 2, L], BF16)
        CT = bm.alloc([128, 2, L], BF16)
        dtm = bm.alloc([128, 64], F32)
        atm = bm.alloc([128, 64], F32)
        acs = bm.alloc([128, 64], F32)
        dte = bm.alloc([128, 64], F32)
        b1_start = bm.off
        cin = [bm.alloc([128, 4 + 512], BF16) for _ in range(2)]
        bm_cin3 = bm.alloc([128, 4 + 512], BF16)
        accs = [bm.alloc([128, 512], F32) for _ in range(2)]
        dws = [bm.alloc([128, 4, 128], BF16) for _ in range(2)]
        wdt = bm.alloc([128, 8, 4], BF16)
        dtr = bm.alloc([128, 64], F32)
        t64a = bm.alloc([128, 64], F32)
        t64b = bm.alloc([128, 64], F32)
        Ab = bm.alloc([128, 4], F32)
        P.dma('pool', wdt, win_piece(1536, 4), 'wdt', writes_sb=[wdt])

        bD = psbank()
        for t in range(NT):
            for dc in range(8):
                P.mm(psA[:, bD, t * 4:(t + 1) * 4], hT[:, dc, t * 128:(t + 1) * 128], wdt[:, dc, :], start=(dc == 0), stop=(dc == 7))
        v3 = lambda ap: ap.rearrange("p (c h) -> p c h", h=4)
        P.tt('dve', v3(dtr), v3(psA[:, bD, 0:64]), rowp[:, RP_DTB:RP_DTB + 4].unsqueeze(1).to_broadcast([128, 16, 4]), ALU.add)
        P.act(t64a, dtr, AF.Abs)
        P.act(t64a, t64a, AF.Exp, scale=-1.0)
        P.act(t64a, t64a, AF.Ln, bias=1.0)
        P.ts('dve', t64b, dtr, 0.0, None, op0=ALU.max)
        P.tt('dve', dtm, t64a, t64b, ALU.add)
        P.act(Ab, rowp[:, RP_ALOG:RP_ALOG + 4], AF.Exp)
        P.ts('dve', Ab, Ab, -1.0, None, op0=ALU.mult)
        P.tt('dve', v3(atm), v3(dtm), Ab.unsqueeze(1).to_broadcast([128, 16, 4]), ALU.mult)
        bX = psbank()
        P.mm(psA[:, bX, 0:64], Uf, atm, start=True, stop=True)
        P.copy('dve', acs, psA[:, bX, 0:64])
        P.mm(psA[:, bX, 64:128], SLf, atm, start=True, stop=True)
        P.act(dte, psA[:, bX, 64:128], AF.Exp)

        for zc in range(2):
            for s in range(NS):
                b = psbank()
                proj_fm(pB1, zc * 128, s, b)
                P.act(zs[:, zc, s * 512:(s + 1) * 512], psA[:, b, :], AF.Silu)
        dests = [xsT[:, 0, :], xsT[:, 1, :], BT[:, 0, :], BT[:, 1, :], CT[:, 0, :], CT[:, 1, :]]
        cin = cin + [bm_cin3]
        conv_units = [(cb, s_) for cb in range(6) for s_ in range(NS)]

        def conv_proj(i):
            cb, s_ = conv_units[i]
            piece, col0 = (pB1, 256 + cb * 128) if cb < 2 else (pB2, (cb - 2) * 128)
            if s_ == 0:
                dw = dws[cb % 2]
                cw = CP_CONVW + cb * 4
                for kk in range(2):
                    P.ts('dve', dw[:, kk, :], identb[:], colp[:, cw + kk:cw + kk + 1], None, op0=ALU.mult)
            ci = cin[i % 3]
            prev = cin[(i - 1) % 3]
            b = psbank(0, 3)
            proj_fm(piece, col0, s_, b)
            P.copy('act', ci[:, 3:515], psA[:, b, :])
            if s_ == 0:
                P.memset('pool', ci[:, 0:3], 0.0)
            else:
                P.copy('pool', ci[:, 0:3], prev[:, 512:515])

        def conv_mm(i):
            cb, s_ = conv_units[i]
            dw = dws[cb % 2]
            ci = cin[i % 3]
            ac = accs[i % 2]
            cw = CP_CONVW + cb * 4
            b2 = psbank(3, 6)
            for kk in range(2):
                P.mm(psA[:, b2, :], dw[:, kk, :], ci[:, kk:kk + 512], start=(kk == 0), stop=(kk == 1))
            P.stt(ac, ci[:, 2:2 + 512], colp[:, cw + 2:cw + 3], psA[:, b2, :], ALU.mult, ALU.add)
            P.stt(ac, ci[:, 3:3 + 512], colp[:, cw + 3:cw + 4], ac, ALU.mult, ALU.add)
            P.act(dests[cb][:, s_ * 512:(s_ + 1) * 512], ac, AF.Silu, bias=colp[:, CP_CONVB + cb:CP_CONVB + cb + 1])

        conv_proj(0)
        for i in range(len(conv_units)):
            if i + 1 < len(conv_units):
                conv_proj(i + 1)
            conv_mm(i)
        chk('B1')
        pD1 = load_piece(0, win_piece(1796, 512), 512)
        pD2 = load_piece(1, win_piece(2308, 256), 256)

        bm2 = Bump(arena, b1_start, ARENA)
        xdts = [bm2.alloc([128, 256], BF16) for _ in range(2)]
        xdtw = bm2.alloc([128, 256], BF16)
        Btm = bm2.alloc([128, 256], BF16)
        t1 = bm2.alloc([128, 512], F32)
        MTs = [bm2.alloc([128, 512], BF16) for _ in range(2)]
        E = bm2.alloc([128, 512], F32)
        Cdecs = [bm2.alloc([128, 512], BF16) for _ in range(2)]
        cds = [bm2.alloc([128, 4], F32) for _ in range(2)]
        S = bm2.alloc([128, 256], F32)
        Sbf = bm2.alloc([128, 256], BF16)
        yg = bm2.alloc([128, 2, 512], F32)
        sqy = bm2.alloc([128, 512], BF16)
        msy = t1
        rsy = E
        P.memset('pool', S, 0.0)
        P.memset('pool', Sbf, 0.0)
        h4 = lambda ap: ap.rearrange("p (h d) -> p h d", h=4)
        g22 = lambda ap: ap.rearrange("p (g r l) -> p g r l", g=2, r=2)

        def b2_X(c):
            par = c % 2
            cl = slice(c * 128, (c + 1) * 128)
            xdt, MT, Cdec, cd = xdts[par], MTs[par], Cdecs[par], cds[par]
            tb = par * 8
            P.transpose(pst[:, tb + 0, :], xsT[:, 0, cl], identb[:])
            P.transpose(pst[:, tb + 1, :], xsT[:, 1, cl], identb[:])
            P.transpose(pst[:, tb + 2, :], BT[:, 0, cl], identb[:])
            P.transpose(pst[:, tb + 3, :], BT[:, 1, cl], identb[:])
            P.tt('dve', h4(xdt), pst[:, tb:tb + 2, :].rearrange("p a (h d) -> p (a h) d", h=2),
                 dtm[:, c * 4:(c + 1) * 4].unsqueeze(2).to_broadcast([128, 4, 64]), ALU.mult)
            P.tt('dve', h4(xdtw), h4(xdt), dte[:, c * 4:(c + 1) * 4].unsqueeze(2).to_broadcast([128, 4, 64]), ALU.mult)
            P.copy('act', Btm.rearrange("p (a n) -> p a n", a=2), pst[:, tb + 2:tb + 4, :])
            bC = psbank(0, 2)
            for g in range(2):
                P.mm(psA[:, bC, g * 128:(g + 1) * 128], BT[:, g, cl], CT[:, g, cl], start=True, stop=True)
            bR = psbank(2, 4)
            for h in range(4):
                P.mm(psA[:, bR, h * 128:(h + 1) * 128], atm[:, c * 4 + h:c * 4 + h + 1].to_broadcast([128, 128]), Uf,
                     start=True, stop=True)
            R4 = psA[:, bR, :].rearrange("p (h l) -> p h l", h=4)
            t14 = t1.rearrange("p (h l) -> p h l", h=4)
            for h in range(4):
                P.stt(t14[:, h, :], R4[:, h, :], acs[:, c * 4 + h:c * 4 + h + 1], NEGf, ALU.subtract, ALU.add)
            P.act(t1, t1, AF.Exp)
            P.tt('dve', g22(MT), g22(t1),
                 psA[:, bC, 0:256].rearrange("p (g l) -> p g l", g=2).unsqueeze(2).to_broadcast([128, 2, 2, 128]), ALU.mult)
            P.act(E, psA[:, bR, :], AF.Exp)
            P.copy('act', cd, E.rearrange("p (h l) -> p h l", h=4)[:, :, 127])
            P.tt('pool', g22(Cdec), g22(E), CT[:, :, cl].unsqueeze(2).to_broadcast([128, 2, 2, 128]), ALU.mult)
            bY = 4 + par
            for h in range(4):
                P.mm(psA[:, bY, 256 + h * 64:256 + (h + 1) * 64], Btm[:, (h // 2) * 128:(h // 2 + 1) * 128],
                     xdtw[:, h * 64:(h + 1) * 64], start=True, stop=True)

        def b2_Y(c):
            par = c % 2
            cl = slice(c * 128, (c + 1) * 128)
            xdt, MT, Cdec, cd = xdts[par], MTs[par], Cdecs[par], cds[par]
            bY = 4 + par
            for h in range(4):
                o = psA[(h % 2) * 64:(h % 2 + 1) * 64, bY, (h // 2) * 128:(h // 2 + 1) * 128]
                P.mm(o, xdt[:, h * 64:(h + 1) * 64], MT[:, h * 128:(h + 1) * 128], start=True, stop=False)
                P.mm(o, Sbf[:, h * 64:(h + 1) * 64], Cdec[:, h * 128:(h + 1) * 128], start=False, stop=True)
            for h in range(4):
                P.stt(S[:, h * 64:(h + 1) * 64], S[:, h * 64:(h + 1) * 64], cd[:, h:h + 1],
                      psA[:, bY, 256 + h * 64:256 + (h + 1) * 64], ALU.mult, ALU.add)
            P.copy('act', Sbf, S)
            cs = (c % 4) * 128
            for hc in range(2):
                P.stt(yg[:, hc, cs:cs + 128], xsT[:, hc, cl], colp[:, CP_DCOL + hc:CP_DCOL + hc + 1],
                      psA[:, bY, hc * 128:(hc + 1) * 128], ALU.mult, ALU.add)
            if c % 4 == 3:
                s = c // 4
                sl = slice(s * 512, (s + 1) * 512)
                for hc in range(2):
                    P.tt('pool', yg[:, hc, :], yg[:, hc, :], zs[:, hc, sl], ALU.mult)
                    P.act(sqy, yg[:, hc, :], AF.Square)
                    bN = psbank(0, 2)
                    P.mm(psA[:, bN, :], onesb[:], sqy, start=True, stop=True)
                    P.act(msy, psA[:, bN, :], AF.Ln, scale=1.0 / 128, bias=EPS)
                    P.act(rsy, msy, AF.Exp, scale=-0.5)
                    P.stt(mixT[:, 2 + hc, sl], yg[:, hc, :], colp[:, CP_SSMNW + hc:CP_SSMNW + hc + 1], rsy, ALU.mult, ALU.mult)

        b2_X(0)
        for c in range(NT):
            if c + 1 < NT:
                b2_X(c + 1)
            b2_Y(c)

        chk('B2')
        bm = Bump(arena, W0, ARENA)
        qT2 = bm.alloc([128, 2, 2, L], BF16)
        kT2 = bm.alloc([128, 2, 2, L], BF16)
        vflat = bm.alloc([128, NT, 324], BF16)
        vaug = vflat[:, :, 0:260].rearrange("p t (h e) -> p t h e", h=4)
        PT = [bm.alloc([128, 512], BF16) for _ in range(3)]
        oTs = [bm.alloc([128, 512], F32) for _ in range(2)]
        sqb = PT[0]
        msq = oTs[0]
        rsq = oTs[1]
        o0 = bm.alloc([128, 256], F32)
        o1 = bm.alloc([128, 256], F32)
        onbs = [bm.alloc([128, 256], BF16) for _ in range(2)]
        lt = bm.alloc([128, 32], F32)
        s1 = small[:, 0:1]
        s2 = small[:, 1:2]
        neglam = small[:, 2:3]
        qnws = small[:, 4:6]
        rc = small[:, 8:16]
        nr1 = small[:, 16:20]
        ss4 = small[:, 20:24]
        rs4 = small[:, 24:28]
        P.tt('dve', lt, rowp[:, RP_LQ1:RP_LQ1 + 32], rowp[:, RP_LK1:RP_LK1 + 32], ALU.mult)
        P.rsum(s1, lt)
        P.tt('dve', lt, rowp[:, RP_LQ2:RP_LQ2 + 32], rowp[:, RP_LK2:RP_LK2 + 32], ALU.mult)
        P.rsum(s2, lt)
        P.act(small[:, 0:2], small[:, 0:2], AF.Exp)
        P.tt('dve', neglam, s2, s1, ALU.subtract)
        P.ts('dve', neglam, neglam, -lam_init, None, op0=ALU.add)
        P.ts('dve', qnws, cF[:, CF_MASKC:CF_MASKC + 2], colp[:, CP_QNW:CP_QNW + 1], SM_SCALE, op0=ALU.mult, op1=ALU.mult)
        P.memset('pool', vflat[:, :, 260:324], 0.0)
        P.memset('pool', vaug[:, :, :, 64:65], 1.0)
        P.ts('dve', small[:, 6:8], cF[:, CF_MASKH:CF_MASKH + 2], colp[:, CP_KNW:CP_KNW + 1], None, op0=ALU.mult)
        for blk in range(4):
            for s in range(NS):
                sl = slice(s * 512, (s + 1) * 512)
                b = psbank(0, 3)
                proj_fm(pD1, blk * 128, s, b)
                P.act(sqb, psA[:, b, :], AF.Square)
                b2 = psbank(3, 6)
                P.mm(psA[:, b2, :], bd32b[:], sqb, start=True, stop=True)
                P.act(msq, psA[:, b2, :], AF.Ln, scale=1.0 / 32, bias=EPS)
                P.act(rsq, msq, AF.Exp, scale=-0.5)
                if blk < 2:
                    for c in range(2):
                        P.stt(qT2[:, blk, c, sl], psA[:, b, :], qnws[:, c:c + 1], rsq, ALU.mult, ALU.mult)
                else:
                    for hh in range(2):
                        P.stt(kT2[:, blk - 2, hh, sl], psA[:, b, :], small[:, 6 + hh:7 + hh], rsq, ALU.mult, ALU.mult)
        for t in range(NT):
            b = psbank()
            for dc in range(8):
                P.mm(psA[:, b, 0:256], hT[:, dc, t * 128:(t + 1) * 128], pD2[:, dc, :], start=(dc == 0), stop=(dc == 7))
            P.copy('act' if t % 2 else 'dve', vaug[:, t, :, 0:64], psA[:, b, 0:256].rearrange("p (h d) -> p h d", h=4))

        pO = [load_piece(0, w_out_d[li][:, 0:512].rearrange("(c p) n -> p c n", p=128), 512),
              load_piece(1, w_out_d[li][:, 512:1024].rearrange("(c p) n -> p c n", p=128), 512)]

        psSum = pst[:, 4:8, :].rearrange("p a b -> p (a b)").bitcast(F32)
        S3 = pst[:, 8:16, :].rearrange("p a b -> p (a b)").bitcast(F32)
        Sbanks = [psA[:, 0, :], psA[:, 1, :], S3]
        units = []
        for qs in range(8):
            for h in range(4):
                kbs = []
                for kb in range(2 * qs + 2):
                    if SLOPES[h] * (256 * qs - (128 * kb + 127)) > 130.0:
                        continue
                    kbs.append(kb)
                for n_, kb in enumerate(kbs):
                    units.append((qs, h, kb, n_ == 0, n_ == len(kbs) - 1))
        s_rot = [0]

        def emit_S(u):
            qs, h, kb, first, last = u
            kc, hh = h // 2, h % 2
            b = s_rot[0] % 3
            s_rot[0] += 1
            diag = kb >= 2 * qs
            P.mm(Sbanks[b], kT2[:, kc, hh, kb * 128:(kb + 1) * 128],
                 qT2[:, kc, :, qs * 256:(qs + 1) * 256].rearrange("p c q -> p (c q)") if False else qT2[:, kc, :, qs * 256:(qs + 1) * 256],
                 start=True, stop=not diag)
            if diag:
                P.mm(Sbanks[b], identb[:], (M0 if kb == 2 * qs else M1)[:], start=False, stop=True)
            return b

        def emit_exp_pv(u, b):
            qs, h, kb, first, last = u
            e = 2 * qs - kb + 1
            P.act(PT[b], Sbanks[b], AF.Exp, bias=cF[:, CF_ALIBI + h * 17 + e:CF_ALIBI + h * 17 + e + 1])
            ob = 2 + (qs * 4 + h) % 2
            P.mm(psA[:, ob, :], vflat[:, kb, h * 65:h * 65 + 128], PT[b], start=first, stop=last)
            if last:
                ot = oTs[(qs * 4 + h) % 2]
                for d in [d for d in deferred if d[1] == 'T' and (d[2][0] * 4 + d[2][1]) % 2 == (qs * 4 + h) % 2]:
                    deferred.remove(d)
                    fire(d)
                P.copy('dve' if h % 2 else 'act', ot[0:65, :], psA[0:65, ob, :])

        def emit_oT_transposes(qs, h, c, j):
            ot = oTs[(qs * 4 + h) % 2]
            cols = slice(c * 256 + j * 128, c * 256 + (j + 1) * 128)
            P.transpose(psA[:, 4 + j, (c * 4 + h) * 64:(c * 4 + h + 1) * 64], ot[0:64, cols], identf[0:64, 0:64])
            P.transpose(psSum[:, j * 8 + c * 4 + h:j * 8 + c * 4 + h + 1], ot[64:65, cols], identf[64:65, 64:65])

        def epilogue_a(qb):
            j = qb % 2
            onb = onbs[j]
            P.recip(rc, psSum[:, j * 8:(j + 1) * 8])
            P.ts('dve', nr1, rc[:, 4:8], neglam, None, op0=ALU.mult)
            P.tt('dve', h4(o0), h4(psA[:, 4 + j, 0:256]), rc[:, 0:4].unsqueeze(2).to_broadcast([128, 4, 64]), ALU.mult)
            P.tt('dve', h4(o1), h4(psA[:, 4 + j, 256:512]), nr1.unsqueeze(2).to_broadcast([128, 4, 64]), ALU.mult)
            P.tt('dve', o0, o0, o1, ALU.add)
            P.tt('dve', o1, o0, o0, ALU.mult)
            P.rsum(ss4, h4(o1))
            P.ts('dve', ss4, ss4, 1.0 / 64, EPS, op0=ALU.mult, op1=ALU.add)
            P.tt('pool', rs4, ss4, neghalf[:, 0:1].to_broadcast([128, 4]), ALU.pow)
            P.tt('dve', h4(o1), h4(o0), rs4.unsqueeze(2).to_broadcast([128, 4, 64]), ALU.mult)
            P.stt(onb, o1, 1.0 - lam_init, rowp[:, RP_SUBLN:RP_SUBLN + 256], ALU.mult, ALU.mult)

        def epilogue_b(qb):
            ql = slice(qb * 128, (qb + 1) * 128)
            j = qb % 2
            ts0 = j * 2
            for jj in range(2):
                P.transpose(pst[:, ts0 + jj, :], onbs[j][:, jj * 128:(jj + 1) * 128], identb[:])
            P.copy('act', mixT[:, 6:8, ql], pst[:, ts0:ts0 + 2, :])

        pend = []
        deferred = []

        def fire(item):
            _, kind, args = item
            if kind == 'T':
                qs_, h_, c_, j_ = args
                emit_oT_transposes(qs_, h_, c_, j_)
                if h_ == 3 and c_ == 1 and j_ == 1:
                    epilogue_a(2 * qs_)
                    epilogue_a(2 * qs_ + 1)
                    deferred.append([6, 'E', 2 * qs_])
                    deferred.append([6, 'E', 2 * qs_ + 1])
            else:
                epilogue_b(args)

        def retire():
            pu, pb = pend.pop(0)
            emit_exp_pv(pu, pb)
            for d in deferred:
                d[0] -= 1
            ready = [d for d in deferred if d[0] <= 0]
            for d in ready:
                deferred.remove(d)
                fire(d)
            if pu[4]:
                n_ = 0
                for c_ in range(2):
                    for j_ in range(2):
                        deferred.append([2 + n_, 'T', (pu[0], pu[1], c_, j_)])
                        n_ += 1

        for u in units:
            pend.append((u, emit_S(u)))
            if len(pend) > 2:
                retire()
        while pend:
            retire()
        while deferred:
            fire(deferred.pop(0))

        if stop != 'D':
            dump('mix%d' % li, mixT)
        chk('D')

        nb = norm_bufs()
        for s in range(NS):
            sl = slice(s * 512, (s + 1) * 512)
            for c in range(8):
                half, cj = c // 4, c % 4
                b = psbank()
                for mc in range(8):
                    P.mm(psA[:, b, :], pO[half][:, mc, cj * 128:(cj + 1) * 128], mixT[:, mc, sl], start=(mc == 0), stop=(mc == 7))
                P.tt('dve', xT[:, c, sl], xT[:, c, sl], psA[:, b, :], ALU.add)
            if s >= 1:
                norm_slab(s - 1, CP_N2W, nb)
        norm_slab(NS - 1, CP_N2W, nb)

        dump('x1_%d' % li, xT)
        chk('wout')

        groups = []
        j0 = 0
        while j0 < NJ:
            n = min(4, NJ - j0)
            groups.append((j0, n))
            j0 += n

        def load_group(gi):
            j0, n = groups[gi]
            base = 3 * (gi % 2)
            wg = load_piece(base + 0, wg_d[li][:, j0 * 128:(j0 + n) * 128].rearrange("(c p) n -> p c n", p=128), n * 128)
            wu = load_piece(base + 1, wu_d[li][:, j0 * 128:(j0 + n) * 128].rearrange("(c p) n -> p c n", p=128), n * 128)
            wd = load_piece(base + 2, wd_d[li][j0 * 128:(j0 + n) * 128, :].rearrange("(j p) n -> p j n", p=128), n, kind="down")
            return wg, wu, wd

        loaded = {0: load_group(0)}
        if len(groups) > 1:
            loaded[1] = load_group(1)
        dump('h2_%d' % li, hT)
        chk('norm2')
        bmf = Bump(arena, 48 * 1024, ARENA)
        sg = [bmf.alloc([128, 512], BF16) for _ in range(2)]
        actT = [mixT[:, 0:4, 0:512], mixT[:, 4:8, 0:512]]
        fuse_next = (li + 1 < nlayers) and stop is None
        fuse_final = (li + 1 == nlayers) and stop is None and not dbg_d
        if fuse_final:
            bmo = Bump(arena, 0, 24 * 1024)
            fin_osts = [bmo.alloc([128, D], F32) for _ in range(2)]
            assert (len(groups) - 1) % 2 == 1
        steps = [(gi, s_) for gi in range(len(groups)) for s_ in range(NS)]

        def ffn_gu(k):
            gi, s_ = steps[k]
            n = groups[gi][1]
            wg, wu, wd = loaded[gi]
            sl = slice(s_ * 512, (s_ + 1) * 512)
            at = actT[k % 2]
            for jb in range(n):
                bg = psbank(0, 2)
                bu = psbank(2, 4)
                for dc in range(8):
                    P.mm(psA[:, bg, :], wg[:, dc, jb * 128:(jb + 1) * 128], hT[:, dc, sl], start=(dc == 0), stop=(dc == 7))
                for dc in range(8):
                    P.mm(psA[:, bu, :], wu[:, dc, jb * 128:(jb + 1) * 128], hT[:, dc, sl], start=(dc == 0), stop=(dc == 7))
                sgt = sg[(k * 4 + jb) % 2]
                P.act(sgt, psA[:, bg, :], AF.Silu)
                P.tt('dve', at[:, jb, :], sgt, psA[:, bu, :], ALU.mult)

        def ffn_dn(k):
            gi, s_ = steps[k]
            n = groups[gi][1]
            wg, wu, wd = loaded[gi]
            sl = slice(s_ * 512, (s_ + 1) * 512)
            at = actT[k % 2]
            for c in range(8):
                bd = psbank(4, 6)
                for jb in range(n):
                    P.mm(psA[:, bd, :], wd[:, jb, c * 128:(c + 1) * 128], at[:, jb, :], start=(jb == 0), stop=(jb == n - 1))
                P.tt('dve', xT[:, c, sl], xT[:, c, sl], psA[:, bd, :], ALU.add)

        ffn_gu(0)
        for k, (gi, s_) in enumerate(steps):
            lastg = gi == len(groups) - 1
            if lastg and s_ == 0 and fuse_next:
                load_params(li + 1)
            if k + 1 < len(steps):
                ffn_gu(k + 1)
            ffn_dn(k)
            if lastg and fuse_next and s_ >= 1:
                norm_slab(s_ - 1, CP_N1W, nb)
            if lastg and fuse_final and s_ >= 1:
                for t in range(4 * (s_ - 1), 4 * s_):
                    final_tile(t, fin_osts, 4, 6)
            if s_ == NS - 1:
                if lastg and fuse_next:
                    norm_slab(NS - 1, CP_N1W, nb)
                if lastg and fuse_final:
                    for t in range(4 * (NS - 1), 4 * NS):
                        final_tile(t, fin_osts, 4, 6)
                    final_done_flag.append(True)
                if gi + 2 < len(groups):
                    loaded[gi + 2] = load_group(gi + 2)

        dump('x2_%d' % li, xT)
        P.mark('ffn')

    if stop is not None:
        for c in range(8):
            P.memset('pool', mixT[:, c, :], 0.0)
    fuse0 = stop is None
    if fuse0:
        load_params(0)
    load_x(fuse0)
    P.mark('loadx')
    try:
        if stop != 'loadx':
            for li in range(nlayers):
                layer(li, pre_normed=(stop is None))
    except _Stop:
        pass

    final_done = bool(final_done_flag)
    if not final_done:
        bmo = Bump(arena, 0, ARENA)
        osts = [bmo.alloc([128, D], F32) for _ in range(2)]
        for t in range(NT):
            final_tile(t, osts)
    toks = list(final_tok) + [(s, v[1]) for s, v in P.dsem.items() if s == 'd_dbg']
    P.wait_all('sp', toks)

    with nc.Block() as block:
        P.emit(block)
    es.close()
    return nc, P


def host_pack(inputs):
    f = lambda k: np.asarray(inputs[k], dtype=np.float32)
    colp = np.zeros((2, 128, NCOLP), np.float32)
    rowp = np.zeros((2, 1, NROWP), np.float32)
    p = np.arange(128)
    for l in range(2):
        colp[l, :, CP_N1W:CP_N1W + 8] = f('norm1_w')[l].reshape(8, 128).T
        colp[l, :, CP_N2W:CP_N2W + 8] = f('norm2_w')[l].reshape(8, 128).T
        colp[l, :, CP_CONVW:CP_CONVW + 24] = f('ssm_conv_w')[l].reshape(6, 128, 4).transpose(1, 0, 2).reshape(128, 24)
        colp[l, :, CP_CONVB:CP_CONVB + 6] = f('ssm_conv_b')[l].reshape(6, 128).T
        for hc in range(2):
            colp[l, :, CP_DCOL + hc] = f('ssm_d')[l][2 * hc + p // 64]
        colp[l, :, CP_SSMNW:CP_SSMNW + 2] = f('ssm_norm_w')[l].reshape(2, 128).T
        colp[l, :, CP_PSCALE:CP_PSCALE + 2] = f('pool_scale')[l].reshape(2, 128).T
        colp[l, :, CP_QNW] = f('da_q_norm_w')[l][p % 32]
        colp[l, :, CP_KNW] = f('da_k_norm_w')[l][p % 32]
        colp[l, :, CP_BS:CP_BS + 4] = f('gm_bs')[l].T
        rowp[l, 0, RP_GMNW:RP_GMNW + 256] = f('gm_norm_w')[l].reshape(256)
        rowp[l, 0, RP_DTB:RP_DTB + 4] = f('ssm_dt_bias')[l]
        rowp[l, 0, RP_ALOG:RP_ALOG + 4] = f('ssm_a_log')[l]
        rowp[l, 0, RP_LQ1:RP_LQ1 + 32] = f('da_lambda_q1')[l]
        rowp[l, 0, RP_LK1:RP_LK1 + 32] = f('da_lambda_k1')[l]
        rowp[l, 0, RP_LQ2:RP_LQ2 + 32] = f('da_lambda_q2')[l]
        rowp[l, 0, RP_LK2:RP_LK2 + 32] = f('da_lambda_k2')[l]
        rowp[l, 0, RP_SUBLN:RP_SUBLN + 256] = np.tile(f('da_subln_w')[l], 4)
    shared = {
        "w_in": np.ascontiguousarray(f('w_in')),
        "w_out": np.ascontiguousarray(f('w_out')),
        "ffn_w_gate": np.ascontiguousarray(f('ffn_w_gate')),
        "ffn_w_up": np.ascontiguousarray(f('ffn_w_up')),
        "ffn_w_down": np.ascontiguousarray(f('ffn_w_down')),
        "gm_wsT": np.ascontiguousarray(f('gm_ws').transpose(0, 1, 3, 2)),
        "pool_w": np.ascontiguousarray(f('pool_w')),
        "colp": colp,
        "rowp": rowp,
        "cf": make_consts(),
    }
    return shared


_CACHE = {}


def kernel(**inputs):
    if 'nc' not in _CACHE:
        _CACHE['nc'] = build(2)[0]
    nc = _CACHE['nc']
    shared = host_pack(inputs)
    x = np.asarray(inputs['x'], dtype=np.float32)
    in_maps = []
    for b in range(8):
        m = dict(shared)
        m["x"] = np.ascontiguousarray(x[b])
        in_maps.append(m)
    res = run_bass_kernel_spmd(nc, in_maps, core_ids=list(range(8)))
    out = np.stack([np.asarray(res.results[b]["out"], dtype=np.float32) for b in range(8)], axis=0)
    return out
```
